# Optimizing a Trainium2 kernel written in Bass

```python
import math
import jax, jax.numpy as jnp
from jax import lax
import numpy as np

D_MODEL = 2048
BATCH = 4
SEQ = 4096
DEPTH = 1

SSM_HEAD_DIM = 64
SSM_HEADS = D_MODEL // SSM_HEAD_DIM
D_SSM = SSM_HEADS * SSM_HEAD_DIM
SSM_GROUPS = 8
HEADS_PER_GROUP = SSM_HEADS // SSM_GROUPS
D_STATE = 128
CONV_WIDTH = 4
SSD_CHUNK = 128
D_XBC = D_SSM + 2 * SSM_GROUPS * D_STATE

ATT_HEAD_DIM = 128
ATT_HEADS = D_MODEL // ATT_HEAD_DIM
D_ATT = ATT_HEADS * ATT_HEAD_DIM
DILATION_PAIRS = ((128, 1), (512, 4), (2048, 16))
ATT_BLOCK = 128

D_MIX = D_SSM + D_ATT
IN_SPLITS = (D_SSM,
             D_SSM + D_XBC,
             D_SSM + D_XBC + SSM_HEADS,
             D_SSM + D_XBC + SSM_HEADS + D_ATT,
             D_SSM + D_XBC + SSM_HEADS + 2 * D_ATT)
D_IN_PROJ = D_SSM + D_XBC + SSM_HEADS + 3 * D_ATT
D_FF = 4 * D_MODEL
EPS = 1e-6

kernel_name = "hymba_ssd_dilated_swa_sqrelu"


def rmsnorm(x, w):
    xf = x.astype(jnp.float32)
    xf = xf * lax.rsqrt(jnp.mean(xf * xf, axis=-1, keepdims=True) + EPS)
    return (xf * w.astype(jnp.float32)).astype(x.dtype)


def causal_depthwise_conv(u, w, b):
    out = lax.conv_general_dilated(
        u, w[:, None, :].astype(u.dtype), window_strides=(1,),
        padding=[(CONV_WIDTH - 1, 0)],
        dimension_numbers=('NWC', 'WIO', 'NWC'),
        feature_group_count=u.shape[-1])
    return out + b.astype(u.dtype)


def ssd_chunked(xh, dt, a, bm, cm):
    b_, s_ = xh.shape[:2]
    nc = s_ // SSD_CHUNK

    def chunk(t):
        return t.reshape((b_, nc, SSD_CHUNK) + t.shape[2:])

    xc, dtc, bc, cc = chunk(xh), chunk(dt), chunk(bm), chunk(cm)
    a_cs = jnp.cumsum(dtc * a, axis=2).transpose(0, 1, 3, 4, 2)
    causal = jnp.tril(jnp.ones((SSD_CHUNK, SSD_CHUNK), dtype=bool))
    decay_in = jnp.exp(jnp.where(causal, a_cs[..., :, None] - a_cs[..., None, :], -jnp.inf))
    cb = jnp.einsum('bcign,bcjgn->bcgij', cc, bc)
    xdt = xc * dtc[..., None]
    y_diag = jnp.einsum('bcgrij,bcjgrp->bcigrp', cb[:, :, :, None] * decay_in, xdt)

    decay_to_end = jnp.exp(a_cs[..., -1:] - a_cs)
    states = jnp.einsum('bcjgn,bcgrj,bcjgrp->bcgrpn', bc, decay_to_end, xdt)
    chunk_decay = jnp.exp(a_cs[..., -1])

    def step(h, inp):
        st, dec = inp
        return h * dec[..., None, None] + st, h

    h0 = jnp.zeros(states.shape[:1] + states.shape[2:], jnp.float32)
    _, prev = lax.scan(step, h0, (jnp.swapaxes(states, 0, 1), jnp.swapaxes(chunk_decay, 0, 1)))
    prev = jnp.swapaxes(prev, 0, 1)
    y_off = jnp.einsum('bcign,bcgrpn,bcgri->bcigrp', cc, prev, jnp.exp(a_cs))
    return (y_diag + y_off).reshape(xh.shape)


def dilated_window_attention(q, k, v, window, dilation):
    b_, s_, h_, d_ = q.shape
    sd = s_ // dilation
    reach = window // dilation
    nb = -(-sd // ATT_BLOCK)
    lp = nb * ATT_BLOCK
    bd = b_ * dilation

    def decimate(t):
        t = t.reshape(b_, sd, dilation, h_, d_).transpose(0, 2, 1, 3, 4)
        t = t.reshape(bd, sd, h_, d_)
        return jnp.pad(t, ((0, 0), (0, lp - sd), (0, 0), (0, 0)))

    def with_prev(t):
        t = jnp.pad(t, ((0, 0), (ATT_BLOCK, 0), (0, 0), (0, 0)))
        t = t.reshape(bd, nb + 1, ATT_BLOCK, h_, d_)
        return jnp.concatenate([t[:, :-1], t[:, 1:]], axis=2)

    qb = decimate(q).reshape(bd, nb, ATT_BLOCK, h_, d_)
    kb = with_prev(decimate(k))
    vb = with_prev(decimate(v))
    s = jnp.einsum('bnqhd,bnkhd->bnhqk', qb, kb)
    qi = jnp.arange(ATT_BLOCK)[:, None]
    kj = jnp.arange(2 * ATT_BLOCK)[None, :]
    dist = ATT_BLOCK + qi - kj
    key_pos = (jnp.arange(nb)[:, None, None] - 1) * ATT_BLOCK + kj[None]
    mask = (dist >= 0) & (dist <= reach) & (key_pos >= 0)
    s = jnp.where(mask[None, :, None], s, -jnp.inf)
    m = jnp.max(s, axis=-1, keepdims=True)
    p = jnp.exp(s - m)
    den = jnp.sum(p, axis=-1)
    o = jnp.einsum('bnhqk,bnkhd->bnqhd', p, vb) / jnp.swapaxes(den, 2, 3)[..., None]
    lse = jnp.swapaxes(m[..., 0] + jnp.log(den), 2, 3)

    def undecimate(t):
        t = t.reshape((bd, lp) + t.shape[3:])[:, :sd]
        t = t.reshape((b_, dilation, sd) + t.shape[2:])
        return jnp.moveaxis(t, 1, 2).reshape((b_, s_) + t.shape[3:])

    return undecimate(o), undecimate(lse)


def hybrid_mixer(u, w_in, conv_w, conv_b, dt_bias, a_log, d_skip, ssm_norm_w, w_out):
    b_, s_, _ = u.shape
    f32 = jnp.float32
    proj = jnp.einsum('bsd,de->bse', u, w_in)
    z, xbc, dt_raw, q, k, v = jnp.split(proj, IN_SPLITS, axis=-1)

    xbc = jax.nn.silu(causal_depthwise_conv(xbc, conv_w, conv_b))
    xs, bm, cm = jnp.split(xbc, (D_SSM, D_SSM + SSM_GROUPS * D_STATE), axis=-1)
    xh = xs.astype(f32).reshape(b_, s_, SSM_GROUPS, HEADS_PER_GROUP, SSM_HEAD_DIM)
    dt = jax.nn.softplus(dt_raw.astype(f32) + dt_bias.astype(f32))
    dt = dt.reshape(b_, s_, SSM_GROUPS, HEADS_PER_GROUP)
    a = -jnp.exp(a_log.astype(f32)).reshape(SSM_GROUPS, HEADS_PER_GROUP)
    bm = bm.astype(f32).reshape(b_, s_, SSM_GROUPS, D_STATE)
    cm = cm.astype(f32).reshape(b_, s_, SSM_GROUPS, D_STATE)
    y = ssd_chunked(xh, dt, a, bm, cm)
    y = y + d_skip.astype(f32).reshape(SSM_GROUPS, HEADS_PER_GROUP)[:, :, None] * xh
    yg = y.reshape(b_, s_, SSM_GROUPS, -1) * jax.nn.silu(z.astype(f32)).reshape(b_, s_, SSM_GROUPS, -1)
    yg = yg * lax.rsqrt(jnp.mean(yg * yg, axis=-1, keepdims=True) + EPS)
    y_ssm = (yg.reshape(b_, s_, D_SSM) * ssm_norm_w.astype(f32)).astype(u.dtype)

    qh = q.astype(f32).reshape(b_, s_, ATT_HEADS, ATT_HEAD_DIM) * (ATT_HEAD_DIM ** -0.5)
    kh = k.astype(f32).reshape(b_, s_, ATT_HEADS, ATT_HEAD_DIM)
    vh = v.astype(f32).reshape(b_, s_, ATT_HEADS, ATT_HEAD_DIM)
    outs, lses = [], []
    for window, dilation in DILATION_PAIRS:
        o, l = dilated_window_attention(qh, kh, vh, window, dilation)
        outs.append(o)
        lses.append(l)
    wts = jax.nn.softmax(jnp.stack(lses, axis=0), axis=0)
    y_att = jnp.einsum('ibsh,ibshd->bshd', wts, jnp.stack(outs, axis=0))
    y_att = y_att.reshape(b_, s_, D_ATT).astype(u.dtype)

    y_mix = jnp.concatenate([y_ssm, y_att], axis=-1)
    return jnp.einsum('bse,ed->bsd', y_mix, w_out)


def squared_relu_mlp(u, w_up, w_down):
    hdn = jax.nn.relu(jnp.einsum('bsd,df->bsf', u, w_up))
    return jnp.einsum('bsf,fd->bsd', hdn * hdn, w_down)


def setup_inputs(seed: int = 0) -> dict:
    key = jax.random.key(seed)
    ks = jax.random.split(key, 16)
    L = DEPTH

    def gain(k, n):
        return 1.0 + 0.1 * jax.random.normal(k, (L, n), jnp.float32)

    dt0 = jnp.exp(jax.random.uniform(ks[5], (L, SSM_HEADS), jnp.float32,
                                     math.log(1e-3), math.log(1e-1)))
    dt_bias = dt0 + jnp.log(-jnp.expm1(-dt0))
    return {
        "x": jax.random.normal(ks[0], (BATCH, SEQ, D_MODEL), jnp.float32),
        "norm_mix_pre": gain(ks[1], D_MODEL),
        "w_in": jax.random.normal(ks[2], (L, D_MODEL, D_IN_PROJ), jnp.float32) * D_MODEL ** -0.5,
        "conv_w": jax.random.normal(ks[3], (L, CONV_WIDTH, D_XBC), jnp.float32) * CONV_WIDTH ** -0.5,
        "conv_b": 0.01 * jax.random.normal(ks[4], (L, D_XBC), jnp.float32),
        "dt_bias": dt_bias,
        "a_log": jnp.log(jax.random.uniform(ks[6], (L, SSM_HEADS), jnp.float32, 1.0, 16.0)),
        "d_skip": gain(ks[7], SSM_HEADS),
        "ssm_norm_w": gain(ks[8], D_SSM),
        "w_out": jax.random.normal(ks[9], (L, D_MIX, D_MODEL), jnp.float32) * D_MIX ** -0.5,
        "norm_mix_post": gain(ks[10], D_MODEL),
        "norm_mlp_pre": gain(ks[11], D_MODEL),
        "w_up": jax.random.normal(ks[12], (L, D_MODEL, D_FF), jnp.float32) * D_MODEL ** -0.5,
        "w_down": jax.random.normal(ks[13], (L, D_FF, D_MODEL), jnp.float32) * D_FF ** -0.5,
        "norm_mlp_post": gain(ks[14], D_MODEL),
    }


def reference(x, norm_mix_pre, w_in, conv_w, conv_b, dt_bias, a_log, d_skip, ssm_norm_w,
              w_out, norm_mix_post, norm_mlp_pre, w_up, w_down, norm_mlp_post):
    h = x
    for i in range(DEPTH):
        mix = hybrid_mixer(rmsnorm(h, norm_mix_pre[i]), w_in[i], conv_w[i], conv_b[i],
                           dt_bias[i], a_log[i], d_skip[i], ssm_norm_w[i], w_out[i])
        h = h + rmsnorm(mix, norm_mix_post[i])
        ff = squared_relu_mlp(rmsnorm(h, norm_mlp_pre[i]), w_up[i], w_down[i])
        h = h + rmsnorm(ff, norm_mlp_post[i])
    return h
```

```python
from contextlib import ExitStack
import numpy as np
import concourse.bass as bass
import concourse.mybir as mybir
from concourse.bass_utils import run_bass_kernel_spmd

F32 = mybir.dt.float32
BF16 = mybir.dt.bfloat16
AF = mybir.ActivationFunctionType
ALU = mybir.AluOpType

ENGS = ("pe", "act", "dve", "pool", "sp")
EPS = 1e-6


class Dep:
    __slots__ = ("w", "r", "chan", "name")

    def __init__(self, name=""):
        self.w = None
        self.r = []
        self.chan = None
        self.name = name


class Chan:
    __slots__ = ("key", "sem", "n")


class FW:
    def __init__(self, nc, stack, same_engine_sync=True):
        self.nc = nc
        self.stack = stack
        self.streams = {e: [] for e in ENGS}
        self.cnt = {e: 0 for e in ENGS}
        self.seen = {e: {} for e in ENGS}
        self.sems = {}
        self.latest = {}
        self.nchan = 0
        self.same = same_engine_sync
        for e in ENGS:
            self.sems[e] = stack.enter_context(nc.semaphore("s_" + e))
            self.latest[e] = 0

    def sb(self, name, shape, dtype, stack=None):
        t = (stack or self.stack).enter_context(self.nc.sbuf_tensor(name, list(shape), dtype))
        return t, Dep(name)

    def ps(self, name, shape, dtype=F32, stack=None):
        t = (stack or self.stack).enter_context(self.nc.psum_tensor(name, list(shape), dtype))
        return t, Dep(name)

    def chan_of(self, dep):
        if dep.chan is None:
            c = Chan()
            c.key = "c%d" % self.nchan
            self.nchan += 1
            c.sem = self.stack.enter_context(self.nc.semaphore("d_" + c.key))
            c.n = 0
            self.sems[c.key] = c.sem
            self.latest[c.key] = 0
            dep.chan = c
        return dep.chan

    def _waits(self, eng, reads, writes, extra, group_key=None):
        need = {}

        def add(t):
            if t is None:
                return
            k, v = t
            if need.get(k, 0) < v:
                need[k] = v

        for d in reads:
            add(d.w)
        for d in writes:
            if not (group_key is not None and d.w is not None and d.w[0] == group_key):
                add(d.w)
            for t in d.r:
                add(t)
        for t in extra:
            add(t)
        st = self.streams[eng]
        for k, v in need.items():
            if k == eng and (eng == "pe" or not self.same):
                continue
            if self.seen[eng].get(k, 0) >= v:
                continue
            self.seen[eng][k] = v
            st.append(("wait", k, v))

    def op(self, eng, fn, reads=(), writes=(), extra=(), inc=True):
        self._waits(eng, reads, writes, extra)
        if inc:
            self.cnt[eng] += 1
            self.latest[eng] = self.cnt[eng]
        ticket = (eng, self.cnt[eng] if inc else self.cnt[eng] + 1)
        self.streams[eng].append(("op", fn, eng if inc else None))
        for d in reads:
            d.r.append(ticket)
            if len(d.r) > 24:
                d.r = _compact(d.r)
        for d in writes:
            d.w = ticket
            d.r = []
        return ticket

    def dma(self, q, out_ap, in_ap, reads=(), writes=(), extra=(), chan_dep=None, group=True, **kw):
        cd = chan_dep if chan_dep is not None else writes[0]
        ch = self.chan_of(cd)
        self._waits(q, reads, writes, extra, group_key=ch.key if group else None)
        ch.n += 1
        ticket = (ch.key, 16 * ch.n)
        self.latest[ch.key] = 16 * ch.n

        def fn(e, out_ap=out_ap, in_ap=in_ap, kw=kw):
            return e.dma_start(out=out_ap, in_=in_ap, **kw)

        self.streams[q].append(("dma", fn, ch.key))
        for d in reads:
            d.r.append(ticket)
        for d in writes:
            d.w = ticket
            d.r = []
        return ticket

    def barrier(self, engs=ENGS):
        for e in engs:
            st = self.streams[e]
            for k, v in self.latest.items():
                if v == 0 or self.seen[e].get(k, 0) >= v:
                    continue
                if k == e:
                    continue
                self.seen[e][k] = v
                st.append(("wait", k, v))

    def emit(self):
        nc = self.nc
        sems = self.sems
        streams = self.streams
        with nc.Block() as block:

            def run(engname, e):
                for item in streams[engname]:
                    if item[0] == "wait":
                        e.wait_ge(sems[item[1]], item[2])
                    elif item[0] == "op":
                        ins = item[1](e)
                        if item[2] is not None:
                            ins.then_inc(sems[item[2]], 1)
                    else:
                        ins = item[1](e)
                        ins.then_inc(sems[item[2]], 16)

            @block.tensor
            def _(e):
                run("pe", e)

            @block.scalar
            def _(e):
                run("act", e)

            @block.vector
            def _(e):
                run("dve", e)

            @block.gpsimd
            def _(e):
                run("pool", e)

            @block.sync
            def _(e):
                run("sp", e)


def _compact(tickets):
    best = {}
    for k, v in tickets:
        if best.get(k, 0) < v:
            best[k] = v
    return list(best.items())


FULL_CFG = dict(D=2048, T=2048, G=8, H=16, FF=8192, DIL=(1, 4, 16))


def build_program(cfg):
    D, T, G, H, FF, DIL = cfg["D"], cfg["T"], cfg["G"], cfg["H"], cfg["FF"], cfg["DIL"]
    KC = D // 128
    C = T
    TT = C + T
    NBLK = TT // 128
    NQ = TT // 512
    NQO = T // 512
    NH = 4 * G
    DSSM = G * 256
    DATT = H * 128
    DMIX = DSSM + DATT
    MKC = DMIX // 128
    FC = FF // 128
    CH = DSSM + 2 * G * 128
    OFF_Z, OFF_X, OFF_B = 0, DSSM, 2 * DSSM
    OFF_C = OFF_B + G * 128
    OFF_DT = OFF_C + G * 128
    OFF_Q = OFF_DT + NH
    OFF_K = OFF_Q + DATT
    OFF_V = OFF_K + DATT
    NIN = OFF_V + DATT
    assert 128 * DIL[2] == T and T % 512 == 0

    nc = bass.Bass("TRN2", target_bir_lowering=False)

    def din(name, shape, dt=F32):
        return nc.dram_tensor(name, list(shape), dt, kind="ExternalInput").ap()

    xc = din("xc", [TT, D])
    flag = din("flag", [128, 1])
    w_in = din("w_in", [D, NIN])
    cwb = din("cwb", [CH, 5])
    hv = din("hv", [3, NH])
    snw = din("snw", [1, DSSM])
    nw4 = din("nw4", [4, D])
    w_out = din("w_out", [DMIX, D])
    w_up = din("w_up", [D, FF])
    w_down = din("w_down", [FF, D])
    out = nc.dram_tensor("out", [T, D], F32, kind="ExternalOutput").ap()
    uT_d = nc.dram_tensor("uT_d", [D, TT], BF16, kind="Internal").ap()
    ymT_d = nc.dram_tensor("ymT_d", [DMIX, T], BF16, kind="Internal").ap()
    wout_b = nc.dram_tensor("wout_b", [DMIX, D], BF16, kind="Internal").ap()
    wup_b = nc.dram_tensor("wup_b", [D, FF], BF16, kind="Internal").ap()
    wdown_b = nc.dram_tensor("wdown_b", [FF, D], BF16, kind="Internal").ap()

    with ExitStack() as top:
        fw = FW(nc, top)
        op = fw.op
        banks = [fw.ps("bank%d" % i, [128, 512], F32) for i in range(8)]

        ident, identd = fw.sb("ident", [128, 128], BF16)
        tri, trid = fw.sb("tri", [128, 128], F32)
        onesf, onesfd = fw.sb("onesf", [128, 128], F32)
        onesb, onesbd = fw.sb("onesb", [128, 128], BF16)
        maskA, maskAd = fw.sb("maskA", [128, 2, 128], BF16)
        maskB, maskBd = fw.sb("maskB", [128, 2, 128], BF16)
        flg, flgd = fw.sb("flg", [128, 1], F32)
        sgt, sgtd = fw.sb("sgt", [128, 128], F32)
        tmpc, tmpcd = fw.sb("tmpc", [128, 128], F32)
        hvb, hvbd = fw.sb("hvb", [128, 3, NH], F32)
        p01 = top.enter_context(ExitStack())
        s_dt, s_dtd = fw.sb("s_dt", [128, NBLK, NH], F32, p01)
        s_dta, s_dtad = fw.sb("s_dta", [128, NBLK, NH], F32, p01)
        s_eacs, s_eacsd = fw.sb("s_eacs", [128, NBLK, NH], F32, p01)
        s_nacs, s_nacsd = fw.sb("s_nacs", [128, NBLK, NH], F32, p01)
        s_dtdte, s_dtdted = fw.sb("s_dtdte", [128, NBLK, NH], F32, p01)
        s_cd, s_cdd = fw.sb("s_cd", [128, NBLK, NH], F32, p01)

        fw.dma("sp", flg[:], flag[:, :], writes=[flgd])
        fw.dma("sp", hvb[:], hv.partition_broadcast(128), writes=[hvbd])
        op("pool", lambda e: e.memset(onesf[:], 1.0), writes=[onesfd])
        op("pool", lambda e: e.memset(onesb[:], 1.0), writes=[onesbd])
        op("pool", lambda e: e.affine_select(out=tri[:], in_=onesf[:], pattern=[[1, 128]], compare_op=ALU.is_ge, fill=0.0, base=0, channel_multiplier=-1), reads=[onesfd], writes=[trid])
        op("pool", lambda e: e.affine_select(out=sgt[:], in_=onesf[:], pattern=[[-1, 128]], compare_op=ALU.is_ge, fill=0.0, base=-1, channel_multiplier=1), reads=[onesfd], writes=[sgtd])
        op("pool", lambda e: e.affine_select(out=tmpc[:], in_=onesf[:], pattern=[[-1, 128]], compare_op=ALU.is_equal, fill=0.0, base=0, channel_multiplier=1), reads=[onesfd], writes=[tmpcd])
        op("dve", lambda e: e.tensor_copy(out=ident[:], in_=tmpc[:]), reads=[tmpcd], writes=[identd])
        op("pool", lambda e: e.affine_select(out=tmpc[:], in_=onesf[:], pattern=[[-1, 128]], compare_op=ALU.is_ge, fill=0.0, base=0, channel_multiplier=1), reads=[onesfd, identd], writes=[tmpcd])
        NEGB = 30000.0
        op("dve", lambda e: e.tensor_scalar(out=maskA[:, 0, :], in0=tmpc[:], scalar1=-1.0, scalar2=NEGB, op0=ALU.add, op1=ALU.mult), reads=[tmpcd], writes=[maskAd])
        op("dve", lambda e: e.tensor_scalar(out=tmpc[:], in0=tmpc[:], scalar1=flg[:, 0:1], scalar2=None, op0=ALU.mult), reads=[tmpcd, flgd], writes=[tmpcd])
        op("dve", lambda e: e.tensor_scalar(out=maskB[:, 0, :], in0=tmpc[:], scalar1=-1.0, scalar2=NEGB, op0=ALU.add, op1=ALU.mult), reads=[tmpcd], writes=[maskBd])
        op("dve", lambda e: e.tensor_scalar(out=maskA[:, 1, :], in0=tri[:], scalar1=-1.0, scalar2=NEGB, op0=ALU.add, op1=ALU.mult), reads=[trid], writes=[maskAd])
        op("dve", lambda e: e.tensor_scalar(out=maskB[:, 1, :], in0=tri[:], scalar1=-1.0, scalar2=NEGB, op0=ALU.add, op1=ALU.mult), reads=[trid], writes=[maskBd])
        op("act", lambda e: e.activation(out=hvb[:, 1, :], in_=hvb[:, 1, :], func=AF.Exp), reads=[hvbd], writes=[hvbd])
        op("dve", lambda e: e.tensor_scalar(out=hvb[:, 1, :], in0=hvb[:, 1, :], scalar1=-1.0, scalar2=None, op0=ALU.mult), reads=[hvbd], writes=[hvbd])

        NCAST = 4
        wcastds = [Dep("wcast%d" % i) for i in range(NCAST)]
        cast_list = []
        for (src, dst, rows, cols) in ((w_out, wout_b, DMIX, D), (w_up, wup_b, D, FF), (w_down, wdown_b, FF, D)):
            for r0 in range(0, rows, 128):
                for c0 in range(0, cols, 2048):
                    c1 = min(cols, c0 + 2048)
                    cast_list.append((dst[r0:r0 + 128, c0:c1], src[r0:r0 + 128, c0:c1]))
        cast_pos = [0]

        def cast_some(n):
            for _ in range(n):
                i = cast_pos[0]
                if i >= len(cast_list):
                    return
                cast_pos[0] += 1
                fw.dma("pool", cast_list[i][0], cast_list[i][1], writes=[wcastds[i % NCAST]], group=False)

        with ExitStack() as ph:
            nwt, nwtd = fw.sb("nwt", [128, D], F32, ph)
            wdt, wdtd = fw.sb("wdt", [128, KC, 128], BF16, ph)
            xr = [fw.sb("xr%d" % i, [128, D], F32, ph) for i in range(2)]
            ur = [fw.sb("ur%d" % i, [128, D], BF16, ph) for i in range(2)]
            junk, junkd = fw.sb("junk", [128, D], BF16, ph)
            uq = [fw.sb("uq%d" % i, [128, KC, 512], BF16, ph) for i in range(2)]
            st0, st0d = fw.sb("st0", [128, NBLK, 4], F32, ph)
            dtt = [fw.sb("dtt%d" % i, [128, 2, NH], F32, ph) for i in range(2)]
            fw.dma("sp", nwt[:], nw4[0, :].partition_broadcast(128), writes=[nwtd])
            fw.dma("pool", wdt[:], w_in[:, OFF_DT + NH - 128:OFF_DT + NH].rearrange("(kc p) n -> p kc n", p=128), writes=[wdtd])
            ptb = [banks[0], banks[1], banks[2], banks[3]]
            nper = 8 if KC >= 8 else KC
            for tb in range(NBLK):
                xt, xtd = xr[tb % 2]
                u, ud = ur[tb % 2]
                uqt, uqd = uq[(tb // 4) % 2]
                c4 = tb % 4
                fw.dma("sp", xt[:], xc[tb * 128:(tb + 1) * 128, :], writes=[xtd])
                op("act", lambda e, xt=xt, tb=tb: e.activation(out=junk[:], in_=xt[:], func=AF.Square, scale=float(D) ** -0.5, accum_out=st0[:, tb, 0:1]), reads=[xtd], writes=[junkd, st0d])
                op("act", lambda e, tb=tb: e.activation(out=st0[:, tb, 1:2], in_=st0[:, tb, 0:1], func=AF.Ln, bias=EPS), reads=[st0d], writes=[st0d])
                op("act", lambda e, tb=tb: e.activation(out=st0[:, tb, 2:3], in_=st0[:, tb, 1:2], func=AF.Exp, scale=-0.5), reads=[st0d], writes=[st0d])
                op("dve", lambda e, xt=xt, u=u, tb=tb: e.scalar_tensor_tensor(out=u[:], in0=xt[:], scalar=st0[:, tb, 2:3], in1=nwt[:], op0=ALU.mult, op1=ALU.mult), reads=[xtd, st0d, nwtd], writes=[ud])
                ngrp = (KC + nper - 1) // nper
                for gi in range(ngrp):
                    pb, pbd = ptb[(tb * ngrp + gi) % 4]
                    pbv = pb[:].bitcast(BF16)
                    n_in = min(nper, KC - gi * nper)
                    for j in range(n_in):
                        kc = gi * nper + j
                        op("pe", lambda e, pbv=pbv, u=u, kc=kc, j=j: e.transpose(out=pbv[:, j * 128:(j + 1) * 128], in_=u[:, kc * 128:(kc + 1) * 128], identity=ident[:]),
                           reads=[ud, identd], writes=[pbd], inc=(j == n_in - 1))
                    eng = "act" if gi % 2 == 0 else "dve"
                    src = pbv[:, 0:n_in * 128].rearrange("p (j t) -> p j t", t=128)
                    dst = uqt[:, gi * nper:gi * nper + n_in, c4 * 128:(c4 + 1) * 128]
                    if eng == "act":
                        op("act", lambda e, src=src, dst=dst: e.copy(out=dst, in_=src), reads=[pbd], writes=[uqd])
                    else:
                        op("dve", lambda e, src=src, dst=dst: e.tensor_copy(out=dst, in_=src), reads=[pbd], writes=[uqd])
                pd, pdd = banks[4 + tb % 2]
                dbg = cfg.get("dbg", 99)
                if dbg == 1:
                    if c4 == 3:
                        qd = tb // 4
                        fw.dma("sp", uT_d[:, qd * 512:(qd + 1) * 512].rearrange("(kc p) t -> p kc t", p=128), uqt[:], reads=[uqd], writes=[Dep("uTd")], chan_dep=uqd)
                    continue
                for kc in range(KC):
                    op("pe", lambda e, pd=pd, uqt=uqt, kc=kc, c4=c4: e.matmul(out=pd[:, 0:128], lhsT=uqt[:, kc, c4 * 128:(c4 + 1) * 128], rhs=wdt[:, kc, :], start=(kc == 0), stop=(kc == KC - 1)),
                       reads=[uqd, wdtd], writes=[pdd], inc=(kc == KC - 1))
                dt_, dtd_ = dtt[tb % 2]
                op("dve", lambda e, pd=pd, dt_=dt_: e.tensor_tensor(out=dt_[:, 0, :], in0=pd[:, 128 - NH:128], in1=hvb[:, 0, :], op=ALU.add), reads=[pdd, hvbd], writes=[dtd_])
                op("act", lambda e, dt_=dt_: e.activation(out=dt_[:, 0, :], in_=dt_[:, 0, :], func=AF.Exp), reads=[dtd_], writes=[dtd_])
                op("act", lambda e, dt_=dt_, tb=tb: e.activation(out=s_dt[:, tb, :], in_=dt_[:, 0, :], func=AF.Ln, bias=1.0), reads=[dtd_], writes=[s_dtd])
                op("dve", lambda e, tb=tb: e.tensor_tensor(out=s_dta[:, tb, :], in0=s_dt[:, tb, :], in1=hvb[:, 1, :], op=ALU.mult), reads=[s_dtd, hvbd], writes=[s_dtad])
                if dbg == 2:
                    if c4 == 3:
                        qd = tb // 4
                        fw.dma("sp", uT_d[:, qd * 512:(qd + 1) * 512].rearrange("(kc p) t -> p kc t", p=128), uqt[:], reads=[uqd], writes=[Dep("uTd")], chan_dep=uqd)
                    continue
                op("pe", lambda e, pd=pd, tb=tb: e.matmul(out=pd[:, 192:192 + NH], lhsT=tri[:], rhs=s_dta[:, tb, :], start=True, stop=True), reads=[trid, s_dtad], writes=[pdd], inc=False)
                op("pe", lambda e, pd=pd, tb=tb: e.matmul(out=pd[:, 256:256 + NH], lhsT=onesf[:], rhs=s_dta[:, tb, :], start=True, stop=True), reads=[onesfd, s_dtad], writes=[pdd])
                op("act", lambda e, pd=pd, tb=tb: e.activation(out=s_eacs[:, tb, :], in_=pd[:, 192:192 + NH], func=AF.Exp), reads=[], writes=[s_eacsd, pdd])
                op("act", lambda e, pd=pd, tb=tb: e.activation(out=s_cd[:, tb, :], in_=pd[:, 256:256 + NH], func=AF.Exp), reads=[], writes=[s_cdd, pdd])
                op("dve", lambda e, pd=pd, tb=tb: e.tensor_scalar(out=s_nacs[:, tb, :], in0=pd[:, 192:192 + NH], scalar1=-1.0, scalar2=None, op0=ALU.mult), reads=[], writes=[s_nacsd, pdd])
                op("dve", lambda e, pd=pd, dt_=dt_, tb=tb: e.tensor_tensor(out=dt_[:, 1, :], in0=pd[:, 256:256 + NH], in1=s_nacs[:, tb, :], op=ALU.add), reads=[s_nacsd], writes=[dtd_, pdd])
                op("act", lambda e, dt_=dt_: e.activation(out=dt_[:, 1, :], in_=dt_[:, 1, :], func=AF.Exp), reads=[dtd_], writes=[dtd_])
                op("dve", lambda e, dt_=dt_, tb=tb: e.tensor_tensor(out=s_dtdte[:, tb, :], in0=dt_[:, 1, :], in1=s_dt[:, tb, :], op=ALU.mult), reads=[dtd_, s_dtd], writes=[s_dtdted])
                if c4 == 3:
                    qd = tb // 4
                    fw.dma("sp", uT_d[:, qd * 512:(qd + 1) * 512].rearrange("(kc p) t -> p kc t", p=128), uqt[:], reads=[uqd], writes=[Dep("uTd")], chan_dep=uqd)
            fw.barrier()
        uT_ready = [(uq[i][1].chan.key, 16 * uq[i][1].chan.n) for i in range(2)]
        if cfg.get("stop") == 0:
            fw.emit()
            return nc

        with ExitStack() as ph:
            uring = [fw.sb("uring%d" % i, [128, KC, 512], BF16, ph) for i in range(3)]
            NW = 12
            wring = [fw.sb("wring%d" % i, [128, KC, 128], BF16, ph) for i in range(NW)]
            wctr = [0]
            uctr = [0]

            def load_w(col0):
                wt, wd = wring[wctr[0] % NW]
                wctr[0] += 1
                fw.dma("pool", wt[:], w_in[:, col0:col0 + 128].rearrange("(kc p) n -> p kc n", p=128), writes=[wd])
                return wt, wd

            unit_cols = []
            for g in range(G):
                unit_cols.append([OFF_Z + g * 256, OFF_Z + g * 256 + 128, OFF_X + g * 256, OFF_X + g * 256 + 128, OFF_B + g * 128, OFF_C + g * 128])
            for hd in range(H):
                unit_cols.append([OFF_Q + hd * 128, OFF_K + hd * 128, OFF_V + hd * 128])
            unit_w = {}

            def prefetch_unit(i):
                if i < len(unit_cols) and i not in unit_w:
                    unit_w[i] = [load_w(c) for c in unit_cols[i]]

            prefetch_unit(0)
            cast_per_quad = -(-len(cast_list) // max(1, (G + H // 2) * NQ))

            def load_u(qd):
                ut, utd = uring[uctr[0] % 3]
                uctr[0] += 1
                fw.dma("sp", ut[:], uT_d[:, qd * 512:(qd + 1) * 512].rearrange("(kc p) t -> p kc t", p=128), writes=[utd], extra=uT_ready)
                return ut, utd

            pjb = [banks[0], banks[1]]
            pjc = [0]

            def proj_fm(wt, wd, ut, utd):
                pb, pbd = pjb[pjc[0] % 2]
                pjc[0] += 1
                for kc in range(KC):
                    op("pe", lambda e, pb=pb, wt=wt, ut=ut, kc=kc: e.matmul(out=pb[:, :], lhsT=wt[:, kc, :], rhs=ut[:, kc, :], start=(kc == 0), stop=(kc == KC - 1)),
                       reads=[wd, utd], writes=[pbd], inc=(kc == KC - 1))
                return pb, pbd

            with ExitStack() as pa:
                assert G % 2 == 0
                snwt, snwtd = fw.sb("snwt", [128, DSSM], F32, pa)
                fw.dma("sp", snwt[:], snw[0, :].partition_broadcast(128), writes=[snwtd])
                op("dve", lambda e: e.tensor_scalar(out=snwt[:], in0=snwt[:], scalar1=0.5, scalar2=None, op0=ALU.mult), reads=[snwtd], writes=[snwtd])
                TB = []
                for th in range(2):
                    n = lambda x, th=th: "%s_%d" % (x, th)
                    TB.append(dict(
                        cw=fw.sb(n("cw"), [128, 4, 5], F32, pa),
                        stage=[fw.sb(n("stage%d" % i), [128, 515], F32, pa) for i in range(2)],
                        acc=[fw.sb(n("acc%d" % i), [128, 512], F32, pa) for i in range(2)],
                        carry=fw.sb(n("carry"), [128, 4, 3], F32, pa),
                        xTq=[fw.sb(n("xTq%d" % i), [128, 2, 512], BF16, pa) for i in range(2)],
                        BTq=[fw.sb(n("BTq%d" % i), [128, 512], BF16, pa) for i in range(2)],
                        CTq=[fw.sb(n("CTq%d" % i), [128, 512], BF16, pa) for i in range(2)],
                        xB=fw.sb(n("xB"), [128, 384], BF16, pa),
                        xdte=fw.sb(n("xdte"), [128, 256], BF16, pa),
                        xdt=fw.sb(n("xdt"), [128, 256], BF16, pa),
                        cbm=fw.sb(n("cbm"), [128, 128], BF16, pa),
                        ldta=[fw.sb(n("ldta%d" % i), [128, 2, 128], F32, pa) for i in range(2)],
                        Lh=[fw.sb(n("Lh%d" % i), [128, 2, 128], F32, pa) for i in range(2)],
                        Mh=[fw.sb(n("Mh%d" % i), [128, 2, 128], BF16, pa) for i in range(2)],
                        S=fw.sb(n("S"), [128, 256], F32, pa),
                        Sb=fw.sb(n("Sb"), [128, 256], BF16, pa),
                        t1=fw.sb(n("t1"), [128, 256], F32, pa),
                        t2=fw.sb(n("t2"), [128, 256], F32, pa),
                        sz=fw.sb(n("sz"), [128, 256], F32, pa),
                        yj=fw.sb(n("yj"), [128, 256], BF16, pa),
                        yo=fw.sb(n("yo"), [128, 256], BF16, pa),
                        gst=fw.sb(n("gst"), [128, 4], F32, pa),
                        yT=fw.sb(n("yTs"), [128, 2, T], BF16, pa),
                    ))
                b_tr, b_trd = banks[2]
                b_cb, b_cbd = banks[2]
                b_ar, b_ard = banks[3]
                b_ys = [banks[4], banks[5]]
                b_z, b_zd = banks[6]
                b_st, b_std = banks[7]

                def run_rr(gens):
                    gens = list(gens)
                    while gens:
                        for gn in list(gens):
                            try:
                                next(gn)
                            except StopIteration:
                                gens.remove(gn)

                def ssd_quad(th, g, qd, ut, utd, uw, part):
                    Bf = TB[th]
                    cw, cwd = Bf["cw"]
                    stage, acc = Bf["stage"], Bf["acc"]
                    carry, carryd = Bf["carry"]
                    xTq, xTqd = Bf["xTq"][qd % 2]
                    BTq, BTqd = Bf["BTq"][qd % 2]
                    CTq, CTqd = Bf["CTq"][qd % 2]
                    xB, xBd = Bf["xB"]
                    xdte, xdted = Bf["xdte"]
                    xdt, xdtd = Bf["xdt"]
                    cbm, cbmd = Bf["cbm"]
                    ldta, Lh, Mh = Bf["ldta"], Bf["Lh"], Bf["Mh"]
                    S, Sd = Bf["S"]
                    Sb, Sbd = Bf["Sb"]
                    t1, t1d = Bf["t1"]
                    t2, t2d = Bf["t2"]
                    sz, szd = Bf["sz"]
                    yj, yjd = Bf["yj"]
                    yo, yod = Bf["yo"]
                    gst, gstd = Bf["gst"]
                    yT, yTd = Bf["yT"]
                    wz = [uw[0], uw[1]]
                    wx = [uw[2], uw[3]]
                    wB = uw[4]
                    wC = uw[5]
                    b_y, b_yd = b_ys[th]
                    sto = th * 256
                    own = qd >= NQ - NQO
                    chunks = [(0, wx[0], xTq[:, 0, :], xTqd), (1, wx[1], xTq[:, 1, :], xTqd), (2, wB, BTq[:, :], BTqd)]
                    if own or qd == NQ - NQO - 1:
                        chunks.append((3, wC, CTq[:, :], CTqd))
                    for ci, (wt, wd), dst, dstd in (chunks if part == "proj" else []):
                        pb, pbd = proj_fm(wt, wd, ut, utd)
                        sg, sgd = stage[ci % 2]
                        ac, acd = acc[ci % 2]
                        op("dve", lambda e, sg=sg, ci=ci: e.tensor_copy(out=sg[:, 0:3], in_=carry[:, ci, :]), reads=[carryd], writes=[sgd])
                        op("act", lambda e, sg=sg, pb=pb: e.copy(out=sg[:, 3:515], in_=pb[:, :]), reads=[pbd], writes=[sgd])
                        op("dve", lambda e, sg=sg, ci=ci: e.tensor_copy(out=carry[:, ci, :], in_=sg[:, 512:515]), reads=[sgd], writes=[carryd])
                        yield
                        if ci == 3 and not own:
                            continue
                        op("dve", lambda e, sg=sg, ac=ac, ci=ci: e.tensor_scalar(out=ac[:], in0=sg[:, 3:515], scalar1=cw[:, ci, 3:4], scalar2=cw[:, ci, 4:5], op0=ALU.mult, op1=ALU.add), reads=[sgd, cwd], writes=[acd])
                        for k in (2, 1, 0):
                            op("dve", lambda e, sg=sg, ac=ac, ci=ci, k=k: e.scalar_tensor_tensor(out=ac[:], in0=sg[:, k:k + 512], scalar=cw[:, ci, k:k + 1], in1=ac[:], op0=ALU.mult, op1=ALU.add), reads=[sgd, cwd, acd], writes=[acd])
                        op("act", lambda e, ac=ac, sg=sg: e.activation(out=sg[:, 3:515], in_=ac[:], func=AF.Tanh), reads=[acd], writes=[sgd])
                        op("dve", lambda e, ac=ac, sg=sg, dst=dst: e.scalar_tensor_tensor(out=dst, in0=sg[:, 3:515], scalar=1.0, in1=ac[:], op0=ALU.add, op1=ALU.mult), reads=[sgd, acd], writes=[dstd])
                        yield
                    for c in (range(4) if part == "core" else []):
                        tb = qd * 4 + c
                        cs = slice(c * 128, (c + 1) * 128)
                        hs = slice(g * 4, g * 4 + 4)
                        trv = b_tr[:].bitcast(BF16)
                        if own:
                            for hp in range(2):
                                hh = g * 4 + 2 * hp
                                ld_, ldd_ = ldta[hp % 2]
                                L_, Ld_ = Lh[hp % 2]
                                ar = 2 * th * 128
                                op("pool", lambda e, ld_=ld_, tb=tb, hh=hh: e.tensor_tensor(out=ld_[:], in0=sgt[:].unsqueeze(1).to_broadcast([128, 2, 128]),
                                                                                 in1=s_dta[:, tb, hh:hh + 2].unsqueeze(2).to_broadcast([128, 2, 128]), op=ALU.mult), reads=[sgtd, s_dtad], writes=[ldd_])
                                for k2 in range(2):
                                    op("pe", lambda e, ld_=ld_, ar=ar, k2=k2: e.matmul(out=b_ar[:, ar + k2 * 128:ar + (k2 + 1) * 128], lhsT=ld_[:, k2, :], rhs=tri[:], start=True, stop=True), reads=[ldd_, trid], writes=[b_ard], inc=(k2 == 1))
                                yield
                                op("act", lambda e, L_=L_, ar=ar: e.activation(out=L_[:].rearrange("p a b -> p (a b)"), in_=b_ar[:, ar:ar + 256], func=AF.Exp), reads=[b_ard], writes=[Ld_])
                                yield
                        for i in range(2):
                            op("pe", lambda e, i=i, cs=cs, trv=trv: e.transpose(out=trv[:, i * 128:(i + 1) * 128], in_=xTq[:, i, cs], identity=ident[:]), reads=[xTqd, identd], writes=[b_trd], inc=False)
                        op("pe", lambda e, cs=cs, trv=trv: e.transpose(out=trv[:, 256:384], in_=BTq[:, cs], identity=ident[:]), reads=[BTqd, identd], writes=[b_trd])
                        op("act", lambda e, trv=trv: e.copy(out=xB[:], in_=trv[:, 0:384]), reads=[b_trd], writes=[xBd])
                        yield
                        if own:
                            for i in range(2):
                                for kc in range(KC):
                                    op("pe", lambda e, kc=kc, i=i, cs=cs: e.matmul(out=b_z[:, i * 128:(i + 1) * 128], lhsT=ut[:, kc, cs], rhs=wz[i][0][:, kc, :], start=(kc == 0), stop=(kc == KC - 1)),
                                       reads=[utd, wz[i][1]], writes=[b_zd], inc=(kc == KC - 1 and i == 1))
                            op("act", lambda e: e.activation(out=sz[:], in_=b_z[:, 0:256], func=AF.Tanh, scale=0.5), reads=[b_zd], writes=[szd])
                            op("dve", lambda e: e.scalar_tensor_tensor(out=sz[:], in0=sz[:], scalar=1.0, in1=b_z[:, 0:256], op0=ALU.add, op1=ALU.mult), reads=[szd, b_zd], writes=[szd])
                            yield
                        op("dve", lambda e, tb=tb, hs=hs: e.tensor_tensor(out=xdte[:].rearrange("p (h d) -> p h d", d=64), in0=xB[:, 0:256].rearrange("p (h d) -> p h d", d=64),
                                                                  in1=s_dtdte[:, tb, hs].unsqueeze(2).to_broadcast([128, 4, 64]), op=ALU.mult), reads=[xBd, s_dtdted], writes=[xdted])
                        op("pe", lambda e: e.matmul(out=b_st[:, sto:sto + 256], lhsT=xB[:, 256:384], rhs=xdte[:], start=True, stop=True), reads=[xBd, xdted], writes=[b_std])
                        yield
                        if own:
                            op("pe", lambda e, cs=cs: e.matmul(out=b_cb[:, 256:384], lhsT=BTq[:, cs], rhs=CTq[:, cs], start=True, stop=True), reads=[BTqd, CTqd], writes=[b_cbd])
                            op("dve", lambda e: e.tensor_tensor(out=cbm[:], in0=b_cb[:, 256:384], in1=tri[:], op=ALU.mult), reads=[b_cbd, trid], writes=[cbmd])
                            op("dve", lambda e, tb=tb, hs=hs: e.tensor_tensor(out=xdt[:].rearrange("p (h d) -> p h d", d=64), in0=xB[:, 0:256].rearrange("p (h d) -> p h d", d=64),
                                                                      in1=s_dt[:, tb, hs].unsqueeze(2).to_broadcast([128, 4, 64]), op=ALU.mult), reads=[xBd, s_dtd], writes=[xdtd])
                            op("pe", lambda e, cs=cs: e.matmul(out=b_y[:, 256:512], lhsT=CTq[:, cs], rhs=Sb[:], start=True, stop=True), reads=[CTqd, Sbd], writes=[b_yd])
                            yield
                        op("dve", lambda e, tb=tb, hs=hs: e.tensor_tensor(out=S[:].rearrange("p (h d) -> p h d", d=64), in0=S[:].rearrange("p (h d) -> p h d", d=64),
                                                                  in1=s_cd[:, tb, hs].unsqueeze(2).to_broadcast([128, 4, 64]), op=ALU.mult), reads=[Sd, s_cdd], writes=[Sd])
                        op("dve", lambda e: e.tensor_tensor(out=S[:], in0=b_st[:, sto:sto + 256], in1=S[:], op=ALU.add), reads=[b_std, Sd], writes=[Sd])
                        if tb == NBLK - T // 128 - 1:
                            op("dve", lambda e: e.tensor_scalar(out=S[:], in0=S[:], scalar1=flg[:, 0:1], scalar2=None, op0=ALU.mult), reads=[Sd, flgd], writes=[Sd])
                        op("act", lambda e: e.copy(out=Sb[:], in_=S[:]), reads=[Sd], writes=[Sbd])
                        yield
                        if own:
                            for hp in range(2):
                                L_, Ld_ = Lh[hp % 2]
                                M_, Md_ = Mh[hp % 2]
                                op("pool", lambda e, L_=L_, M_=M_: e.tensor_tensor(out=M_[:], in0=cbm[:].unsqueeze(1).to_broadcast([128, 2, 128]), in1=L_[:], op=ALU.mult), reads=[cbmd, Ld_], writes=[Md_])
                                yield
                                for k2 in range(2):
                                    h = 2 * hp + k2
                                    op("pe", lambda e, M_=M_, h=h, k2=k2: e.matmul(out=b_y[:, h * 64:(h + 1) * 64], lhsT=M_[:, k2, :], rhs=xdt[:, h * 64:(h + 1) * 64], start=True, stop=True), reads=[Md_, xdtd], writes=[b_yd], inc=(h == 3))
                            yield
                            op("pool", lambda e, hs=hs: e.tensor_tensor(out=t2[:].rearrange("p (h d) -> p h d", d=64), in0=xB[:, 0:256].rearrange("p (h d) -> p h d", d=64),
                                                                 in1=hvb[:, 2, hs].unsqueeze(2).to_broadcast([128, 4, 64]), op=ALU.mult), reads=[xBd, hvbd], writes=[t2d])
                            op("dve", lambda e, tb=tb, hs=hs: e.tensor_tensor(out=t1[:].rearrange("p (h d) -> p h d", d=64), in0=b_y[:, 256:512].rearrange("p (h d) -> p h d", d=64),
                                                                      in1=s_eacs[:, tb, hs].unsqueeze(2).to_broadcast([128, 4, 64]), op=ALU.mult), reads=[b_yd, s_eacsd], writes=[t1d])
                            op("dve", lambda e: e.tensor_tensor(out=t1[:], in0=b_y[:, 0:256], in1=t1[:], op=ALU.add), reads=[b_yd, t1d], writes=[t1d])
                            yield
                            op("pool", lambda e: e.tensor_tensor(out=t1[:], in0=t1[:], in1=t2[:], op=ALU.add), reads=[t1d, t2d], writes=[t1d])
                            op("dve", lambda e: e.tensor_tensor(out=t1[:], in0=t1[:], in1=sz[:], op=ALU.mult), reads=[t1d, szd], writes=[t1d])
                            yield
                            op("act", lambda e: e.activation(out=yj[:], in_=t1[:], func=AF.Square, scale=1.0 / 32.0, accum_out=gst[:, 0:1]), reads=[t1d], writes=[yjd, gstd])
                            op("act", lambda e: e.activation(out=gst[:, 1:2], in_=gst[:, 0:1], func=AF.Ln, bias=EPS), reads=[gstd], writes=[gstd])
                            op("act", lambda e: e.activation(out=gst[:, 2:3], in_=gst[:, 1:2], func=AF.Exp, scale=-0.5), reads=[gstd], writes=[gstd])
                            yield
                            op("dve", lambda e: e.scalar_tensor_tensor(out=yo[:], in0=t1[:], scalar=gst[:, 2:3], in1=snwt[:, g * 256:(g + 1) * 256], op0=ALU.mult, op1=ALU.mult), reads=[t1d, gstd, snwtd], writes=[yod])
                            zv = b_z[:].bitcast(BF16)
                            for i in range(2):
                                op("pe", lambda e, i=i, zv=zv: e.transpose(out=zv[:, 512 + i * 128:512 + (i + 1) * 128], in_=yo[:, i * 128:(i + 1) * 128], identity=ident[:]), reads=[yod, identd], writes=[b_zd], inc=(i == 1))
                            to = (qd - (NQ - NQO)) * 512 + c * 128
                            op("act", lambda e, zv=zv, to=to: e.copy(out=yT[:, :, to:to + 128], in_=zv[:, 512:768].rearrange("p (i t) -> p i t", t=128)), reads=[b_zd], writes=[yTd])
                            yield

                for g0 in range(0, G, 2):
                    grp = (g0, g0 + 1)
                    for th, g in enumerate(grp):
                        prefetch_unit(g)
                        cw, cwd = TB[th]["cw"]
                        for i, r0 in enumerate((g * 256, g * 256 + 128, DSSM + g * 128, DSSM + G * 128 + g * 128)):
                            fw.dma("sp", cw[:, i, :], cwb[r0:r0 + 128, :], writes=[cwd])
                        op("dve", lambda e, cw=cw: e.tensor_scalar(out=cw[:], in0=cw[:], scalar1=0.5, scalar2=None, op0=ALU.mult), reads=[cwd], writes=[cwd])
                        S, Sd = TB[th]["S"]
                        Sb, Sbd = TB[th]["Sb"]
                        carry, carryd = TB[th]["carry"]
                        op("dve", lambda e, S=S: e.memset(S[:], 0.0), writes=[Sd])
                        op("dve", lambda e, Sb=Sb: e.memset(Sb[:], 0.0), writes=[Sbd])
                        op("dve", lambda e, carry=carry: e.memset(carry[:], 0.0), writes=[carryd])
                    uts = {0: load_u(0)}
                    run_rr([ssd_quad(th, g, 0, uts[0][0], uts[0][1], unit_w[g], "proj") for th, g in enumerate(grp)])
                    for qd in range(NQ):
                        cast_some(2 * cast_per_quad)
                        gens = [ssd_quad(th, g, qd, uts[qd][0], uts[qd][1], unit_w[g], "core") for th, g in enumerate(grp)]
                        if qd + 1 < NQ:
                            uts[qd + 1] = load_u(qd + 1)
                            gens += [ssd_quad(th, g, qd + 1, uts[qd + 1][0], uts[qd + 1][1], unit_w[g], "proj") for th, g in enumerate(grp)]
                        run_rr(gens)
                    for th, g in enumerate(grp):
                        yT, yTd = TB[th]["yT"]
                        fw.dma("sp", ymT_d[g * 256:(g + 1) * 256, :].rearrange("(i p) t -> p i t", p=128), yT[:], reads=[yTd], writes=[Dep("ym")], chan_dep=yTd)
                ym_ready = [(TB[i]["yT"][1].chan.key, 16 * TB[i]["yT"][1].chan.n) for i in range(2) if TB[i]["yT"][1].chan is not None]
                fw.barrier()
                if cfg.get("stop") == 1:
                    fw.emit()
                    return nc

            with ExitStack() as pa:
                KTs = [fw.sb("KT%d" % i, [128, TT], BF16, pa) for i in range(2)]
                VTs = [fw.sb("VT%d" % i, [128, TT], BF16, pa) for i in range(2)]
                QTs = [fw.sb("QT%d" % i, [128, T], BF16, pa) for i in range(2)]
                NVB = T // 128 + DIL[2]
                Vd_, Vdd_ = fw.sb("Vd", [128, NVB, 128], BF16, pa)
                aacc, aaccd = fw.sb("aacc", [128, 2, T], F32, pa)
                PT = [fw.sb("PT%d" % i, [128, 2, 128], BF16, pa) for i in range(2)]
                PM = [fw.sb("PM%d" % i, [128, 2, 128], BF16, pa) for i in range(3)]
                yA = [fw.sb("yA%d" % i, [128, T], BF16, pa) for i in range(2)]
                b_vt = [banks[2], banks[2]]
                b_s = [banks[3], banks[4], banks[5]]
                b_o = [banks[6], banks[7]]
                uc = [0]
                SKEW = 2

                def att_proj(hd):
                    prefetch_unit(G + hd)
                    prefetch_unit(G + hd + 1)
                    wq, wk, wv = unit_w[G + hd]
                    KT, KTd = KTs[hd % 2]
                    VT, VTd = VTs[hd % 2]
                    QT, QTd = QTs[hd % 2]
                    for qd in range(NQ):
                        own = qd >= NQ - NQO
                        ut, utd = load_u(qd)
                        cast_some(cast_per_quad)
                        ts_ = slice(qd * 512, (qd + 1) * 512)
                        pb, pbd = proj_fm(wk[0], wk[1], ut, utd)
                        op("act", lambda e, pb=pb, ts_=ts_: e.copy(out=KT[:, ts_], in_=pb[:, :]), reads=[pbd], writes=[KTd])
                        yield
                        pb, pbd = proj_fm(wv[0], wv[1], ut, utd)
                        op("dve", lambda e, pb=pb, ts_=ts_: e.tensor_copy(out=VT[:, ts_], in_=pb[:, :]), reads=[pbd], writes=[VTd])
                        yield
                        if own:
                            to = (qd - (NQ - NQO)) * 512
                            pb, pbd = proj_fm(wq[0], wq[1], ut, utd)
                            op("act", lambda e, pb=pb, to=to: e.activation(out=QT[:, to:to + 512], in_=pb[:, :], func=AF.Copy, scale=128.0 ** -0.5), reads=[pbd], writes=[QTd])
                            yield

                def att_units(hd):
                    KT, KTd = KTs[hd % 2]
                    VT, VTd = VTs[hd % 2]
                    QT, QTd = QTs[hd % 2]
                    first = True
                    for d in DIL:
                        nj = T // (128 * d)
                        nblk = d * (nj + 1)
                        for b0 in range(0, nblk, 4):
                            vb, vbd = b_vt[(b0 // 4) % 2]
                            vbv = vb[:].bitcast(BF16)
                            nb = min(4, nblk - b0)
                            for i in range(nb):
                                r, jj = divmod(b0 + i, nj + 1)
                                st_ = C + (jj - 1) * 128 * d + r
                                op("pe", lambda e, vbv=vbv, i=i, st_=st_, d=d: e.transpose(out=vbv[:, i * 128:(i + 1) * 128], in_=VT[:, st_:st_ + 127 * d + 1:d], identity=ident[:]), reads=[VTd, identd], writes=[vbd], inc=(i == nb - 1))
                            op("act", lambda e, vbv=vbv, b0=b0, nb=nb: e.copy(out=Vd_[:, b0:b0 + nb, :], in_=vbv[:, 0:nb * 128].rearrange("p (i t) -> p i t", t=128)), reads=[vbd], writes=[Vdd_])
                            yield
                        units = [(r, j) for r in range(d) for j in range(nj)]
                        info = {}

                        def stage_a(idx):
                            r, j = units[idx]
                            u_i = uc[0]
                            uc[0] += 1
                            bs, bsd = b_s[u_i % 3]
                            Pm_, Pmd_ = PM[u_i % 3]
                            q0 = j * 128 * d + r
                            qsl = slice(q0, q0 + 127 * d + 1, d)
                            kcur = slice(C + q0, C + q0 + 127 * d + 1, d)
                            kprev = slice(C + q0 - 128 * d, C + q0 - d + 1, d)
                            mk, mkd = (maskB, maskBd) if j == 0 else (maskA, maskAd)
                            op("pe", lambda e: e.matmul(out=bs[:, 0:256], lhsT=ident[:], rhs=mk[:].rearrange("p a b -> p (a b)"), start=True, stop=False), reads=[identd, mkd], writes=[bsd], inc=False)
                            op("pe", lambda e: e.matmul(out=bs[:, 0:128], lhsT=KT[:, kprev], rhs=QT[:, qsl], start=False, stop=False), reads=[KTd, QTd], writes=[bsd], inc=False)
                            op("pe", lambda e: e.matmul(out=bs[:, 128:256], lhsT=KT[:, kcur], rhs=QT[:, qsl], start=False, stop=True), reads=[KTd, QTd], writes=[bsd])
                            op("act", lambda e: e.activation(out=Pm_[:].rearrange("p a b -> p (a b)"), in_=bs[:, 0:256], func=AF.Exp), reads=[bsd], writes=[Pmd_])
                            info[idx] = (u_i, Pm_, Pmd_, qsl)

                        def stage_b(idx, first):
                            r, j = units[idx]
                            u_i, Pm_, Pmd_, qsl = info.pop(idx)
                            bo, bod = b_o[u_i % 2]
                            vi_prev = r * (nj + 1) + j
                            vi_cur = vi_prev + 1
                            op("pe", lambda e: e.matmul(out=bo[:, 0:128], lhsT=Vd_[:, vi_prev, :], rhs=Pm_[:, 0, :], start=True, stop=False), reads=[Vdd_, Pmd_], writes=[bod], inc=False)
                            op("pe", lambda e: e.matmul(out=bo[:, 0:128], lhsT=Vd_[:, vi_cur, :], rhs=Pm_[:, 1, :], start=False, stop=True), reads=[Vdd_, Pmd_], writes=[bod], inc=False)
                            op("pe", lambda e: e.matmul(out=bo[:, 128:256], lhsT=onesb[:], rhs=Pm_[:, 0, :], start=True, stop=False), reads=[onesbd, Pmd_], writes=[bod], inc=False)
                            op("pe", lambda e: e.matmul(out=bo[:, 128:256], lhsT=onesb[:], rhs=Pm_[:, 1, :], start=False, stop=True), reads=[onesbd, Pmd_], writes=[bod])
                            src = bo[:, 0:256].rearrange("p (a t) -> p a t", t=128)
                            if first:
                                op("dve", lambda e: e.tensor_copy(out=aacc[:, :, qsl], in_=src), reads=[bod], writes=[aaccd])
                            else:
                                op("dve", lambda e: e.tensor_tensor(out=aacc[:, :, qsl], in0=src, in1=aacc[:, :, qsl], op=ALU.add), reads=[bod, aaccd], writes=[aaccd])

                        n_u = len(units)
                        for idx in range(n_u + SKEW):
                            if idx < n_u:
                                stage_a(idx)
                            if idx >= SKEW:
                                stage_b(idx - SKEW, first)
                            yield
                        first = False
                    ya, yad = yA[hd % 2]
                    op("dve", lambda e: e.reciprocal(out=aacc[:, 1, :], in_=aacc[:, 1, :]), reads=[aaccd], writes=[aaccd])
                    op("dve", lambda e: e.tensor_tensor(out=ya[:], in0=aacc[:, 0, :], in1=aacc[:, 1, :], op=ALU.mult), reads=[aaccd], writes=[yad])
                    fw.dma("sp", ymT_d[DSSM + hd * 128:DSSM + (hd + 1) * 128, :], ya[:], reads=[yad], writes=[Dep("ym")], chan_dep=yad)
                    yield

                for _ in att_proj(0):
                    pass
                for hd in range(H):
                    gu = att_units(hd)
                    gp = att_proj(hd + 1) if hd + 1 < H else None
                    alive_u = True
                    while alive_u or gp is not None:
                        for _ in range(3):
                            if alive_u:
                                try:
                                    next(gu)
                                except StopIteration:
                                    alive_u = False
                        if gp is not None:
                            try:
                                next(gp)
                            except StopIteration:
                                gp = None
                ym_ready += [(yA[i][1].chan.key, 16 * yA[i][1].chan.n) for i in range(2) if yA[i][1].chan is not None]
                fw.barrier()
        cast_some(len(cast_list))
        wcast_ready = [(d_.chan.key, 16 * d_.chan.n) for d_ in wcastds if d_.chan is not None]
        p01.close()

        with ExitStack() as ph:
            nwpost, nwpostd = fw.sb("nwA", [128, D], F32, ph)
            nwpre, nwpred = fw.sb("nwB", [128, D], F32, ph)
            nwfin, nwfind = nwpost, nwpostd
            fw.dma("sp", nwpre[:], nw4[2, :].partition_broadcast(128), writes=[nwpred])
            NWP = 4
            PR = 8
            wp = [fw.sb("wp%d" % i, [128, PR, 512], BF16, ph) for i in range(NWP)]
            wpc = [0]
            big, bigd = fw.sb("big", [128, max(MKC, FC), 512], BF16, ph)
            xh, xhd = fw.sb("xh", [128, 4, D], F32, ph)
            mf, mfd = fw.sb("mf", [128, 4, D], F32, ph)
            hn, hnd = fw.sb("hn", [128, D], BF16, ph)
            hnT, hnTd = fw.sb("hnT", [128, KC, 512], BF16, ph)
            rls = [fw.sb("rl%d" % i, [128, 512], F32, ph) for i in range(2)]
            st2, st2d = fw.sb("st2", [128, 4, 12], F32, ph)

            def load_wp(src, r0, nrow_chunks, c0, ncol):
                wt, wd = wp[wpc[0] % NWP]
                wpc[0] += 1
                fw.dma("sp", wt[:, 0:nrow_chunks, 0:ncol], src[r0:r0 + nrow_chunks * 128, c0:c0 + ncol].rearrange("(kc p) n -> p kc n", p=128), writes=[wd], extra=wcast_ready)
                return wt, wd

            NCB = D // 512 if D >= 512 else 1
            CBW = min(512, D)
            for qd in range(NQO):
                tq = slice(qd * 512, (qd + 1) * 512)
                for k0 in range(0, MKC, 16):
                    k1 = min(MKC, k0 + 16)
                    fw.dma("sp", big[:, k0:k1, :], ymT_d[k0 * 128:k1 * 128, tq].rearrange("(kc p) t -> p kc t", p=128), writes=[bigd], extra=ym_ready)
                fw.dma("sp", xh[:], xc[C + qd * 512:C + (qd + 1) * 512, :].rearrange("(a p) d -> p a d", p=128), writes=[xhd])
                for cb in range(NCB):
                    npiece = (MKC + PR - 1) // PR
                    for pi in range(npiece):
                        nr = min(PR, MKC - pi * PR)
                        wt, wd = load_wp(wout_b, pi * PR * 128, nr, cb * CBW, CBW)
                        for tb in range(4):
                            pb, pbd = banks[(cb % 2) * 4 + tb]
                            for k in range(nr):
                                kc = pi * PR + k
                                op("pe", lambda e, pb=pb, wt=wt, k=k, kc=kc, tb=tb: e.matmul(out=pb[:, 0:CBW], lhsT=big[:, kc, tb * 128:(tb + 1) * 128], rhs=wt[:, k, 0:CBW], start=(kc == 0), stop=(kc == MKC - 1)),
                                   reads=[bigd, wd], writes=[pbd], inc=(k == nr - 1))
                    for tb in range(4):
                        pb, pbd = banks[(cb % 2) * 4 + tb]
                        if tb % 2 == 0:
                            op("act", lambda e, pb=pb, tb=tb, cb=cb: e.copy(out=mf[:, tb, cb * CBW:(cb + 1) * CBW], in_=pb[:, 0:CBW]), reads=[pbd], writes=[mfd])
                        else:
                            op("dve", lambda e, pb=pb, tb=tb, cb=cb: e.tensor_copy(out=mf[:, tb, cb * CBW:(cb + 1) * CBW], in_=pb[:, 0:CBW]), reads=[pbd], writes=[mfd])
                fw.dma("sp", nwpost[:], nw4[1, :].partition_broadcast(128), writes=[nwpostd])
                junk2, junk2d = hn, hnd
                for tb in range(4):
                    op("act", lambda e, tb=tb: e.activation(out=junk2[:], in_=mf[:, tb, :], func=AF.Square, scale=float(D) ** -0.5, accum_out=st2[:, tb, 0:1]), reads=[mfd], writes=[junk2d, st2d])
                    op("act", lambda e, tb=tb: e.activation(out=st2[:, tb, 1:2], in_=st2[:, tb, 0:1], func=AF.Ln, bias=EPS), reads=[st2d], writes=[st2d])
                    op("act", lambda e, tb=tb: e.activation(out=st2[:, tb, 2:3], in_=st2[:, tb, 1:2], func=AF.Exp, scale=-0.5), reads=[st2d], writes=[st2d])
                    op("dve", lambda e, tb=tb: e.scalar_tensor_tensor(out=mf[:, tb, :], in0=mf[:, tb, :], scalar=st2[:, tb, 2:3], in1=nwpost[:], op0=ALU.mult, op1=ALU.mult), reads=[mfd, st2d, nwpostd], writes=[mfd])
                    op("pool", lambda e, tb=tb: e.tensor_tensor(out=xh[:, tb, :], in0=xh[:, tb, :], in1=mf[:, tb, :], op=ALU.add), reads=[xhd, mfd], writes=[xhd])
                    op("act", lambda e, tb=tb: e.activation(out=junk2[:], in_=xh[:, tb, :], func=AF.Square, scale=float(D) ** -0.5, accum_out=st2[:, tb, 3:4]), reads=[xhd], writes=[junk2d, st2d])
                    op("act", lambda e, tb=tb: e.activation(out=st2[:, tb, 4:5], in_=st2[:, tb, 3:4], func=AF.Ln, bias=EPS), reads=[st2d], writes=[st2d])
                    op("act", lambda e, tb=tb: e.activation(out=st2[:, tb, 5:6], in_=st2[:, tb, 4:5], func=AF.Exp, scale=-0.5), reads=[st2d], writes=[st2d])
                    op("dve", lambda e, tb=tb: e.scalar_tensor_tensor(out=hn[:], in0=xh[:, tb, :], scalar=st2[:, tb, 5:6], in1=nwpre[:], op0=ALU.mult, op1=ALU.mult), reads=[xhd, st2d, nwpred], writes=[hnd])
                    nper = 8 if KC >= 8 else KC
                    ngrp = (KC + nper - 1) // nper
                    for gi in range(ngrp):
                        pb, pbd = banks[(tb * ngrp + gi) % 8]
                        pbv = pb[:].bitcast(BF16)
                        n_in = min(nper, KC - gi * nper)
                        for j in range(n_in):
                            kc = gi * nper + j
                            op("pe", lambda e, pbv=pbv, kc=kc, j=j: e.transpose(out=pbv[:, j * 128:(j + 1) * 128], in_=hn[:, kc * 128:(kc + 1) * 128], identity=ident[:]), reads=[hnd, identd], writes=[pbd], inc=(j == n_in - 1))
                        op("act", lambda e, pbv=pbv, gi=gi, n_in=n_in, tb=tb: e.copy(out=hnT[:, gi * nper:gi * nper + n_in, tb * 128:(tb + 1) * 128], in_=pbv[:, 0:n_in * 128].rearrange("p (j t) -> p j t", t=128)), reads=[pbd], writes=[hnTd])
                nfc_per = 4
                for f0 in range(0, FC, nfc_per):
                    nf = min(nfc_per, FC - f0)
                    halves = []
                    for k0 in range(0, KC, PR):
                        nr = min(PR, KC - k0)
                        halves.append((load_wp(wup_b, k0 * 128, nr, f0 * 128, nf * 128), k0, nr))
                    for fi in range(nf):
                        fc = f0 + fi
                        pb, pbd = banks[fc % 8]
                        for (wt, wd), k0, nr in halves:
                            for k in range(nr):
                                kc = k0 + k
                                op("pe", lambda e, pb=pb, wt=wt, fi=fi, k=k, kc=kc: e.matmul(out=pb[:, :], lhsT=wt[:, k, fi * 128:(fi + 1) * 128], rhs=hnT[:, kc, :], start=(kc == 0), stop=(kc == KC - 1)),
                                   reads=[wd, hnTd], writes=[pbd], inc=(kc == KC - 1))
                        rl, rld = rls[fc % 2]
                        op("act", lambda e, pb=pb, rl=rl: e.activation(out=rl[:], in_=pb[:, :], func=AF.Relu), reads=[pbd], writes=[rld])
                        op("dve", lambda e, fc=fc, rl=rl: e.tensor_tensor(out=big[:, fc, :], in0=rl[:], in1=rl[:], op=ALU.mult), reads=[rld], writes=[bigd])
                for cb in range(NCB):
                    npiece = (FC + PR - 1) // PR
                    for pi in range(npiece):
                        nr = min(PR, FC - pi * PR)
                        wt, wd = load_wp(wdown_b, pi * PR * 128, nr, cb * CBW, CBW)
                        for tb in range(4):
                            pb, pbd = banks[(cb % 2) * 4 + tb]
                            for k in range(nr):
                                fc = pi * PR + k
                                op("pe", lambda e, pb=pb, wt=wt, k=k, fc=fc, tb=tb: e.matmul(out=pb[:, 0:CBW], lhsT=big[:, fc, tb * 128:(tb + 1) * 128], rhs=wt[:, k, 0:CBW], start=(fc == 0), stop=(fc == FC - 1)),
                                   reads=[bigd, wd], writes=[pbd], inc=(k == nr - 1))
                    for tb in range(4):
                        pb, pbd = banks[(cb % 2) * 4 + tb]
                        if tb % 2 == 0:
                            op("act", lambda e, pb=pb, tb=tb, cb=cb: e.copy(out=mf[:, tb, cb * CBW:(cb + 1) * CBW], in_=pb[:, 0:CBW]), reads=[pbd], writes=[mfd])
                        else:
                            op("dve", lambda e, pb=pb, tb=tb, cb=cb: e.tensor_copy(out=mf[:, tb, cb * CBW:(cb + 1) * CBW], in_=pb[:, 0:CBW]), reads=[pbd], writes=[mfd])
                fw.dma("sp", nwfin[:], nw4[3, :].partition_broadcast(128), writes=[nwfind])
                for tb in range(4):
                    op("act", lambda e, tb=tb: e.activation(out=junk2[:], in_=mf[:, tb, :], func=AF.Square, scale=float(D) ** -0.5, accum_out=st2[:, tb, 6:7]), reads=[mfd], writes=[junk2d, st2d])
                    op("act", lambda e, tb=tb: e.activation(out=st2[:, tb, 7:8], in_=st2[:, tb, 6:7], func=AF.Ln, bias=EPS), reads=[st2d], writes=[st2d])
                    op("act", lambda e, tb=tb: e.activation(out=st2[:, tb, 8:9], in_=st2[:, tb, 7:8], func=AF.Exp, scale=-0.5), reads=[st2d], writes=[st2d])
                    op("dve", lambda e, tb=tb: e.scalar_tensor_tensor(out=mf[:, tb, :], in0=mf[:, tb, :], scalar=st2[:, tb, 8:9], in1=nwfin[:], op0=ALU.mult, op1=ALU.mult), reads=[mfd, st2d, nwfind], writes=[mfd])
                    op("pool", lambda e, tb=tb: e.tensor_tensor(out=mf[:, tb, :], in0=xh[:, tb, :], in1=mf[:, tb, :], op=ALU.add), reads=[xhd, mfd], writes=[mfd])
                fw.dma("sp", out[tq, :].rearrange("(a p) d -> p a d", p=128), mf[:], reads=[mfd], writes=[Dep("out")], chan_dep=mfd)
            fw.barrier()
        fw.emit()
    return nc


_CACHE = {}


def make_in_maps(cfg, ncores, inputs):
    D, T, G = cfg["D"], cfg["T"], cfg["G"]
    x = np.asarray(inputs["x"], np.float32)
    B_, S_, _ = x.shape
    halves = S_ // T
    assert halves == 2 and B_ * halves == ncores
    shared = {
        "w_in": np.ascontiguousarray(np.asarray(inputs["w_in"], np.float32)[0]),
        "cwb": np.ascontiguousarray(np.concatenate([np.asarray(inputs["conv_w"], np.float32)[0].T, np.asarray(inputs["conv_b"], np.float32)[0][:, None]], axis=1)),
        "hv": np.ascontiguousarray(np.stack([np.asarray(inputs["dt_bias"], np.float32)[0], np.asarray(inputs["a_log"], np.float32)[0], np.asarray(inputs["d_skip"], np.float32)[0]], axis=0)),
        "snw": np.ascontiguousarray(np.asarray(inputs["ssm_norm_w"], np.float32)),
        "nw4": np.ascontiguousarray(np.stack([np.asarray(inputs[k], np.float32)[0] for k in ("norm_mix_pre", "norm_mix_post", "norm_mlp_pre", "norm_mlp_post")], axis=0)),
        "w_out": np.ascontiguousarray(np.asarray(inputs["w_out"], np.float32)[0]),
        "w_up": np.ascontiguousarray(np.asarray(inputs["w_up"], np.float32)[0]),
        "w_down": np.ascontiguousarray(np.asarray(inputs["w_down"], np.float32)[0]),
    }
    maps = []
    for c in range(ncores):
        b, h = divmod(c, 2)
        if h == 0:
            xcc = np.concatenate([np.zeros((T, D), np.float32), x[b, :T]], axis=0)
            fl = np.zeros((128, 1), np.float32)
        else:
            xcc = x[b]
            fl = np.ones((128, 1), np.float32)
        m = dict(shared)
        m["xc"] = np.ascontiguousarray(xcc)
        m["flag"] = fl
        maps.append(m)
    return maps


def kernel(**inputs):
    cfg = FULL_CFG
    if "nc" not in _CACHE:
        _CACHE["nc"] = build_program(cfg)
    nc = _CACHE["nc"]
    maps = make_in_maps(cfg, 8, inputs)
    res = run_bass_kernel_spmd(nc, maps, core_ids=list(range(8)))
    x = inputs["x"]
    B_, S_, D = x.shape
    T = cfg["T"]
    outp = np.empty((B_, S_, D), np.float32)
    for c in range(8):
        b, h = divmod(c, 2)
        outp[b, h * T:(h + 1) * T] = res.results[c]["out"]
    return outp
```

```python
from contextlib import ExitStack
import numpy as np
import concourse.bass as bass
import concourse.mybir as mybir
from concourse.bass_utils import run_bass_kernel_spmd

F32 = mybir.dt.float32
BF16 = mybir.dt.bfloat16
AF = mybir.ActivationFunctionType
ALU = mybir.AluOpType

ENGS = ("pe", "act", "dve", "pool", "sp")
EPS = 1e-6


class Dep:
    __slots__ = ("w", "r", "chan", "name")

    def __init__(self, name=""):
        self.w = None
        self.r = []
        self.chan = None
        self.name = name


class Chan:
    __slots__ = ("key", "sem", "n")


class FW:
    def __init__(self, nc, stack, same_engine_sync=True):
        self.nc = nc
        self.stack = stack
        self.streams = {e: [] for e in ENGS}
        self.cnt = {e: 0 for e in ENGS}
        self.seen = {e: {} for e in ENGS}
        self.sems = {}
        self.latest = {}
        self.nchan = 0
        self.same = same_engine_sync
        for e in ENGS:
            self.sems[e] = stack.enter_context(nc.semaphore("s_" + e))
            self.latest[e] = 0

    def sb(self, name, shape, dtype, stack=None):
        t = (stack or self.stack).enter_context(self.nc.sbuf_tensor(name, list(shape), dtype))
        return t, Dep(name)

    def ps(self, name, shape, dtype=F32, stack=None):
        t = (stack or self.stack).enter_context(self.nc.psum_tensor(name, list(shape), dtype))
        return t, Dep(name)

    def chan_of(self, dep):
        if dep.chan is None:
            c = Chan()
            c.key = "c%d" % self.nchan
            self.nchan += 1
            c.sem = self.stack.enter_context(self.nc.semaphore("d_" + c.key))
            c.n = 0
            self.sems[c.key] = c.sem
            self.latest[c.key] = 0
            dep.chan = c
        return dep.chan

    def _waits(self, eng, reads, writes, extra, group_key=None):
        need = {}

        def add(t):
            if t is None:
                return
            k, v = t
            if need.get(k, 0) < v:
                need[k] = v

        for d in reads:
            add(d.w)
        for d in writes:
            if not (group_key is not None and d.w is not None and d.w[0] == group_key):
                add(d.w)
            for t in d.r:
                add(t)
        for t in extra:
            add(t)
        st = self.streams[eng]
        for k, v in need.items():
            if k == eng and (eng == "pe" or not self.same):
                continue
            if self.seen[eng].get(k, 0) >= v:
                continue
            self.seen[eng][k] = v
            st.append(("wait", k, v))

    def op(self, eng, fn, reads=(), writes=(), extra=(), inc=True):
        self._waits(eng, reads, writes, extra)
        if inc:
            self.cnt[eng] += 1
            self.latest[eng] = self.cnt[eng]
        ticket = (eng, self.cnt[eng] if inc else self.cnt[eng] + 1)
        self.streams[eng].append(("op", fn, eng if inc else None))
        for d in reads:
            d.r.append(ticket)
            if len(d.r) > 24:
                d.r = _compact(d.r)
        for d in writes:
            d.w = ticket
            d.r = []
        return ticket

    def dma(self, q, out_ap, in_ap, reads=(), writes=(), extra=(), chan_dep=None, group=True, **kw):
        cd = chan_dep if chan_dep is not None else writes[0]
        ch = self.chan_of(cd)
        self._waits(q, reads, writes, extra, group_key=ch.key if group else None)
        ch.n += 1
        ticket = (ch.key, 16 * ch.n)
        self.latest[ch.key] = 16 * ch.n

        def fn(e, out_ap=out_ap, in_ap=in_ap, kw=kw):
            return e.dma_start(out=out_ap, in_=in_ap, **kw)

        self.streams[q].append(("dma", fn, ch.key))
        for d in reads:
            d.r.append(ticket)
        for d in writes:
            d.w = ticket
            d.r = []
        return ticket

    def barrier(self, engs=ENGS):
        for e in engs:
            st = self.streams[e]
            for k, v in self.latest.items():
                if v == 0 or self.seen[e].get(k, 0) >= v:
                    continue
                if k == e:
                    continue
                self.seen[e][k] = v
                st.append(("wait", k, v))

    def emit(self):
        nc = self.nc
        sems = self.sems
        streams = self.streams
        with nc.Block() as block:

            def run(engname, e):
                for item in streams[engname]:
                    if item[0] == "wait":
                        e.wait_ge(sems[item[1]], item[2])
                    elif item[0] == "op":
                        ins = item[1](e)
                        if item[2] is not None:
                            ins.then_inc(sems[item[2]], 1)
                    else:
                        ins = item[1](e)
                        ins.then_inc(sems[item[2]], 16)

            @block.tensor
            def _(e):
                run("pe", e)

            @block.scalar
            def _(e):
                run("act", e)

            @block.vector
            def _(e):
                run("dve", e)

            @block.gpsimd
            def _(e):
                run("pool", e)

            @block.sync
            def _(e):
                run("sp", e)


def _compact(tickets):
    best = {}
    for k, v in tickets:
        if best.get(k, 0) < v:
            best[k] = v
    return list(best.items())


FULL_CFG = dict(D=2048, T=2048, G=8, H=16, FF=8192, DIL=(1, 4, 16))


def build_program(cfg):
    D, T, G, H, FF, DIL = cfg["D"], cfg["T"], cfg["G"], cfg["H"], cfg["FF"], cfg["DIL"]
    KC = D // 128
    C = T
    TT = C + T
    NBLK = TT // 128
    NQ = TT // 512
    NQO = T // 512
    NH = 4 * G
    DSSM = G * 256
    DATT = H * 128
    DMIX = DSSM + DATT
    MKC = DMIX // 128
    FC = FF // 128
    CH = DSSM + 2 * G * 128
    OFF_Z, OFF_X, OFF_B = 0, DSSM, 2 * DSSM
    OFF_C = OFF_B + G * 128
    OFF_DT = OFF_C + G * 128
    OFF_Q = OFF_DT + NH
    OFF_K = OFF_Q + DATT
    OFF_V = OFF_K + DATT
    NIN = OFF_V + DATT
    assert 128 * DIL[2] == T and T % 512 == 0

    nc = bass.Bass("TRN2", target_bir_lowering=False)

    def din(name, shape, dt=F32):
        return nc.dram_tensor(name, list(shape), dt, kind="ExternalInput").ap()

    xc = din("xc", [TT, D])
    flag = din("flag", [128, 1])
    w_in = din("w_in", [D, NIN])
    cwb = din("cwb", [CH, 5])
    hv = din("hv", [3, NH])
    snw = din("snw", [1, DSSM])
    nw4 = din("nw4", [4, D])
    w_out = din("w_out", [DMIX, D])
    w_up = din("w_up", [D, FF])
    w_down = din("w_down", [FF, D])
    out = nc.dram_tensor("out", [T, D], F32, kind="ExternalOutput").ap()
    uT_d = nc.dram_tensor("uT_d", [D, TT], BF16, kind="Internal").ap()
    ymT_d = nc.dram_tensor("ymT_d", [DMIX, T], BF16, kind="Internal").ap()
    wout_b = nc.dram_tensor("wout_b", [DMIX, D], BF16, kind="Internal").ap()
    wup_b = nc.dram_tensor("wup_b", [D, FF], BF16, kind="Internal").ap()
    wdown_b = nc.dram_tensor("wdown_b", [FF, D], BF16, kind="Internal").ap()

    with ExitStack() as top:
        fw = FW(nc, top)
        op = fw.op
        banks = [fw.ps("bank%d" % i, [128, 512], F32) for i in range(8)]

        ident, identd = fw.sb("ident", [128, 128], BF16)
        tri, trid = fw.sb("tri", [128, 128], F32)
        onesf, onesfd = fw.sb("onesf", [128, 128], F32)
        onesb, onesbd = fw.sb("onesb", [128, 128], BF16)
        maskA, maskAd = fw.sb("maskA", [128, 2, 128], BF16)
        maskB, maskBd = fw.sb("maskB", [128, 2, 128], BF16)
        flg, flgd = fw.sb("flg", [128, 1], F32)
        sgt, sgtd = fw.sb("sgt", [128, 128], F32)
        tmpc, tmpcd = fw.sb("tmpc", [128, 128], F32)
        hvb, hvbd = fw.sb("hvb", [128, 3, NH], F32)
        p01 = top.enter_context(ExitStack())
        s_dt, s_dtd = fw.sb("s_dt", [128, NBLK, NH], F32, p01)
        s_dta, s_dtad = fw.sb("s_dta", [128, NBLK, NH], F32, p01)
        s_eacs, s_eacsd = fw.sb("s_eacs", [128, NBLK, NH], F32, p01)
        s_nacs, s_nacsd = fw.sb("s_nacs", [128, NBLK, NH], F32, p01)
        s_dtdte, s_dtdted = fw.sb("s_dtdte", [128, NBLK, NH], F32, p01)
        s_cd, s_cdd = fw.sb("s_cd", [128, NBLK, NH], F32, p01)

        fw.dma("sp", flg[:], flag[:, :], writes=[flgd])
        fw.dma("sp", hvb[:], hv.partition_broadcast(128), writes=[hvbd])
        op("pool", lambda e: e.memset(onesf[:], 1.0), writes=[onesfd])
        op("pool", lambda e: e.memset(onesb[:], 1.0), writes=[onesbd])
        op("pool", lambda e: e.affine_select(out=tri[:], in_=onesf[:], pattern=[[1, 128]], compare_op=ALU.is_ge, fill=0.0, base=0, channel_multiplier=-1), reads=[onesfd], writes=[trid])
        op("pool", lambda e: e.affine_select(out=sgt[:], in_=onesf[:], pattern=[[-1, 128]], compare_op=ALU.is_ge, fill=0.0, base=-1, channel_multiplier=1), reads=[onesfd], writes=[sgtd])
        op("pool", lambda e: e.affine_select(out=tmpc[:], in_=onesf[:], pattern=[[-1, 128]], compare_op=ALU.is_equal, fill=0.0, base=0, channel_multiplier=1), reads=[onesfd], writes=[tmpcd])
        op("dve", lambda e: e.tensor_copy(out=ident[:], in_=tmpc[:]), reads=[tmpcd], writes=[identd])
        op("pool", lambda e: e.affine_select(out=tmpc[:], in_=onesf[:], pattern=[[-1, 128]], compare_op=ALU.is_ge, fill=0.0, base=0, channel_multiplier=1), reads=[onesfd, identd], writes=[tmpcd])
        NEGB = 30000.0
        op("dve", lambda e: e.tensor_scalar(out=maskA[:, 0, :], in0=tmpc[:], scalar1=-1.0, scalar2=NEGB, op0=ALU.add, op1=ALU.mult), reads=[tmpcd], writes=[maskAd])
        op("dve", lambda e: e.tensor_scalar(out=tmpc[:], in0=tmpc[:], scalar1=flg[:, 0:1], scalar2=None, op0=ALU.mult), reads=[tmpcd, flgd], writes=[tmpcd])
        op("dve", lambda e: e.tensor_scalar(out=maskB[:, 0, :], in0=tmpc[:], scalar1=-1.0, scalar2=NEGB, op0=ALU.add, op1=ALU.mult), reads=[tmpcd], writes=[maskBd])
        op("dve", lambda e: e.tensor_scalar(out=maskA[:, 1, :], in0=tri[:], scalar1=-1.0, scalar2=NEGB, op0=ALU.add, op1=ALU.mult), reads=[trid], writes=[maskAd])
        op("dve", lambda e: e.tensor_scalar(out=maskB[:, 1, :], in0=tri[:], scalar1=-1.0, scalar2=NEGB, op0=ALU.add, op1=ALU.mult), reads=[trid], writes=[maskBd])
        op("act", lambda e: e.activation(out=hvb[:, 1, :], in_=hvb[:, 1, :], func=AF.Exp), reads=[hvbd], writes=[hvbd])
        op("dve", lambda e: e.tensor_scalar(out=hvb[:, 1, :], in0=hvb[:, 1, :], scalar1=-1.0, scalar2=None, op0=ALU.mult), reads=[hvbd], writes=[hvbd])

        NCAST = 4
        wcastds = [Dep("wcast%d" % i) for i in range(NCAST)]
        cast_list = []
        for (src, dst, rows, cols) in ((w_out, wout_b, DMIX, D), (w_up, wup_b, D, FF), (w_down, wdown_b, FF, D)):
            for r0 in range(0, rows, 128):
                for c0 in range(0, cols, 2048):
                    c1 = min(cols, c0 + 2048)
                    cast_list.append((dst[r0:r0 + 128, c0:c1], src[r0:r0 + 128, c0:c1]))
        cast_pos = [0]

        def cast_some(n):
            for _ in range(n):
                i = cast_pos[0]
                if i >= len(cast_list):
                    return
                cast_pos[0] += 1
                fw.dma("pool", cast_list[i][0], cast_list[i][1], writes=[wcastds[i % NCAST]], group=False)

        with ExitStack() as ph:
            nwt, nwtd = fw.sb("nwt", [128, D], F32, ph)
            wdt, wdtd = fw.sb("wdt", [128, KC, 128], BF16, ph)
            xr = [fw.sb("xr%d" % i, [128, D], F32, ph) for i in range(2)]
            ur = [fw.sb("ur%d" % i, [128, D], BF16, ph) for i in range(2)]
            junk, junkd = fw.sb("junk", [128, D], BF16, ph)
            uq = [fw.sb("uq%d" % i, [128, KC, 512], BF16, ph) for i in range(2)]
            st0, st0d = fw.sb("st0", [128, NBLK, 4], F32, ph)
            dtt = [fw.sb("dtt%d" % i, [128, 2, NH], F32, ph) for i in range(2)]
            fw.dma("sp", nwt[:], nw4[0, :].partition_broadcast(128), writes=[nwtd])
            fw.dma("pool", wdt[:], w_in[:, OFF_DT + NH - 128:OFF_DT + NH].rearrange("(kc p) n -> p kc n", p=128), writes=[wdtd])
            ptb = [banks[0], banks[1], banks[2], banks[3]]
            nper = 8 if KC >= 8 else KC
            for tb in range(NBLK):
                xt, xtd = xr[tb % 2]
                u, ud = ur[tb % 2]
                uqt, uqd = uq[(tb // 4) % 2]
                c4 = tb % 4
                fw.dma("sp", xt[:], xc[tb * 128:(tb + 1) * 128, :], writes=[xtd])
                op("act", lambda e, xt=xt, tb=tb: e.activation(out=junk[:], in_=xt[:], func=AF.Square, scale=float(D) ** -0.5, accum_out=st0[:, tb, 0:1]), reads=[xtd], writes=[junkd, st0d])
                op("act", lambda e, tb=tb: e.activation(out=st0[:, tb, 1:2], in_=st0[:, tb, 0:1], func=AF.Ln, bias=EPS), reads=[st0d], writes=[st0d])
                op("act", lambda e, tb=tb: e.activation(out=st0[:, tb, 2:3], in_=st0[:, tb, 1:2], func=AF.Exp, scale=-0.5), reads=[st0d], writes=[st0d])
                op("dve", lambda e, xt=xt, u=u, tb=tb: e.scalar_tensor_tensor(out=u[:], in0=xt[:], scalar=st0[:, tb, 2:3], in1=nwt[:], op0=ALU.mult, op1=ALU.mult), reads=[xtd, st0d, nwtd], writes=[ud])
                ngrp = (KC + nper - 1) // nper
                for gi in range(ngrp):
                    pb, pbd = ptb[(tb * ngrp + gi) % 4]
                    pbv = pb[:].bitcast(BF16)
                    n_in = min(nper, KC - gi * nper)
                    for j in range(n_in):
                        kc = gi * nper + j
                        op("pe", lambda e, pbv=pbv, u=u, kc=kc, j=j: e.transpose(out=pbv[:, j * 128:(j + 1) * 128], in_=u[:, kc * 128:(kc + 1) * 128], identity=ident[:]),
                           reads=[ud, identd], writes=[pbd], inc=(j == n_in - 1))
                    eng = "act" if gi % 2 == 0 else "dve"
                    src = pbv[:, 0:n_in * 128].rearrange("p (j t) -> p j t", t=128)
                    dst = uqt[:, gi * nper:gi * nper + n_in, c4 * 128:(c4 + 1) * 128]
                    if eng == "act":
                        op("act", lambda e, src=src, dst=dst: e.copy(out=dst, in_=src), reads=[pbd], writes=[uqd])
                    else:
                        op("dve", lambda e, src=src, dst=dst: e.tensor_copy(out=dst, in_=src), reads=[pbd], writes=[uqd])
                pd, pdd = banks[4 + tb % 2]
                dbg = cfg.get("dbg", 99)
                if dbg == 1:
                    if c4 == 3:
                        qd = tb // 4
                        fw.dma("sp", uT_d[:, qd * 512:(qd + 1) * 512].rearrange("(kc p) t -> p kc t", p=128), uqt[:], reads=[uqd], writes=[Dep("uTd")], chan_dep=uqd)
                    continue
                for kc in range(KC):
                    op("pe", lambda e, pd=pd, uqt=uqt, kc=kc, c4=c4: e.matmul(out=pd[:, 0:128], lhsT=uqt[:, kc, c4 * 128:(c4 + 1) * 128], rhs=wdt[:, kc, :], start=(kc == 0), stop=(kc == KC - 1)),
                       reads=[uqd, wdtd], writes=[pdd], inc=(kc == KC - 1))
                dt_, dtd_ = dtt[tb % 2]
                op("dve", lambda e, pd=pd, dt_=dt_: e.tensor_tensor(out=dt_[:, 0, :], in0=pd[:, 128 - NH:128], in1=hvb[:, 0, :], op=ALU.add), reads=[pdd, hvbd], writes=[dtd_])
                op("act", lambda e, dt_=dt_: e.activation(out=dt_[:, 0, :], in_=dt_[:, 0, :], func=AF.Exp), reads=[dtd_], writes=[dtd_])
                op("act", lambda e, dt_=dt_, tb=tb: e.activation(out=s_dt[:, tb, :], in_=dt_[:, 0, :], func=AF.Ln, bias=1.0), reads=[dtd_], writes=[s_dtd])
                op("dve", lambda e, tb=tb: e.tensor_tensor(out=s_dta[:, tb, :], in0=s_dt[:, tb, :], in1=hvb[:, 1, :], op=ALU.mult), reads=[s_dtd, hvbd], writes=[s_dtad])
                if dbg == 2:
                    if c4 == 3:
                        qd = tb // 4
                        fw.dma("sp", uT_d[:, qd * 512:(qd + 1) * 512].rearrange("(kc p) t -> p kc t", p=128), uqt[:], reads=[uqd], writes=[Dep("uTd")], chan_dep=uqd)
                    continue
                op("pe", lambda e, pd=pd, tb=tb: e.matmul(out=pd[:, 192:192 + NH], lhsT=tri[:], rhs=s_dta[:, tb, :], start=True, stop=True), reads=[trid, s_dtad], writes=[pdd], inc=False)
                op("pe", lambda e, pd=pd, tb=tb: e.matmul(out=pd[:, 256:256 + NH], lhsT=onesf[:], rhs=s_dta[:, tb, :], start=True, stop=True), reads=[onesfd, s_dtad], writes=[pdd])
                op("act", lambda e, pd=pd, tb=tb: e.activation(out=s_eacs[:, tb, :], in_=pd[:, 192:192 + NH], func=AF.Exp), reads=[], writes=[s_eacsd, pdd])
                op("act", lambda e, pd=pd, tb=tb: e.activation(out=s_cd[:, tb, :], in_=pd[:, 256:256 + NH], func=AF.Exp), reads=[], writes=[s_cdd, pdd])
                op("dve", lambda e, pd=pd, tb=tb: e.tensor_scalar(out=s_nacs[:, tb, :], in0=pd[:, 192:192 + NH], scalar1=-1.0, scalar2=None, op0=ALU.mult), reads=[], writes=[s_nacsd, pdd])
                op("dve", lambda e, pd=pd, dt_=dt_, tb=tb: e.tensor_tensor(out=dt_[:, 1, :], in0=pd[:, 256:256 + NH], in1=s_nacs[:, tb, :], op=ALU.add), reads=[s_nacsd], writes=[dtd_, pdd])
                op("act", lambda e, dt_=dt_: e.activation(out=dt_[:, 1, :], in_=dt_[:, 1, :], func=AF.Exp), reads=[dtd_], writes=[dtd_])
                op("dve", lambda e, dt_=dt_, tb=tb: e.tensor_tensor(out=s_dtdte[:, tb, :], in0=dt_[:, 1, :], in1=s_dt[:, tb, :], op=ALU.mult), reads=[dtd_, s_dtd], writes=[s_dtdted])
                if c4 == 3:
                    qd = tb // 4
                    fw.dma("sp", uT_d[:, qd * 512:(qd + 1) * 512].rearrange("(kc p) t -> p kc t", p=128), uqt[:], reads=[uqd], writes=[Dep("uTd")], chan_dep=uqd)
            fw.barrier()
        uT_ready = [(uq[i][1].chan.key, 16 * uq[i][1].chan.n) for i in range(2)]
        if cfg.get("stop") == 0:
            fw.emit()
            return nc

        with ExitStack() as ph:
            uring = [fw.sb("uring%d" % i, [128, KC, 512], BF16, ph) for i in range(3)]
            NW = 12
            wring = [fw.sb("wring%d" % i, [128, KC, 128], BF16, ph) for i in range(NW)]
            wctr = [0]
            uctr = [0]

            def load_w(col0):
                wt, wd = wring[wctr[0] % NW]
                wctr[0] += 1
                fw.dma("pool", wt[:], w_in[:, col0:col0 + 128].rearrange("(kc p) n -> p kc n", p=128), writes=[wd])
                return wt, wd

            unit_cols = []
            for g in range(G):
                unit_cols.append([OFF_Z + g * 256, OFF_Z + g * 256 + 128, OFF_X + g * 256, OFF_X + g * 256 + 128, OFF_B + g * 128, OFF_C + g * 128])
            for hd in range(H):
                unit_cols.append([OFF_Q + hd * 128, OFF_K + hd * 128, OFF_V + hd * 128])
            unit_w = {}

            def prefetch_unit(i):
                if i < len(unit_cols) and i not in unit_w:
                    unit_w[i] = [load_w(c) for c in unit_cols[i]]

            prefetch_unit(0)
            cast_per_quad = -(-len(cast_list) // max(1, (G + H // 2) * NQ))

            def load_u(qd):
                ut, utd = uring[uctr[0] % 3]
                uctr[0] += 1
                fw.dma("sp", ut[:], uT_d[:, qd * 512:(qd + 1) * 512].rearrange("(kc p) t -> p kc t", p=128), writes=[utd], extra=uT_ready)
                return ut, utd

            pjb = [banks[0], banks[1]]
            pjc = [0]

            def proj_fm(wt, wd, ut, utd):
                pb, pbd = pjb[pjc[0] % 2]
                pjc[0] += 1
                for kc in range(KC):
                    op("pe", lambda e, pb=pb, wt=wt, ut=ut, kc=kc: e.matmul(out=pb[:, :], lhsT=wt[:, kc, :], rhs=ut[:, kc, :], start=(kc == 0), stop=(kc == KC - 1)),
                       reads=[wd, utd], writes=[pbd], inc=(kc == KC - 1))
                return pb, pbd

            with ExitStack() as pa:
                assert G % 2 == 0
                snwt, snwtd = fw.sb("snwt", [128, DSSM], F32, pa)
                fw.dma("sp", snwt[:], snw[0, :].partition_broadcast(128), writes=[snwtd])
                TB = []
                for th in range(2):
                    n = lambda x, th=th: "%s_%d" % (x, th)
                    TB.append(dict(
                        cw=fw.sb(n("cw"), [128, 4, 5], F32, pa),
                        stage=[fw.sb(n("stage%d" % i), [128, 515], F32, pa) for i in range(2)],
                        acc=[fw.sb(n("acc%d" % i), [128, 512], F32, pa) for i in range(2)],
                        carry=fw.sb(n("carry"), [128, 4, 3], F32, pa),
                        xTq=[fw.sb(n("xTq%d" % i), [128, 2, 512], BF16, pa) for i in range(2)],
                        BTq=[fw.sb(n("BTq%d" % i), [128, 512], BF16, pa) for i in range(2)],
                        CTq=[fw.sb(n("CTq%d" % i), [128, 512], BF16, pa) for i in range(2)],
                        xB=fw.sb(n("xB"), [128, 384], BF16, pa),
                        xdte=fw.sb(n("xdte"), [128, 256], BF16, pa),
                        xdt=fw.sb(n("xdt"), [128, 256], BF16, pa),
                        cbm=fw.sb(n("cbm"), [128, 128], BF16, pa),
                        ldta=[fw.sb(n("ldta%d" % i), [128, 2, 128], F32, pa) for i in range(2)],
                        Lh=[fw.sb(n("Lh%d" % i), [128, 2, 128], F32, pa) for i in range(2)],
                        Mh=[fw.sb(n("Mh%d" % i), [128, 2, 128], BF16, pa) for i in range(2)],
                        S=fw.sb(n("S"), [128, 256], F32, pa),
                        Sb=fw.sb(n("Sb"), [128, 256], BF16, pa),
                        t1=fw.sb(n("t1"), [128, 256], F32, pa),
                        t2=fw.sb(n("t2"), [128, 256], F32, pa),
                        sz=fw.sb(n("sz"), [128, 256], F32, pa),
                        yj=fw.sb(n("yj"), [128, 256], BF16, pa),
                        yo=fw.sb(n("yo"), [128, 256], BF16, pa),
                        gst=fw.sb(n("gst"), [128, 4], F32, pa),
                        yT=fw.sb(n("yTs"), [128, 2, T], BF16, pa),
                    ))
                b_tr, b_trd = banks[2]
                b_cb, b_cbd = banks[2]
                b_ar, b_ard = banks[3]
                b_ys = [banks[4], banks[5]]
                b_z, b_zd = banks[6]
                b_st, b_std = banks[7]

                def run_rr(gens):
                    gens = list(gens)
                    while gens:
                        for gn in list(gens):
                            try:
                                next(gn)
                            except StopIteration:
                                gens.remove(gn)

                def ssd_quad(th, g, qd, ut, utd, uw, part):
                    Bf = TB[th]
                    cw, cwd = Bf["cw"]
                    stage, acc = Bf["stage"], Bf["acc"]
                    carry, carryd = Bf["carry"]
                    xTq, xTqd = Bf["xTq"][qd % 2]
                    BTq, BTqd = Bf["BTq"][qd % 2]
                    CTq, CTqd = Bf["CTq"][qd % 2]
                    xB, xBd = Bf["xB"]
                    xdte, xdted = Bf["xdte"]
                    xdt, xdtd = Bf["xdt"]
                    cbm, cbmd = Bf["cbm"]
                    ldta, Lh, Mh = Bf["ldta"], Bf["Lh"], Bf["Mh"]
                    S, Sd = Bf["S"]
                    Sb, Sbd = Bf["Sb"]
                    t1, t1d = Bf["t1"]
                    t2, t2d = Bf["t2"]
                    sz, szd = Bf["sz"]
                    yj, yjd = Bf["yj"]
                    yo, yod = Bf["yo"]
                    gst, gstd = Bf["gst"]
                    yT, yTd = Bf["yT"]
                    wz = [uw[0], uw[1]]
                    wx = [uw[2], uw[3]]
                    wB = uw[4]
                    wC = uw[5]
                    b_y, b_yd = b_ys[th]
                    sto = th * 256
                    own = qd >= NQ - NQO
                    chunks = [(0, wx[0], xTq[:, 0, :], xTqd), (1, wx[1], xTq[:, 1, :], xTqd), (2, wB, BTq[:, :], BTqd)]
                    if own or qd == NQ - NQO - 1:
                        chunks.append((3, wC, CTq[:, :], CTqd))
                    for ci, (wt, wd), dst, dstd in (chunks if part == "proj" else []):
                        pb, pbd = proj_fm(wt, wd, ut, utd)
                        sg, sgd = stage[ci % 2]
                        ac, acd = acc[ci % 2]
                        op("dve", lambda e, sg=sg, ci=ci: e.tensor_copy(out=sg[:, 0:3], in_=carry[:, ci, :]), reads=[carryd], writes=[sgd])
                        op("act", lambda e, sg=sg, pb=pb: e.copy(out=sg[:, 3:515], in_=pb[:, :]), reads=[pbd], writes=[sgd])
                        op("dve", lambda e, sg=sg, ci=ci: e.tensor_copy(out=carry[:, ci, :], in_=sg[:, 512:515]), reads=[sgd], writes=[carryd])
                        yield
                        if ci == 3 and not own:
                            continue
                        op("dve", lambda e, sg=sg, ac=ac, ci=ci: e.tensor_scalar(out=ac[:], in0=sg[:, 3:515], scalar1=cw[:, ci, 3:4], scalar2=cw[:, ci, 4:5], op0=ALU.mult, op1=ALU.add), reads=[sgd, cwd], writes=[acd])
                        for k in (2, 1, 0):
                            op("dve", lambda e, sg=sg, ac=ac, ci=ci, k=k: e.scalar_tensor_tensor(out=ac[:], in0=sg[:, k:k + 512], scalar=cw[:, ci, k:k + 1], in1=ac[:], op0=ALU.mult, op1=ALU.add), reads=[sgd, cwd, acd], writes=[acd])
                        sgv = sg[:, 3:515]
                        op("act", lambda e, ac=ac, sgv=sgv: e.activation(out=sgv, in_=ac[:], func=AF.Exp, scale=-1.0), reads=[acd], writes=[sgd])
                        op("act", lambda e, sgv=sgv: e.activation(out=sgv, in_=sgv, func=AF.Ln, bias=1.0), reads=[sgd], writes=[sgd])
                        op("act", lambda e, sgv=sgv: e.activation(out=sgv, in_=sgv, func=AF.Exp, scale=-1.0), reads=[sgd], writes=[sgd])
                        op("dve", lambda e, ac=ac, sgv=sgv, dst=dst: e.tensor_tensor(out=dst, in0=ac[:], in1=sgv, op=ALU.mult), reads=[sgd, acd], writes=[dstd])
                        yield
                    for c in (range(4) if part == "core" else []):
                        tb = qd * 4 + c
                        cs = slice(c * 128, (c + 1) * 128)
                        hs = slice(g * 4, g * 4 + 4)
                        trv = b_tr[:].bitcast(BF16)
                        if own:
                            for hp in range(2):
                                hh = g * 4 + 2 * hp
                                ld_, ldd_ = ldta[hp % 2]
                                L_, Ld_ = Lh[hp % 2]
                                ar = 2 * th * 128
                                op("pool", lambda e, ld_=ld_, tb=tb, hh=hh: e.tensor_tensor(out=ld_[:], in0=sgt[:].unsqueeze(1).to_broadcast([128, 2, 128]),
                                                                                 in1=s_dta[:, tb, hh:hh + 2].unsqueeze(2).to_broadcast([128, 2, 128]), op=ALU.mult), reads=[sgtd, s_dtad], writes=[ldd_])
                                for k2 in range(2):
                                    op("pe", lambda e, ld_=ld_, ar=ar, k2=k2: e.matmul(out=b_ar[:, ar + k2 * 128:ar + (k2 + 1) * 128], lhsT=ld_[:, k2, :], rhs=tri[:], start=True, stop=True), reads=[ldd_, trid], writes=[b_ard], inc=(k2 == 1))
                                yield
                                op("act", lambda e, L_=L_, ar=ar: e.activation(out=L_[:].rearrange("p a b -> p (a b)"), in_=b_ar[:, ar:ar + 256], func=AF.Exp), reads=[b_ard], writes=[Ld_])
                                yield
                        for i in range(2):
                            op("pe", lambda e, i=i, cs=cs, trv=trv: e.transpose(out=trv[:, i * 128:(i + 1) * 128], in_=xTq[:, i, cs], identity=ident[:]), reads=[xTqd, identd], writes=[b_trd], inc=False)
                        op("pe", lambda e, cs=cs, trv=trv: e.transpose(out=trv[:, 256:384], in_=BTq[:, cs], identity=ident[:]), reads=[BTqd, identd], writes=[b_trd])
                        op("act", lambda e, trv=trv: e.copy(out=xB[:], in_=trv[:, 0:384]), reads=[b_trd], writes=[xBd])
                        yield
                        if own:
                            for i in range(2):
                                for kc in range(KC):
                                    op("pe", lambda e, kc=kc, i=i, cs=cs: e.matmul(out=b_z[:, i * 128:(i + 1) * 128], lhsT=ut[:, kc, cs], rhs=wz[i][0][:, kc, :], start=(kc == 0), stop=(kc == KC - 1)),
                                       reads=[utd, wz[i][1]], writes=[b_zd], inc=(kc == KC - 1 and i == 1))
                            op("act", lambda e: e.activation(out=sz[:], in_=b_z[:, 0:256], func=AF.Exp, scale=-1.0), reads=[b_zd], writes=[szd])
                            op("act", lambda e: e.activation(out=sz[:], in_=sz[:], func=AF.Ln, bias=1.0), reads=[szd], writes=[szd])
                            op("act", lambda e: e.activation(out=sz[:], in_=sz[:], func=AF.Exp, scale=-1.0), reads=[szd], writes=[szd])
                            op("dve", lambda e: e.tensor_tensor(out=sz[:], in0=b_z[:, 0:256], in1=sz[:], op=ALU.mult), reads=[b_zd, szd], writes=[szd])
                            yield
                        op("dve", lambda e, tb=tb, hs=hs: e.tensor_tensor(out=xdte[:].rearrange("p (h d) -> p h d", d=64), in0=xB[:, 0:256].rearrange("p (h d) -> p h d", d=64),
                                                                  in1=s_dtdte[:, tb, hs].unsqueeze(2).to_broadcast([128, 4, 64]), op=ALU.mult), reads=[xBd, s_dtdted], writes=[xdted])
                        op("pe", lambda e: e.matmul(out=b_st[:, sto:sto + 256], lhsT=xB[:, 256:384], rhs=xdte[:], start=True, stop=True), reads=[xBd, xdted], writes=[b_std])
                        yield
                        if own:
                            op("pe", lambda e, cs=cs: e.matmul(out=b_cb[:, 256:384], lhsT=BTq[:, cs], rhs=CTq[:, cs], start=True, stop=True), reads=[BTqd, CTqd], writes=[b_cbd])
                            op("dve", lambda e: e.tensor_tensor(out=cbm[:], in0=b_cb[:, 256:384], in1=tri[:], op=ALU.mult), reads=[b_cbd, trid], writes=[cbmd])
                            op("dve", lambda e, tb=tb, hs=hs: e.tensor_tensor(out=xdt[:].rearrange("p (h d) -> p h d", d=64), in0=xB[:, 0:256].rearrange("p (h d) -> p h d", d=64),
                                                                      in1=s_dt[:, tb, hs].unsqueeze(2).to_broadcast([128, 4, 64]), op=ALU.mult), reads=[xBd, s_dtd], writes=[xdtd])
                            op("pe", lambda e, cs=cs: e.matmul(out=b_y[:, 256:512], lhsT=CTq[:, cs], rhs=Sb[:], start=True, stop=True), reads=[CTqd, Sbd], writes=[b_yd])
                            yield
                        op("dve", lambda e, tb=tb, hs=hs: e.tensor_tensor(out=S[:].rearrange("p (h d) -> p h d", d=64), in0=S[:].rearrange("p (h d) -> p h d", d=64),
                                                                  in1=s_cd[:, tb, hs].unsqueeze(2).to_broadcast([128, 4, 64]), op=ALU.mult), reads=[Sd, s_cdd], writes=[Sd])
                        op("dve", lambda e: e.tensor_tensor(out=S[:], in0=b_st[:, sto:sto + 256], in1=S[:], op=ALU.add), reads=[b_std, Sd], writes=[Sd])
                        if tb == NBLK - T // 128 - 1:
                            op("dve", lambda e: e.tensor_scalar(out=S[:], in0=S[:], scalar1=flg[:, 0:1], scalar2=None, op0=ALU.mult), reads=[Sd, flgd], writes=[Sd])
                        op("act", lambda e: e.copy(out=Sb[:], in_=S[:]), reads=[Sd], writes=[Sbd])
                        yield
                        if own:
                            for hp in range(2):
                                L_, Ld_ = Lh[hp % 2]
                                M_, Md_ = Mh[hp % 2]
                                op("pool", lambda e, L_=L_, M_=M_: e.tensor_tensor(out=M_[:], in0=cbm[:].unsqueeze(1).to_broadcast([128, 2, 128]), in1=L_[:], op=ALU.mult), reads=[cbmd, Ld_], writes=[Md_])
                                yield
                                for k2 in range(2):
                                    h = 2 * hp + k2
                                    op("pe", lambda e, M_=M_, h=h, k2=k2: e.matmul(out=b_y[:, h * 64:(h + 1) * 64], lhsT=M_[:, k2, :], rhs=xdt[:, h * 64:(h + 1) * 64], start=True, stop=True), reads=[Md_, xdtd], writes=[b_yd], inc=(h == 3))
                            yield
                            op("pool", lambda e, hs=hs: e.tensor_tensor(out=t2[:].rearrange("p (h d) -> p h d", d=64), in0=xB[:, 0:256].rearrange("p (h d) -> p h d", d=64),
                                                                 in1=hvb[:, 2, hs].unsqueeze(2).to_broadcast([128, 4, 64]), op=ALU.mult), reads=[xBd, hvbd], writes=[t2d])
                            op("dve", lambda e, tb=tb, hs=hs: e.tensor_tensor(out=t1[:].rearrange("p (h d) -> p h d", d=64), in0=b_y[:, 256:512].rearrange("p (h d) -> p h d", d=64),
                                                                      in1=s_eacs[:, tb, hs].unsqueeze(2).to_broadcast([128, 4, 64]), op=ALU.mult), reads=[b_yd, s_eacsd], writes=[t1d])
                            op("dve", lambda e: e.tensor_tensor(out=t1[:], in0=b_y[:, 0:256], in1=t1[:], op=ALU.add), reads=[b_yd, t1d], writes=[t1d])
                            yield
                            op("pool", lambda e: e.tensor_tensor(out=t1[:], in0=t1[:], in1=t2[:], op=ALU.add), reads=[t1d, t2d], writes=[t1d])
                            op("dve", lambda e: e.tensor_tensor(out=t1[:], in0=t1[:], in1=sz[:], op=ALU.mult), reads=[t1d, szd], writes=[t1d])
                            yield
                            op("act", lambda e: e.activation(out=yj[:], in_=t1[:], func=AF.Square, scale=1.0 / 16.0, accum_out=gst[:, 0:1]), reads=[t1d], writes=[yjd, gstd])
                            op("act", lambda e: e.activation(out=gst[:, 1:2], in_=gst[:, 0:1], func=AF.Ln, bias=EPS), reads=[gstd], writes=[gstd])
                            op("act", lambda e: e.activation(out=gst[:, 2:3], in_=gst[:, 1:2], func=AF.Exp, scale=-0.5), reads=[gstd], writes=[gstd])
                            yield
                            op("dve", lambda e: e.scalar_tensor_tensor(out=yo[:], in0=t1[:], scalar=gst[:, 2:3], in1=snwt[:, g * 256:(g + 1) * 256], op0=ALU.mult, op1=ALU.mult), reads=[t1d, gstd, snwtd], writes=[yod])
                            zv = b_z[:].bitcast(BF16)
                            for i in range(2):
                                op("pe", lambda e, i=i, zv=zv: e.transpose(out=zv[:, 512 + i * 128:512 + (i + 1) * 128], in_=yo[:, i * 128:(i + 1) * 128], identity=ident[:]), reads=[yod, identd], writes=[b_zd], inc=(i == 1))
                            to = (qd - (NQ - NQO)) * 512 + c * 128
                            op("act", lambda e, zv=zv, to=to: e.copy(out=yT[:, :, to:to + 128], in_=zv[:, 512:768].rearrange("p (i t) -> p i t", t=128)), reads=[b_zd], writes=[yTd])
                            yield

                for g0 in range(0, G, 2):
                    grp = (g0, g0 + 1)
                    for th, g in enumerate(grp):
                        prefetch_unit(g)
                        cw, cwd = TB[th]["cw"]
                        for i, r0 in enumerate((g * 256, g * 256 + 128, DSSM + g * 128, DSSM + G * 128 + g * 128)):
                            fw.dma("sp", cw[:, i, :], cwb[r0:r0 + 128, :], writes=[cwd])
                        S, Sd = TB[th]["S"]
                        Sb, Sbd = TB[th]["Sb"]
                        carry, carryd = TB[th]["carry"]
                        op("dve", lambda e, S=S: e.memset(S[:], 0.0), writes=[Sd])
                        op("dve", lambda e, Sb=Sb: e.memset(Sb[:], 0.0), writes=[Sbd])
                        op("dve", lambda e, carry=carry: e.memset(carry[:], 0.0), writes=[carryd])
                    uts = {0: load_u(0)}
                    run_rr([ssd_quad(th, g, 0, uts[0][0], uts[0][1], unit_w[g], "proj") for th, g in enumerate(grp)])
                    for qd in range(NQ):
                        cast_some(2 * cast_per_quad)
                        gens = [ssd_quad(th, g, qd, uts[qd][0], uts[qd][1], unit_w[g], "core") for th, g in enumerate(grp)]
                        if qd + 1 < NQ:
                            uts[qd + 1] = load_u(qd + 1)
                            gens += [ssd_quad(th, g, qd + 1, uts[qd + 1][0], uts[qd + 1][1], unit_w[g], "proj") for th, g in enumerate(grp)]
                        run_rr(gens)
                    for th, g in enumerate(grp):
                        yT, yTd = TB[th]["yT"]
                        fw.dma("sp", ymT_d[g * 256:(g + 1) * 256, :].rearrange("(i p) t -> p i t", p=128), yT[:], reads=[yTd], writes=[Dep("ym")], chan_dep=yTd)
                ym_ready = [(TB[i]["yT"][1].chan.key, 16 * TB[i]["yT"][1].chan.n) for i in range(2) if TB[i]["yT"][1].chan is not None]
                fw.barrier()
                if cfg.get("stop") == 1:
                    fw.emit()
                    return nc

            with ExitStack() as pa:
                KTs = [fw.sb("KT%d" % i, [128, TT], BF16, pa) for i in range(2)]
                VTs = [fw.sb("VT%d" % i, [128, TT], BF16, pa) for i in range(2)]
                QTs = [fw.sb("QT%d" % i, [128, T], BF16, pa) for i in range(2)]
                NVB = T // 128 + DIL[2]
                Vd_, Vdd_ = fw.sb("Vd", [128, NVB, 128], BF16, pa)
                aacc, aaccd = fw.sb("aacc", [128, 2, T], F32, pa)
                PT = [fw.sb("PT%d" % i, [128, 2, 128], BF16, pa) for i in range(2)]
                PM = [fw.sb("PM%d" % i, [128, 2, 128], BF16, pa) for i in range(3)]
                yA = [fw.sb("yA%d" % i, [128, T], BF16, pa) for i in range(2)]
                b_vt = [banks[2], banks[2]]
                b_s = [banks[3], banks[4], banks[5]]
                b_o = [banks[6], banks[7]]
                uc = [0]
                SKEW = 2

                def att_proj(hd):
                    prefetch_unit(G + hd)
                    prefetch_unit(G + hd + 1)
                    wq, wk, wv = unit_w[G + hd]
                    KT, KTd = KTs[hd % 2]
                    VT, VTd = VTs[hd % 2]
                    QT, QTd = QTs[hd % 2]
                    for qd in range(NQ):
                        own = qd >= NQ - NQO
                        ut, utd = load_u(qd)
                        cast_some(cast_per_quad)
                        ts_ = slice(qd * 512, (qd + 1) * 512)
                        pb, pbd = proj_fm(wk[0], wk[1], ut, utd)
                        op("act", lambda e, pb=pb, ts_=ts_: e.copy(out=KT[:, ts_], in_=pb[:, :]), reads=[pbd], writes=[KTd])
                        yield
                        pb, pbd = proj_fm(wv[0], wv[1], ut, utd)
                        op("dve", lambda e, pb=pb, ts_=ts_: e.tensor_copy(out=VT[:, ts_], in_=pb[:, :]), reads=[pbd], writes=[VTd])
                        yield
                        if own:
                            to = (qd - (NQ - NQO)) * 512
                            pb, pbd = proj_fm(wq[0], wq[1], ut, utd)
                            op("act", lambda e, pb=pb, to=to: e.activation(out=QT[:, to:to + 512], in_=pb[:, :], func=AF.Copy, scale=128.0 ** -0.5), reads=[pbd], writes=[QTd])
                            yield

                def att_units(hd):
                    KT, KTd = KTs[hd % 2]
                    VT, VTd = VTs[hd % 2]
                    QT, QTd = QTs[hd % 2]
                    first = True
                    for d in DIL:
                        nj = T // (128 * d)
                        nblk = d * (nj + 1)
                        for b0 in range(0, nblk, 4):
                            vb, vbd = b_vt[(b0 // 4) % 2]
                            vbv = vb[:].bitcast(BF16)
                            nb = min(4, nblk - b0)
                            for i in range(nb):
                                r, jj = divmod(b0 + i, nj + 1)
                                st_ = C + (jj - 1) * 128 * d + r
                                op("pe", lambda e, vbv=vbv, i=i, st_=st_, d=d: e.transpose(out=vbv[:, i * 128:(i + 1) * 128], in_=VT[:, st_:st_ + 127 * d + 1:d], identity=ident[:]), reads=[VTd, identd], writes=[vbd], inc=(i == nb - 1))
                            op("act", lambda e, vbv=vbv, b0=b0, nb=nb: e.copy(out=Vd_[:, b0:b0 + nb, :], in_=vbv[:, 0:nb * 128].rearrange("p (i t) -> p i t", t=128)), reads=[vbd], writes=[Vdd_])
                            yield
                        units = [(r, j) for r in range(d) for j in range(nj)]
                        info = {}

                        def stage_a(idx):
                            r, j = units[idx]
                            u_i = uc[0]
                            uc[0] += 1
                            bs, bsd = b_s[u_i % 3]
                            Pm_, Pmd_ = PM[u_i % 3]
                            q0 = j * 128 * d + r
                            qsl = slice(q0, q0 + 127 * d + 1, d)
                            kcur = slice(C + q0, C + q0 + 127 * d + 1, d)
                            kprev = slice(C + q0 - 128 * d, C + q0 - d + 1, d)
                            mk, mkd = (maskB, maskBd) if j == 0 else (maskA, maskAd)
                            op("pe", lambda e: e.matmul(out=bs[:, 0:256], lhsT=ident[:], rhs=mk[:].rearrange("p a b -> p (a b)"), start=True, stop=False), reads=[identd, mkd], writes=[bsd], inc=False)
                            op("pe", lambda e: e.matmul(out=bs[:, 0:128], lhsT=KT[:, kprev], rhs=QT[:, qsl], start=False, stop=False), reads=[KTd, QTd], writes=[bsd], inc=False)
                            op("pe", lambda e: e.matmul(out=bs[:, 128:256], lhsT=KT[:, kcur], rhs=QT[:, qsl], start=False, stop=True), reads=[KTd, QTd], writes=[bsd])
                            op("act", lambda e: e.activation(out=Pm_[:].rearrange("p a b -> p (a b)"), in_=bs[:, 0:256], func=AF.Exp), reads=[bsd], writes=[Pmd_])
                            info[idx] = (u_i, Pm_, Pmd_, qsl)

                        def stage_b(idx, first):
                            r, j = units[idx]
                            u_i, Pm_, Pmd_, qsl = info.pop(idx)
                            bo, bod = b_o[u_i % 2]
                            vi_prev = r * (nj + 1) + j
                            vi_cur = vi_prev + 1
                            op("pe", lambda e: e.matmul(out=bo[:, 0:128], lhsT=Vd_[:, vi_prev, :], rhs=Pm_[:, 0, :], start=True, stop=False), reads=[Vdd_, Pmd_], writes=[bod], inc=False)
                            op("pe", lambda e: e.matmul(out=bo[:, 0:128], lhsT=Vd_[:, vi_cur, :], rhs=Pm_[:, 1, :], start=False, stop=True), reads=[Vdd_, Pmd_], writes=[bod], inc=False)
                            op("pe", lambda e: e.matmul(out=bo[:, 128:256], lhsT=onesb[:], rhs=Pm_[:, 0, :], start=True, stop=False), reads=[onesbd, Pmd_], writes=[bod], inc=False)
                            op("pe", lambda e: e.matmul(out=bo[:, 128:256], lhsT=onesb[:], rhs=Pm_[:, 1, :], start=False, stop=True), reads=[onesbd, Pmd_], writes=[bod])
                            src = bo[:, 0:256].rearrange("p (a t) -> p a t", t=128)
                            if first:
                                op("dve", lambda e: e.tensor_copy(out=aacc[:, :, qsl], in_=src), reads=[bod], writes=[aaccd])
                            else:
                                op("dve", lambda e: e.tensor_tensor(out=aacc[:, :, qsl], in0=src, in1=aacc[:, :, qsl], op=ALU.add), reads=[bod, aaccd], writes=[aaccd])

                        n_u = len(units)
                        for idx in range(n_u + SKEW):
                            if idx < n_u:
                                stage_a(idx)
                            if idx >= SKEW:
                                stage_b(idx - SKEW, first)
                            yield
                        first = False
                    ya, yad = yA[hd % 2]
                    op("dve", lambda e: e.reciprocal(out=aacc[:, 1, :], in_=aacc[:, 1, :]), reads=[aaccd], writes=[aaccd])
                    op("dve", lambda e: e.tensor_tensor(out=ya[:], in0=aacc[:, 0, :], in1=aacc[:, 1, :], op=ALU.mult), reads=[aaccd], writes=[yad])
                    fw.dma("sp", ymT_d[DSSM + hd * 128:DSSM + (hd + 1) * 128, :], ya[:], reads=[yad], writes=[Dep("ym")], chan_dep=yad)
                    yield

                for _ in att_proj(0):
                    pass
                for hd in range(H):
                    gu = att_units(hd)
                    gp = att_proj(hd + 1) if hd + 1 < H else None
                    alive_u = True
                    while alive_u or gp is not None:
                        for _ in range(3):
                            if alive_u:
                                try:
                                    next(gu)
                                except StopIteration:
                                    alive_u = False
                        if gp is not None:
                            try:
                                next(gp)
                            except StopIteration:
                                gp = None
                ym_ready += [(yA[i][1].chan.key, 16 * yA[i][1].chan.n) for i in range(2) if yA[i][1].chan is not None]
                fw.barrier()
        cast_some(len(cast_list))
        wcast_ready = [(d_.chan.key, 16 * d_.chan.n) for d_ in wcastds if d_.chan is not None]
        p01.close()

        with ExitStack() as ph:
            nwpost, nwpostd = fw.sb("nwA", [128, D], F32, ph)
            nwpre, nwpred = fw.sb("nwB", [128, D], F32, ph)
            nwfin, nwfind = nwpost, nwpostd
            fw.dma("sp", nwpre[:], nw4[2, :].partition_broadcast(128), writes=[nwpred])
            NWP = 4
            PR = 8
            wp = [fw.sb("wp%d" % i, [128, PR, 512], BF16, ph) for i in range(NWP)]
            wpc = [0]
            big, bigd = fw.sb("big", [128, max(MKC, FC), 512], BF16, ph)
            xh, xhd = fw.sb("xh", [128, 4, D], F32, ph)
            mf, mfd = fw.sb("mf", [128, 4, D], F32, ph)
            hn, hnd = fw.sb("hn", [128, D], BF16, ph)
            hnT, hnTd = fw.sb("hnT", [128, KC, 512], BF16, ph)
            rls = [fw.sb("rl%d" % i, [128, 512], F32, ph) for i in range(2)]
            st2, st2d = fw.sb("st2", [128, 4, 12], F32, ph)

            def load_wp(src, r0, nrow_chunks, c0, ncol):
                wt, wd = wp[wpc[0] % NWP]
                wpc[0] += 1
                fw.dma("sp", wt[:, 0:nrow_chunks, 0:ncol], src[r0:r0 + nrow_chunks * 128, c0:c0 + ncol].rearrange("(kc p) n -> p kc n", p=128), writes=[wd], extra=wcast_ready)
                return wt, wd

            NCB = D // 512 if D >= 512 else 1
            CBW = min(512, D)
            for qd in range(NQO):
                tq = slice(qd * 512, (qd + 1) * 512)
                for k0 in range(0, MKC, 16):
                    k1 = min(MKC, k0 + 16)
                    fw.dma("sp", big[:, k0:k1, :], ymT_d[k0 * 128:k1 * 128, tq].rearrange("(kc p) t -> p kc t", p=128), writes=[bigd], extra=ym_ready)
                fw.dma("sp", xh[:], xc[C + qd * 512:C + (qd + 1) * 512, :].rearrange("(a p) d -> p a d", p=128), writes=[xhd])
                for cb in range(NCB):
                    npiece = (MKC + PR - 1) // PR
                    for pi in range(npiece):
                        nr = min(PR, MKC - pi * PR)
                        wt, wd = load_wp(wout_b, pi * PR * 128, nr, cb * CBW, CBW)
                        for tb in range(4):
                            pb, pbd = banks[(cb % 2) * 4 + tb]
                            for k in range(nr):
                                kc = pi * PR + k
                                op("pe", lambda e, pb=pb, wt=wt, k=k, kc=kc, tb=tb: e.matmul(out=pb[:, 0:CBW], lhsT=big[:, kc, tb * 128:(tb + 1) * 128], rhs=wt[:, k, 0:CBW], start=(kc == 0), stop=(kc == MKC - 1)),
                                   reads=[bigd, wd], writes=[pbd], inc=(k == nr - 1))
                    for tb in range(4):
                        pb, pbd = banks[(cb % 2) * 4 + tb]
                        if tb % 2 == 0:
                            op("act", lambda e, pb=pb, tb=tb, cb=cb: e.copy(out=mf[:, tb, cb * CBW:(cb + 1) * CBW], in_=pb[:, 0:CBW]), reads=[pbd], writes=[mfd])
                        else:
                            op("dve", lambda e, pb=pb, tb=tb, cb=cb: e.tensor_copy(out=mf[:, tb, cb * CBW:(cb + 1) * CBW], in_=pb[:, 0:CBW]), reads=[pbd], writes=[mfd])
                fw.dma("sp", nwpost[:], nw4[1, :].partition_broadcast(128), writes=[nwpostd])
                junk2, junk2d = hn, hnd
                for tb in range(4):
                    op("act", lambda e, tb=tb: e.activation(out=junk2[:], in_=mf[:, tb, :], func=AF.Square, scale=float(D) ** -0.5, accum_out=st2[:, tb, 0:1]), reads=[mfd], writes=[junk2d, st2d])
                    op("act", lambda e, tb=tb: e.activation(out=st2[:, tb, 1:2], in_=st2[:, tb, 0:1], func=AF.Ln, bias=EPS), reads=[st2d], writes=[st2d])
                    op("act", lambda e, tb=tb: e.activation(out=st2[:, tb, 2:3], in_=st2[:, tb, 1:2], func=AF.Exp, scale=-0.5), reads=[st2d], writes=[st2d])
                    op("dve", lambda e, tb=tb: e.scalar_tensor_tensor(out=mf[:, tb, :], in0=mf[:, tb, :], scalar=st2[:, tb, 2:3], in1=nwpost[:], op0=ALU.mult, op1=ALU.mult), reads=[mfd, st2d, nwpostd], writes=[mfd])
                    op("pool", lambda e, tb=tb: e.tensor_tensor(out=xh[:, tb, :], in0=xh[:, tb, :], in1=mf[:, tb, :], op=ALU.add), reads=[xhd, mfd], writes=[xhd])
                    op("act", lambda e, tb=tb: e.activation(out=junk2[:], in_=xh[:, tb, :], func=AF.Square, scale=float(D) ** -0.5, accum_out=st2[:, tb, 3:4]), reads=[xhd], writes=[junk2d, st2d])
                    op("act", lambda e, tb=tb: e.activation(out=st2[:, tb, 4:5], in_=st2[:, tb, 3:4], func=AF.Ln, bias=EPS), reads=[st2d], writes=[st2d])
                    op("act", lambda e, tb=tb: e.activation(out=st2[:, tb, 5:6], in_=st2[:, tb, 4:5], func=AF.Exp, scale=-0.5), reads=[st2d], writes=[st2d])
                    op("dve", lambda e, tb=tb: e.scalar_tensor_tensor(out=hn[:], in0=xh[:, tb, :], scalar=st2[:, tb, 5:6], in1=nwpre[:], op0=ALU.mult, op1=ALU.mult), reads=[xhd, st2d, nwpred], writes=[hnd])
                    nper = 8 if KC >= 8 else KC
                    ngrp = (KC + nper - 1) // nper
                    for gi in range(ngrp):
                        pb, pbd = banks[(tb * ngrp + gi) % 8]
                        pbv = pb[:].bitcast(BF16)
                        n_in = min(nper, KC - gi * nper)
                        for j in range(n_in):
                            kc = gi * nper + j
                            op("pe", lambda e, pbv=pbv, kc=kc, j=j: e.transpose(out=pbv[:, j * 128:(j + 1) * 128], in_=hn[:, kc * 128:(kc + 1) * 128], identity=ident[:]), reads=[hnd, identd], writes=[pbd], inc=(j == n_in - 1))
                        op("act", lambda e, pbv=pbv, gi=gi, n_in=n_in, tb=tb: e.copy(out=hnT[:, gi * nper:gi * nper + n_in, tb * 128:(tb + 1) * 128], in_=pbv[:, 0:n_in * 128].rearrange("p (j t) -> p j t", t=128)), reads=[pbd], writes=[hnTd])
                nfc_per = 4
                for f0 in range(0, FC, nfc_per):
                    nf = min(nfc_per, FC - f0)
                    halves = []
                    for k0 in range(0, KC, PR):
                        nr = min(PR, KC - k0)
                        halves.append((load_wp(wup_b, k0 * 128, nr, f0 * 128, nf * 128), k0, nr))
                    for fi in range(nf):
                        fc = f0 + fi
                        pb, pbd = banks[fc % 8]
                        for (wt, wd), k0, nr in halves:
                            for k in range(nr):
                                kc = k0 + k
                                op("pe", lambda e, pb=pb, wt=wt, fi=fi, k=k, kc=kc: e.matmul(out=pb[:, :], lhsT=wt[:, k, fi * 128:(fi + 1) * 128], rhs=hnT[:, kc, :], start=(kc == 0), stop=(kc == KC - 1)),
                                   reads=[wd, hnTd], writes=[pbd], inc=(kc == KC - 1))
                        rl, rld = rls[fc % 2]
                        op("act", lambda e, pb=pb, rl=rl: e.activation(out=rl[:], in_=pb[:, :], func=AF.Relu), reads=[pbd], writes=[rld])
                        op("dve", lambda e, fc=fc, rl=rl: e.tensor_tensor(out=big[:, fc, :], in0=rl[:], in1=rl[:], op=ALU.mult), reads=[rld], writes=[bigd])
                for cb in range(NCB):
                    npiece = (FC + PR - 1) // PR
                    for pi in range(npiece):
                        nr = min(PR, FC - pi * PR)
                        wt, wd = load_wp(wdown_b, pi * PR * 128, nr, cb * CBW, CBW)
                        for tb in range(4):
                            pb, pbd = banks[(cb % 2) * 4 + tb]
                            for k in range(nr):
                                fc = pi * PR + k
                                op("pe", lambda e, pb=pb, wt=wt, k=k, fc=fc, tb=tb: e.matmul(out=pb[:, 0:CBW], lhsT=big[:, fc, tb * 128:(tb + 1) * 128], rhs=wt[:, k, 0:CBW], start=(fc == 0), stop=(fc == FC - 1)),
                                   reads=[bigd, wd], writes=[pbd], inc=(k == nr - 1))
                    for tb in range(4):
                        pb, pbd = banks[(cb % 2) * 4 + tb]
                        if tb % 2 == 0:
                            op("act", lambda e, pb=pb, tb=tb, cb=cb: e.copy(out=mf[:, tb, cb * CBW:(cb + 1) * CBW], in_=pb[:, 0:CBW]), reads=[pbd], writes=[mfd])
                        else:
                            op("dve", lambda e, pb=pb, tb=tb, cb=cb: e.tensor_copy(out=mf[:, tb, cb * CBW:(cb + 1) * CBW], in_=pb[:, 0:CBW]), reads=[pbd], writes=[mfd])
                fw.dma("sp", nwfin[:], nw4[3, :].partition_broadcast(128), writes=[nwfind])
                for tb in range(4):
                    op("act", lambda e, tb=tb: e.activation(out=junk2[:], in_=mf[:, tb, :], func=AF.Square, scale=float(D) ** -0.5, accum_out=st2[:, tb, 6:7]), reads=[mfd], writes=[junk2d, st2d])
                    op("act", lambda e, tb=tb: e.activation(out=st2[:, tb, 7:8], in_=st2[:, tb, 6:7], func=AF.Ln, bias=EPS), reads=[st2d], writes=[st2d])
                    op("act", lambda e, tb=tb: e.activation(out=st2[:, tb, 8:9], in_=st2[:, tb, 7:8], func=AF.Exp, scale=-0.5), reads=[st2d], writes=[st2d])
                    op("dve", lambda e, tb=tb: e.scalar_tensor_tensor(out=mf[:, tb, :], in0=mf[:, tb, :], scalar=st2[:, tb, 8:9], in1=nwfin[:], op0=ALU.mult, op1=ALU.mult), reads=[mfd, st2d, nwfind], writes=[mfd])
                    op("pool", lambda e, tb=tb: e.tensor_tensor(out=mf[:, tb, :], in0=xh[:, tb, :], in1=mf[:, tb, :], op=ALU.add), reads=[xhd, mfd], writes=[mfd])
                fw.dma("sp", out[tq, :].rearrange("(a p) d -> p a d", p=128), mf[:], reads=[mfd], writes=[Dep("out")], chan_dep=mfd)
            fw.barrier()
        fw.emit()
    return nc


_CACHE = {}


def make_in_maps(cfg, ncores, inputs):
    D, T, G = cfg["D"], cfg["T"], cfg["G"]
    x = np.asarray(inputs["x"], np.float32)
    B_, S_, _ = x.shape
    halves = S_ // T
    assert halves == 2 and B_ * halves == ncores
    shared = {
        "w_in": np.ascontiguousarray(np.asarray(inputs["w_in"], np.float32)[0]),
        "cwb": np.ascontiguousarray(np.concatenate([np.asarray(inputs["conv_w"], np.float32)[0].T, np.asarray(inputs["conv_b"], np.float32)[0][:, None]], axis=1)),
        "hv": np.ascontiguousarray(np.stack([np.asarray(inputs["dt_bias"], np.float32)[0], np.asarray(inputs["a_log"], np.float32)[0], np.asarray(inputs["d_skip"], np.float32)[0]], axis=0)),
        "snw": np.ascontiguousarray(np.asarray(inputs["ssm_norm_w"], np.float32)),
        "nw4": np.ascontiguousarray(np.stack([np.asarray(inputs[k], np.float32)[0] for k in ("norm_mix_pre", "norm_mix_post", "norm_mlp_pre", "norm_mlp_post")], axis=0)),
        "w_out": np.ascontiguousarray(np.asarray(inputs["w_out"], np.float32)[0]),
        "w_up": np.ascontiguousarray(np.asarray(inputs["w_up"], np.float32)[0]),
        "w_down": np.ascontiguousarray(np.asarray(inputs["w_down"], np.float32)[0]),
    }
    maps = []
    for c in range(ncores):
        b, h = divmod(c, 2)
        if h == 0:
            xcc = np.concatenate([np.zeros((T, D), np.float32), x[b, :T]], axis=0)
            fl = np.zeros((128, 1), np.float32)
        else:
            xcc = x[b]
            fl = np.ones((128, 1), np.float32)
        m = dict(shared)
        m["xc"] = np.ascontiguousarray(xcc)
        m["flag"] = fl
        maps.append(m)
    return maps


def kernel(**inputs):
    cfg = FULL_CFG
    if "nc" not in _CACHE:
        _CACHE["nc"] = build_program(cfg)
    nc = _CACHE["nc"]
    maps = make_in_maps(cfg, 8, inputs)
    res = run_bass_kernel_spmd(nc, maps, core_ids=list(range(8)))
    x = inputs["x"]
    B_, S_, D = x.shape
    T = cfg["T"]
    outp = np.empty((B_, S_, D), np.float32)
    for c in range(8):
        b, h = divmod(c, 2)
        outp[b, h * T:(h + 1) * T] = res.results[c]["out"]
    return outp
```

```python
from contextlib import ExitStack
import numpy as np
import concourse.bass as bass
import concourse.mybir as mybir
from concourse.bass_utils import run_bass_kernel_spmd

F32 = mybir.dt.float32
BF16 = mybir.dt.bfloat16
AF = mybir.ActivationFunctionType
ALU = mybir.AluOpType

ENGS = ("pe", "act", "dve", "pool", "sp")
EPS = 1e-6


class Dep:
    __slots__ = ("w", "r", "chan", "name")

    def __init__(self, name=""):
        self.w = None
        self.r = []
        self.chan = None
        self.name = name


class Chan:
    __slots__ = ("key", "sem", "n")


class FW:
    def __init__(self, nc, stack, same_engine_sync=True):
        self.nc = nc
        self.stack = stack
        self.streams = {e: [] for e in ENGS}
        self.cnt = {e: 0 for e in ENGS}
        self.seen = {e: {} for e in ENGS}
        self.sems = {}
        self.latest = {}
        self.nchan = 0
        self.same = same_engine_sync
        for e in ENGS:
            self.sems[e] = stack.enter_context(nc.semaphore("s_" + e))
            self.latest[e] = 0

    def sb(self, name, shape, dtype, stack=None):
        t = (stack or self.stack).enter_context(self.nc.sbuf_tensor(name, list(shape), dtype))
        return t, Dep(name)

    def ps(self, name, shape, dtype=F32, stack=None):
        t = (stack or self.stack).enter_context(self.nc.psum_tensor(name, list(shape), dtype))
        return t, Dep(name)

    def chan_of(self, dep):
        if dep.chan is None:
            c = Chan()
            c.key = "c%d" % self.nchan
            self.nchan += 1
            c.sem = self.stack.enter_context(self.nc.semaphore("d_" + c.key))
            c.n = 0
            self.sems[c.key] = c.sem
            self.latest[c.key] = 0
            dep.chan = c
        return dep.chan

    def _waits(self, eng, reads, writes, extra, group_key=None):
        need = {}

        def add(t):
            if t is None:
                return
            k, v = t
            if need.get(k, 0) < v:
                need[k] = v

        for d in reads:
            add(d.w)
        for d in writes:
            if not (group_key is not None and d.w is not None and d.w[0] == group_key):
                add(d.w)
            for t in d.r:
                add(t)
        for t in extra:
            add(t)
        st = self.streams[eng]
        for k, v in need.items():
            if k == eng and (eng == "pe" or not self.same):
                continue
            if self.seen[eng].get(k, 0) >= v:
                continue
            self.seen[eng][k] = v
            st.append(("wait", k, v))

    def op(self, eng, fn, reads=(), writes=(), extra=(), inc=True):
        self._waits(eng, reads, writes, extra)
        if inc:
            self.cnt[eng] += 1
            self.latest[eng] = self.cnt[eng]
        ticket = (eng, self.cnt[eng] if inc else self.cnt[eng] + 1)
        self.streams[eng].append(("op", fn, eng if inc else None))
        for d in reads:
            d.r.append(ticket)
            if len(d.r) > 24:
                d.r = _compact(d.r)
        for d in writes:
            d.w = ticket
            d.r = []
        return ticket

    def dma(self, q, out_ap, in_ap, reads=(), writes=(), extra=(), chan_dep=None, group=True, **kw):
        cd = chan_dep if chan_dep is not None else writes[0]
        ch = self.chan_of(cd)
        self._waits(q, reads, writes, extra, group_key=ch.key if group else None)
        ch.n += 1
        ticket = (ch.key, 16 * ch.n)
        self.latest[ch.key] = 16 * ch.n

        def fn(e, out_ap=out_ap, in_ap=in_ap, kw=kw):
            return e.dma_start(out=out_ap, in_=in_ap, **kw)

        self.streams[q].append(("dma", fn, ch.key))
        for d in reads:
            d.r.append(ticket)
        for d in writes:
            d.w = ticket
            d.r = []
        return ticket

    def barrier(self, engs=ENGS):
        for e in engs:
            st = self.streams[e]
            for k, v in self.latest.items():
                if v == 0 or self.seen[e].get(k, 0) >= v:
                    continue
                if k == e:
                    continue
                self.seen[e][k] = v
                st.append(("wait", k, v))

    def emit(self):
        nc = self.nc
        sems = self.sems
        streams = self.streams
        with nc.Block() as block:

            def run(engname, e):
                for item in streams[engname]:
                    if item[0] == "wait":
                        e.wait_ge(sems[item[1]], item[2])
                    elif item[0] == "op":
                        ins = item[1](e)
                        if item[2] is not None:
                            ins.then_inc(sems[item[2]], 1)
                    else:
                        ins = item[1](e)
                        ins.then_inc(sems[item[2]], 16)

            @block.tensor
            def _(e):
                run("pe", e)

            @block.scalar
            def _(e):
                run("act", e)

            @block.vector
            def _(e):
                run("dve", e)

            @block.gpsimd
            def _(e):
                run("pool", e)

            @block.sync
            def _(e):
                run("sp", e)


def _compact(tickets):
    best = {}
    for k, v in tickets:
        if best.get(k, 0) < v:
            best[k] = v
    return list(best.items())


FULL_CFG = dict(D=2048, T=2048, G=8, H=16, FF=8192, DIL=(1, 4, 16))


def build_program(cfg):
    D, T, G, H, FF, DIL = cfg["D"], cfg["T"], cfg["G"], cfg["H"], cfg["FF"], cfg["DIL"]
    KC = D // 128
    C = T
    TT = C + T
    NBLK = TT // 128
    NQ = TT // 512
    NQO = T // 512
    NH = 4 * G
    DSSM = G * 256
    DATT = H * 128
    DMIX = DSSM + DATT
    MKC = DMIX // 128
    FC = FF // 128
    CH = DSSM + 2 * G * 128
    OFF_Z, OFF_X, OFF_B = 0, DSSM, 2 * DSSM
    OFF_C = OFF_B + G * 128
    OFF_DT = OFF_C + G * 128
    OFF_Q = OFF_DT + NH
    OFF_K = OFF_Q + DATT
    OFF_V = OFF_K + DATT
    NIN = OFF_V + DATT
    assert 128 * DIL[2] == T and T % 512 == 0

    nc = bass.Bass("TRN2", target_bir_lowering=False)

    def din(name, shape, dt=F32):
        return nc.dram_tensor(name, list(shape), dt, kind="ExternalInput").ap()

    xc = din("xc", [TT, D])
    flag = din("flag", [128, 1])
    w_in = din("w_in", [D, NIN])
    cwb = din("cwb", [CH, 5])
    hv = din("hv", [3, NH])
    snw = din("snw", [1, DSSM])
    nw4 = din("nw4", [4, D])
    w_out = din("w_out", [DMIX, D])
    w_up = din("w_up", [D, FF])
    w_down = din("w_down", [FF, D])
    out = nc.dram_tensor("out", [T, D], F32, kind="ExternalOutput").ap()
    uT_d = nc.dram_tensor("uT_d", [D, TT], BF16, kind="Internal").ap()
    ymT_d = nc.dram_tensor("ymT_d", [DMIX, T], BF16, kind="Internal").ap()
    wout_b = nc.dram_tensor("wout_b", [DMIX, D], BF16, kind="Internal").ap()
    wup_b = nc.dram_tensor("wup_b", [D, FF], BF16, kind="Internal").ap()
    wdown_b = nc.dram_tensor("wdown_b", [FF, D], BF16, kind="Internal").ap()

    with ExitStack() as top:
        fw = FW(nc, top)
        op = fw.op
        banks = [fw.ps("bank%d" % i, [128, 512], F32) for i in range(8)]

        ident, identd = fw.sb("ident", [128, 128], BF16)
        tri, trid = fw.sb("tri", [128, 128], F32)
        onesf, onesfd = fw.sb("onesf", [128, 128], F32)
        onesb, onesbd = fw.sb("onesb", [128, 128], BF16)
        maskA, maskAd = fw.sb("maskA", [128, 2, 128], BF16)
        maskB, maskBd = fw.sb("maskB", [128, 2, 128], BF16)
        flg, flgd = fw.sb("flg", [128, 1], F32)
        sgt, sgtd = fw.sb("sgt", [128, 128], F32)
        tmpc, tmpcd = fw.sb("tmpc", [128, 128], F32)
        hvb, hvbd = fw.sb("hvb", [128, 3, NH], F32)
        p01 = top.enter_context(ExitStack())
        s_dt, s_dtd = fw.sb("s_dt", [128, NBLK, NH], F32, p01)
        s_dta, s_dtad = fw.sb("s_dta", [128, NBLK, NH], F32, p01)
        s_eacs, s_eacsd = fw.sb("s_eacs", [128, NBLK, NH], F32, p01)
        s_nacs, s_nacsd = fw.sb("s_nacs", [128, NBLK, NH], F32, p01)
        s_dtdte, s_dtdted = fw.sb("s_dtdte", [128, NBLK, NH], F32, p01)
        s_cd, s_cdd = fw.sb("s_cd", [128, NBLK, NH], F32, p01)

        fw.dma("sp", flg[:], flag[:, :], writes=[flgd])
        fw.dma("sp", hvb[:], hv.partition_broadcast(128), writes=[hvbd])
        op("pool", lambda e: e.memset(onesf[:], 1.0), writes=[onesfd])
        op("pool", lambda e: e.memset(onesb[:], 1.0), writes=[onesbd])
        op("pool", lambda e: e.affine_select(out=tri[:], in_=onesf[:], pattern=[[1, 128]], compare_op=ALU.is_ge, fill=0.0, base=0, channel_multiplier=-1), reads=[onesfd], writes=[trid])
        op("pool", lambda e: e.affine_select(out=sgt[:], in_=onesf[:], pattern=[[-1, 128]], compare_op=ALU.is_ge, fill=0.0, base=-1, channel_multiplier=1), reads=[onesfd], writes=[sgtd])
        op("pool", lambda e: e.affine_select(out=tmpc[:], in_=onesf[:], pattern=[[-1, 128]], compare_op=ALU.is_equal, fill=0.0, base=0, channel_multiplier=1), reads=[onesfd], writes=[tmpcd])
        op("dve", lambda e: e.tensor_copy(out=ident[:], in_=tmpc[:]), reads=[tmpcd], writes=[identd])
        op("pool", lambda e: e.affine_select(out=tmpc[:], in_=onesf[:], pattern=[[-1, 128]], compare_op=ALU.is_ge, fill=0.0, base=0, channel_multiplier=1), reads=[onesfd, identd], writes=[tmpcd])
        NEGB = 30000.0
        op("dve", lambda e: e.tensor_scalar(out=maskA[:, 0, :], in0=tmpc[:], scalar1=-1.0, scalar2=NEGB, op0=ALU.add, op1=ALU.mult), reads=[tmpcd], writes=[maskAd])
        op("dve", lambda e: e.tensor_scalar(out=tmpc[:], in0=tmpc[:], scalar1=flg[:, 0:1], scalar2=None, op0=ALU.mult), reads=[tmpcd, flgd], writes=[tmpcd])
        op("dve", lambda e: e.tensor_scalar(out=maskB[:, 0, :], in0=tmpc[:], scalar1=-1.0, scalar2=NEGB, op0=ALU.add, op1=ALU.mult), reads=[tmpcd], writes=[maskBd])
        op("dve", lambda e: e.tensor_scalar(out=maskA[:, 1, :], in0=tri[:], scalar1=-1.0, scalar2=NEGB, op0=ALU.add, op1=ALU.mult), reads=[trid], writes=[maskAd])
        op("dve", lambda e: e.tensor_scalar(out=maskB[:, 1, :], in0=tri[:], scalar1=-1.0, scalar2=NEGB, op0=ALU.add, op1=ALU.mult), reads=[trid], writes=[maskBd])
        op("act", lambda e: e.activation(out=hvb[:, 1, :], in_=hvb[:, 1, :], func=AF.Exp), reads=[hvbd], writes=[hvbd])
        op("dve", lambda e: e.tensor_scalar(out=hvb[:, 1, :], in0=hvb[:, 1, :], scalar1=-1.0, scalar2=None, op0=ALU.mult), reads=[hvbd], writes=[hvbd])

        NCAST = 4
        wcastds = [Dep("wcast%d" % i) for i in range(NCAST)]
        cast_list = []
        for (src, dst, rows, cols) in ((w_out, wout_b, DMIX, D), (w_up, wup_b, D, FF), (w_down, wdown_b, FF, D)):
            for r0 in range(0, rows, 128):
                for c0 in range(0, cols, 2048):
                    c1 = min(cols, c0 + 2048)
                    cast_list.append((dst[r0:r0 + 128, c0:c1], src[r0:r0 + 128, c0:c1]))
        cast_pos = [0]

        def cast_some(n):
            for _ in range(n):
                i = cast_pos[0]
                if i >= len(cast_list):
                    return
                cast_pos[0] += 1
                fw.dma("pool", cast_list[i][0], cast_list[i][1], writes=[wcastds[i % NCAST]], group=False)

        with ExitStack() as ph:
            nwt, nwtd = fw.sb("nwt", [128, D], F32, ph)
            wdt, wdtd = fw.sb("wdt", [128, KC, 128], BF16, ph)
            xr = [fw.sb("xr%d" % i, [128, D], F32, ph) for i in range(2)]
            ur = [fw.sb("ur%d" % i, [128, D], BF16, ph) for i in range(2)]
            junk, junkd = fw.sb("junk", [128, D], BF16, ph)
            uq = [fw.sb("uq%d" % i, [128, KC, 512], BF16, ph) for i in range(2)]
            st0, st0d = fw.sb("st0", [128, NBLK, 4], F32, ph)
            dtt = [fw.sb("dtt%d" % i, [128, 2, NH], F32, ph) for i in range(2)]
            fw.dma("sp", nwt[:], nw4[0, :].partition_broadcast(128), writes=[nwtd])
            fw.dma("pool", wdt[:], w_in[:, OFF_DT + NH - 128:OFF_DT + NH].rearrange("(kc p) n -> p kc n", p=128), writes=[wdtd])
            ptb = [banks[0], banks[1], banks[2], banks[3]]
            nper = 8 if KC >= 8 else KC
            for tb in range(NBLK):
                xt, xtd = xr[tb % 2]
                u, ud = ur[tb % 2]
                uqt, uqd = uq[(tb // 4) % 2]
                c4 = tb % 4
                fw.dma("sp", xt[:], xc[tb * 128:(tb + 1) * 128, :], writes=[xtd])
                op("act", lambda e, xt=xt, tb=tb: e.activation(out=junk[:], in_=xt[:], func=AF.Square, scale=float(D) ** -0.5, accum_out=st0[:, tb, 0:1]), reads=[xtd], writes=[junkd, st0d])
                op("act", lambda e, tb=tb: e.activation(out=st0[:, tb, 1:2], in_=st0[:, tb, 0:1], func=AF.Ln, bias=EPS), reads=[st0d], writes=[st0d])
                op("act", lambda e, tb=tb: e.activation(out=st0[:, tb, 2:3], in_=st0[:, tb, 1:2], func=AF.Exp, scale=-0.5), reads=[st0d], writes=[st0d])
                op("dve", lambda e, xt=xt, u=u, tb=tb: e.scalar_tensor_tensor(out=u[:], in0=xt[:], scalar=st0[:, tb, 2:3], in1=nwt[:], op0=ALU.mult, op1=ALU.mult), reads=[xtd, st0d, nwtd], writes=[ud])
                ngrp = (KC + nper - 1) // nper
                for gi in range(ngrp):
                    pb, pbd = ptb[(tb * ngrp + gi) % 4]
                    pbv = pb[:].bitcast(BF16)
                    n_in = min(nper, KC - gi * nper)
                    for j in range(n_in):
                        kc = gi * nper + j
                        op("pe", lambda e, pbv=pbv, u=u, kc=kc, j=j: e.transpose(out=pbv[:, j * 128:(j + 1) * 128], in_=u[:, kc * 128:(kc + 1) * 128], identity=ident[:]),
                           reads=[ud, identd], writes=[pbd], inc=(j == n_in - 1))
                    eng = "act" if gi % 2 == 0 else "dve"
                    src = pbv[:, 0:n_in * 128].rearrange("p (j t) -> p j t", t=128)
                    dst = uqt[:, gi * nper:gi * nper + n_in, c4 * 128:(c4 + 1) * 128]
                    if eng == "act":
                        op("act", lambda e, src=src, dst=dst: e.copy(out=dst, in_=src), reads=[pbd], writes=[uqd])
                    else:
                        op("dve", lambda e, src=src, dst=dst: e.tensor_copy(out=dst, in_=src), reads=[pbd], writes=[uqd])
                pd, pdd = banks[4 + tb % 2]
                dbg = cfg.get("dbg", 99)
                if dbg == 1:
                    if c4 == 3:
                        qd = tb // 4
                        fw.dma("sp", uT_d[:, qd * 512:(qd + 1) * 512].rearrange("(kc p) t -> p kc t", p=128), uqt[:], reads=[uqd], writes=[Dep("uTd")], chan_dep=uqd)
                    continue
                for kc in range(KC):
                    op("pe", lambda e, pd=pd, uqt=uqt, kc=kc, c4=c4: e.matmul(out=pd[:, 0:128], lhsT=uqt[:, kc, c4 * 128:(c4 + 1) * 128], rhs=wdt[:, kc, :], start=(kc == 0), stop=(kc == KC - 1)),
                       reads=[uqd, wdtd], writes=[pdd], inc=(kc == KC - 1))
                dt_, dtd_ = dtt[tb % 2]
                op("dve", lambda e, pd=pd, dt_=dt_: e.tensor_tensor(out=dt_[:, 0, :], in0=pd[:, 128 - NH:128], in1=hvb[:, 0, :], op=ALU.add), reads=[pdd, hvbd], writes=[dtd_])
                op("act", lambda e, dt_=dt_: e.activation(out=dt_[:, 0, :], in_=dt_[:, 0, :], func=AF.Exp), reads=[dtd_], writes=[dtd_])
                op("act", lambda e, dt_=dt_, tb=tb: e.activation(out=s_dt[:, tb, :], in_=dt_[:, 0, :], func=AF.Ln, bias=1.0), reads=[dtd_], writes=[s_dtd])
                op("dve", lambda e, tb=tb: e.tensor_tensor(out=s_dta[:, tb, :], in0=s_dt[:, tb, :], in1=hvb[:, 1, :], op=ALU.mult), reads=[s_dtd, hvbd], writes=[s_dtad])
                if dbg == 2:
                    if c4 == 3:
                        qd = tb // 4
                        fw.dma("sp", uT_d[:, qd * 512:(qd + 1) * 512].rearrange("(kc p) t -> p kc t", p=128), uqt[:], reads=[uqd], writes=[Dep("uTd")], chan_dep=uqd)
                    continue
                op("pe", lambda e, pd=pd, tb=tb: e.matmul(out=pd[:, 192:192 + NH], lhsT=tri[:], rhs=s_dta[:, tb, :], start=True, stop=True), reads=[trid, s_dtad], writes=[pdd], inc=False)
                op("pe", lambda e, pd=pd, tb=tb: e.matmul(out=pd[:, 256:256 + NH], lhsT=onesf[:], rhs=s_dta[:, tb, :], start=True, stop=True), reads=[onesfd, s_dtad], writes=[pdd])
                op("act", lambda e, pd=pd, tb=tb: e.activation(out=s_eacs[:, tb, :], in_=pd[:, 192:192 + NH], func=AF.Exp), reads=[], writes=[s_eacsd, pdd])
                op("act", lambda e, pd=pd, tb=tb: e.activation(out=s_cd[:, tb, :], in_=pd[:, 256:256 + NH], func=AF.Exp), reads=[], writes=[s_cdd, pdd])
                op("dve", lambda e, pd=pd, tb=tb: e.tensor_scalar(out=s_nacs[:, tb, :], in0=pd[:, 192:192 + NH], scalar1=-1.0, scalar2=None, op0=ALU.mult), reads=[], writes=[s_nacsd, pdd])
                op("dve", lambda e, pd=pd, dt_=dt_, tb=tb: e.tensor_tensor(out=dt_[:, 1, :], in0=pd[:, 256:256 + NH], in1=s_nacs[:, tb, :], op=ALU.add), reads=[s_nacsd], writes=[dtd_, pdd])
                op("act", lambda e, dt_=dt_: e.activation(out=dt_[:, 1, :], in_=dt_[:, 1, :], func=AF.Exp), reads=[dtd_], writes=[dtd_])
                op("dve", lambda e, dt_=dt_, tb=tb: e.tensor_tensor(out=s_dtdte[:, tb, :], in0=dt_[:, 1, :], in1=s_dt[:, tb, :], op=ALU.mult), reads=[dtd_, s_dtd], writes=[s_dtdted])
                if c4 == 3:
                    qd = tb // 4
                    fw.dma("sp", uT_d[:, qd * 512:(qd + 1) * 512].rearrange("(kc p) t -> p kc t", p=128), uqt[:], reads=[uqd], writes=[Dep("uTd")], chan_dep=uqd)
            fw.barrier()
        uT_ready = [(uq[i][1].chan.key, 16 * uq[i][1].chan.n) for i in range(2)]
        if cfg.get("stop") == 0:
            fw.emit()
            return nc

        with ExitStack() as ph:
            uring = [fw.sb("uring%d" % i, [128, KC, 512], BF16, ph) for i in range(3)]
            NW = 12
            wring = [fw.sb("wring%d" % i, [128, KC, 128], BF16, ph) for i in range(NW)]
            wctr = [0]
            uctr = [0]

            def load_w(col0):
                wt, wd = wring[wctr[0] % NW]
                wctr[0] += 1
                fw.dma("pool", wt[:], w_in[:, col0:col0 + 128].rearrange("(kc p) n -> p kc n", p=128), writes=[wd])
                return wt, wd

            unit_cols = []
            for g in range(G):
                unit_cols.append([OFF_Z + g * 256, OFF_Z + g * 256 + 128, OFF_X + g * 256, OFF_X + g * 256 + 128, OFF_B + g * 128, OFF_C + g * 128])
            for hd in range(H):
                unit_cols.append([OFF_Q + hd * 128, OFF_K + hd * 128, OFF_V + hd * 128])
            unit_w = {}

            def prefetch_unit(i):
                if i < len(unit_cols) and i not in unit_w:
                    unit_w[i] = [load_w(c) for c in unit_cols[i]]

            prefetch_unit(0)
            cast_per_quad = -(-len(cast_list) // max(1, (G + H // 2) * NQ))

            def load_u(qd):
                ut, utd = uring[uctr[0] % 3]
                uctr[0] += 1
                fw.dma("sp", ut[:], uT_d[:, qd * 512:(qd + 1) * 512].rearrange("(kc p) t -> p kc t", p=128), writes=[utd], extra=uT_ready)
                return ut, utd

            pjb = [banks[0], banks[1]]
            pjc = [0]

            def proj_fm(wt, wd, ut, utd):
                pb, pbd = pjb[pjc[0] % 2]
                pjc[0] += 1
                for kc in range(KC):
                    op("pe", lambda e, pb=pb, wt=wt, ut=ut, kc=kc: e.matmul(out=pb[:, :], lhsT=wt[:, kc, :], rhs=ut[:, kc, :], start=(kc == 0), stop=(kc == KC - 1)),
                       reads=[wd, utd], writes=[pbd], inc=(kc == KC - 1))
                return pb, pbd

            with ExitStack() as pa:
                assert G % 2 == 0
                snwt, snwtd = fw.sb("snwt", [128, DSSM], F32, pa)
                fw.dma("sp", snwt[:], snw[0, :].partition_broadcast(128), writes=[snwtd])
                TB = []
                for th in range(2):
                    n = lambda x, th=th: "%s_%d" % (x, th)
                    TB.append(dict(
                        cw=fw.sb(n("cw"), [128, 4, 5], F32, pa),
                        stage=[fw.sb(n("stage%d" % i), [128, 515], F32, pa) for i in range(2)],
                        acc=[fw.sb(n("acc%d" % i), [128, 512], F32, pa) for i in range(2)],
                        carry=fw.sb(n("carry"), [128, 4, 3], F32, pa),
                        xTq=[fw.sb(n("xTq%d" % i), [128, 2, 512], BF16, pa) for i in range(2)],
                        BTq=[fw.sb(n("BTq%d" % i), [128, 512], BF16, pa) for i in range(2)],
                        CTq=[fw.sb(n("CTq%d" % i), [128, 512], BF16, pa) for i in range(2)],
                        xB=fw.sb(n("xB"), [128, 384], BF16, pa),
                        xdte=fw.sb(n("xdte"), [128, 256], BF16, pa),
                        xdt=fw.sb(n("xdt"), [128, 256], BF16, pa),
                        cbm=fw.sb(n("cbm"), [128, 128], BF16, pa),
                        ldta=[fw.sb(n("ldta%d" % i), [128, 2, 128], F32, pa) for i in range(2)],
                        Lh=[fw.sb(n("Lh%d" % i), [128, 2, 128], F32, pa) for i in range(2)],
                        Mh=[fw.sb(n("Mh%d" % i), [128, 2, 128], BF16, pa) for i in range(2)],
                        S=fw.sb(n("S"), [128, 256], F32, pa),
                        Sb=fw.sb(n("Sb"), [128, 256], BF16, pa),
                        t1=fw.sb(n("t1"), [128, 256], F32, pa),
                        t2=fw.sb(n("t2"), [128, 256], F32, pa),
                        sz=fw.sb(n("sz"), [128, 256], F32, pa),
                        yj=fw.sb(n("yj"), [128, 256], BF16, pa),
                        yo=fw.sb(n("yo"), [128, 256], BF16, pa),
                        gst=fw.sb(n("gst"), [128, 4], F32, pa),
                        yT=fw.sb(n("yTs"), [128, 2, T], BF16, pa),
                    ))
                b_tr, b_trd = banks[2]
                b_cb, b_cbd = banks[2]
                b_ar, b_ard = banks[3]
                b_ys = [banks[4], banks[5]]
                b_z, b_zd = banks[6]
                b_st, b_std = banks[7]

                def run_rr(gens):
                    gens = list(gens)
                    while gens:
                        for gn in list(gens):
                            try:
                                next(gn)
                            except StopIteration:
                                gens.remove(gn)

                def ssd_quad(th, g, qd, ut, utd, uw, part):
                    Bf = TB[th]
                    cw, cwd = Bf["cw"]
                    stage, acc = Bf["stage"], Bf["acc"]
                    carry, carryd = Bf["carry"]
                    xTq, xTqd = Bf["xTq"][qd % 2]
                    BTq, BTqd = Bf["BTq"][qd % 2]
                    CTq, CTqd = Bf["CTq"][qd % 2]
                    xB, xBd = Bf["xB"]
                    xdte, xdted = Bf["xdte"]
                    xdt, xdtd = Bf["xdt"]
                    cbm, cbmd = Bf["cbm"]
                    ldta, Lh, Mh = Bf["ldta"], Bf["Lh"], Bf["Mh"]
                    S, Sd = Bf["S"]
                    Sb, Sbd = Bf["Sb"]
                    t1, t1d = Bf["t1"]
                    t2, t2d = Bf["t2"]
                    sz, szd = Bf["sz"]
                    yj, yjd = Bf["yj"]
                    yo, yod = Bf["yo"]
                    gst, gstd = Bf["gst"]
                    yT, yTd = Bf["yT"]
                    wz = [uw[0], uw[1]]
                    wx = [uw[2], uw[3]]
                    wB = uw[4]
                    wC = uw[5]
                    b_y, b_yd = b_ys[th]
                    sto = th * 256
                    own = qd >= NQ - NQO
                    chunks = [(0, wx[0], xTq[:, 0, :], xTqd), (1, wx[1], xTq[:, 1, :], xTqd), (2, wB, BTq[:, :], BTqd)]
                    if own or qd == NQ - NQO - 1:
                        chunks.append((3, wC, CTq[:, :], CTqd))
                    for ci, (wt, wd), dst, dstd in (chunks if part == "proj" else []):
                        pb, pbd = proj_fm(wt, wd, ut, utd)
                        sg, sgd = stage[ci % 2]
                        ac, acd = acc[ci % 2]
                        op("dve", lambda e, sg=sg, ci=ci: e.tensor_copy(out=sg[:, 0:3], in_=carry[:, ci, :]), reads=[carryd], writes=[sgd])
                        op("act", lambda e, sg=sg, pb=pb: e.copy(out=sg[:, 3:515], in_=pb[:, :]), reads=[pbd], writes=[sgd])
                        op("dve", lambda e, sg=sg, ci=ci: e.tensor_copy(out=carry[:, ci, :], in_=sg[:, 512:515]), reads=[sgd], writes=[carryd])
                        yield
                        if ci == 3 and not own:
                            continue
                        op("dve", lambda e, sg=sg, ac=ac, ci=ci: e.tensor_scalar(out=ac[:], in0=sg[:, 3:515], scalar1=cw[:, ci, 3:4], scalar2=cw[:, ci, 4:5], op0=ALU.mult, op1=ALU.add), reads=[sgd, cwd], writes=[acd])
                        for k in (2, 1, 0):
                            op("dve", lambda e, sg=sg, ac=ac, ci=ci, k=k: e.scalar_tensor_tensor(out=ac[:], in0=sg[:, k:k + 512], scalar=cw[:, ci, k:k + 1], in1=ac[:], op0=ALU.mult, op1=ALU.add), reads=[sgd, cwd, acd], writes=[acd])
                        op("act", lambda e, ac=ac, dst=dst: e.activation(out=dst, in_=ac[:], func=AF.Silu), reads=[acd], writes=[dstd])
                        yield
                    for c in (range(4) if part == "core" else []):
                        tb = qd * 4 + c
                        cs = slice(c * 128, (c + 1) * 128)
                        hs = slice(g * 4, g * 4 + 4)
                        trv = b_tr[:].bitcast(BF16)
                        if own:
                            for hp in range(2):
                                hh = g * 4 + 2 * hp
                                ld_, ldd_ = ldta[hp % 2]
                                L_, Ld_ = Lh[hp % 2]
                                ar = 2 * th * 128
                                op("pool", lambda e, ld_=ld_, tb=tb, hh=hh: e.tensor_tensor(out=ld_[:], in0=sgt[:].unsqueeze(1).to_broadcast([128, 2, 128]),
                                                                                 in1=s_dta[:, tb, hh:hh + 2].unsqueeze(2).to_broadcast([128, 2, 128]), op=ALU.mult), reads=[sgtd, s_dtad], writes=[ldd_])
                                for k2 in range(2):
                                    op("pe", lambda e, ld_=ld_, ar=ar, k2=k2: e.matmul(out=b_ar[:, ar + k2 * 128:ar + (k2 + 1) * 128], lhsT=ld_[:, k2, :], rhs=tri[:], start=True, stop=True), reads=[ldd_, trid], writes=[b_ard], inc=(k2 == 1))
                                yield
                                op("act", lambda e, L_=L_, ar=ar: e.activation(out=L_[:].rearrange("p a b -> p (a b)"), in_=b_ar[:, ar:ar + 256], func=AF.Exp), reads=[b_ard], writes=[Ld_])
                                yield
                        for i in range(2):
                            op("pe", lambda e, i=i, cs=cs, trv=trv: e.transpose(out=trv[:, i * 128:(i + 1) * 128], in_=xTq[:, i, cs], identity=ident[:]), reads=[xTqd, identd], writes=[b_trd], inc=False)
                        op("pe", lambda e, cs=cs, trv=trv: e.transpose(out=trv[:, 256:384], in_=BTq[:, cs], identity=ident[:]), reads=[BTqd, identd], writes=[b_trd])
                        op("act", lambda e, trv=trv: e.copy(out=xB[:], in_=trv[:, 0:384]), reads=[b_trd], writes=[xBd])
                        yield
                        if own:
                            for i in range(2):
                                for kc in range(KC):
                                    op("pe", lambda e, kc=kc, i=i, cs=cs: e.matmul(out=b_z[:, i * 128:(i + 1) * 128], lhsT=ut[:, kc, cs], rhs=wz[i][0][:, kc, :], start=(kc == 0), stop=(kc == KC - 1)),
                                       reads=[utd, wz[i][1]], writes=[b_zd], inc=(kc == KC - 1 and i == 1))
                            op("act", lambda e: e.activation(out=sz[:], in_=b_z[:, 0:256], func=AF.Silu), reads=[b_zd], writes=[szd])
                            yield
                        op("dve", lambda e, tb=tb, hs=hs: e.tensor_tensor(out=xdte[:].rearrange("p (h d) -> p h d", d=64), in0=xB[:, 0:256].rearrange("p (h d) -> p h d", d=64),
                                                                  in1=s_dtdte[:, tb, hs].unsqueeze(2).to_broadcast([128, 4, 64]), op=ALU.mult), reads=[xBd, s_dtdted], writes=[xdted])
                        op("pe", lambda e: e.matmul(out=b_st[:, sto:sto + 256], lhsT=xB[:, 256:384], rhs=xdte[:], start=True, stop=True), reads=[xBd, xdted], writes=[b_std])
                        yield
                        if own:
                            op("pe", lambda e, cs=cs: e.matmul(out=b_cb[:, 256:384], lhsT=BTq[:, cs], rhs=CTq[:, cs], start=True, stop=True), reads=[BTqd, CTqd], writes=[b_cbd])
                            op("dve", lambda e: e.tensor_tensor(out=cbm[:], in0=b_cb[:, 256:384], in1=tri[:], op=ALU.mult), reads=[b_cbd, trid], writes=[cbmd])
                            op("dve", lambda e, tb=tb, hs=hs: e.tensor_tensor(out=xdt[:].rearrange("p (h d) -> p h d", d=64), in0=xB[:, 0:256].rearrange("p (h d) -> p h d", d=64),
                                                                      in1=s_dt[:, tb, hs].unsqueeze(2).to_broadcast([128, 4, 64]), op=ALU.mult), reads=[xBd, s_dtd], writes=[xdtd])
                            op("pe", lambda e, cs=cs: e.matmul(out=b_y[:, 256:512], lhsT=CTq[:, cs], rhs=Sb[:], start=True, stop=True), reads=[CTqd, Sbd], writes=[b_yd])
                            yield
                        op("dve", lambda e, tb=tb, hs=hs: e.tensor_tensor(out=S[:].rearrange("p (h d) -> p h d", d=64), in0=S[:].rearrange("p (h d) -> p h d", d=64),
                                                                  in1=s_cd[:, tb, hs].unsqueeze(2).to_broadcast([128, 4, 64]), op=ALU.mult), reads=[Sd, s_cdd], writes=[Sd])
                        op("dve", lambda e: e.tensor_tensor(out=S[:], in0=b_st[:, sto:sto + 256], in1=S[:], op=ALU.add), reads=[b_std, Sd], writes=[Sd])
                        if tb == NBLK - T // 128 - 1:
                            op("dve", lambda e: e.tensor_scalar(out=S[:], in0=S[:], scalar1=flg[:, 0:1], scalar2=None, op0=ALU.mult), reads=[Sd, flgd], writes=[Sd])
                        op("act", lambda e: e.copy(out=Sb[:], in_=S[:]), reads=[Sd], writes=[Sbd])
                        yield
                        if own:
                            for hp in range(2):
                                L_, Ld_ = Lh[hp % 2]
                                M_, Md_ = Mh[hp % 2]
                                op("pool", lambda e, L_=L_, M_=M_: e.tensor_tensor(out=M_[:], in0=cbm[:].unsqueeze(1).to_broadcast([128, 2, 128]), in1=L_[:], op=ALU.mult), reads=[cbmd, Ld_], writes=[Md_])
                                yield
                                for k2 in range(2):
                                    h = 2 * hp + k2
                                    op("pe", lambda e, M_=M_, h=h, k2=k2: e.matmul(out=b_y[:, h * 64:(h + 1) * 64], lhsT=M_[:, k2, :], rhs=xdt[:, h * 64:(h + 1) * 64], start=True, stop=True), reads=[Md_, xdtd], writes=[b_yd], inc=(h == 3))
                            yield
                            op("pool", lambda e, hs=hs: e.tensor_tensor(out=t2[:].rearrange("p (h d) -> p h d", d=64), in0=xB[:, 0:256].rearrange("p (h d) -> p h d", d=64),
                                                                 in1=hvb[:, 2, hs].unsqueeze(2).to_broadcast([128, 4, 64]), op=ALU.mult), reads=[xBd, hvbd], writes=[t2d])
                            op("dve", lambda e, tb=tb, hs=hs: e.tensor_tensor(out=t1[:].rearrange("p (h d) -> p h d", d=64), in0=b_y[:, 256:512].rearrange("p (h d) -> p h d", d=64),
                                                                      in1=s_eacs[:, tb, hs].unsqueeze(2).to_broadcast([128, 4, 64]), op=ALU.mult), reads=[b_yd, s_eacsd], writes=[t1d])
                            op("dve", lambda e: e.tensor_tensor(out=t1[:], in0=b_y[:, 0:256], in1=t1[:], op=ALU.add), reads=[b_yd, t1d], writes=[t1d])
                            yield
                            op("pool", lambda e: e.tensor_tensor(out=t1[:], in0=t1[:], in1=t2[:], op=ALU.add), reads=[t1d, t2d], writes=[t1d])
                            op("dve", lambda e: e.tensor_tensor(out=t1[:], in0=t1[:], in1=sz[:], op=ALU.mult), reads=[t1d, szd], writes=[t1d])
                            yield
                            op("act", lambda e: e.activation(out=yj[:], in_=t1[:], func=AF.Square, scale=1.0 / 16.0, accum_out=gst[:, 0:1]), reads=[t1d], writes=[yjd, gstd])
                            op("act", lambda e: e.activation(out=gst[:, 1:2], in_=gst[:, 0:1], func=AF.Ln, bias=EPS), reads=[gstd], writes=[gstd])
                            op("act", lambda e: e.activation(out=gst[:, 2:3], in_=gst[:, 1:2], func=AF.Exp, scale=-0.5), reads=[gstd], writes=[gstd])
                            yield
                            op("dve", lambda e: e.scalar_tensor_tensor(out=yo[:], in0=t1[:], scalar=gst[:, 2:3], in1=snwt[:, g * 256:(g + 1) * 256], op0=ALU.mult, op1=ALU.mult), reads=[t1d, gstd, snwtd], writes=[yod])
                            zv = b_z[:].bitcast(BF16)
                            for i in range(2):
                                op("pe", lambda e, i=i, zv=zv: e.transpose(out=zv[:, 512 + i * 128:512 + (i + 1) * 128], in_=yo[:, i * 128:(i + 1) * 128], identity=ident[:]), reads=[yod, identd], writes=[b_zd], inc=(i == 1))
                            to = (qd - (NQ - NQO)) * 512 + c * 128
                            op("act", lambda e, zv=zv, to=to: e.copy(out=yT[:, :, to:to + 128], in_=zv[:, 512:768].rearrange("p (i t) -> p i t", t=128)), reads=[b_zd], writes=[yTd])
                            yield

                for g0 in range(0, G, 2):
                    grp = (g0, g0 + 1)
                    for th, g in enumerate(grp):
                        prefetch_unit(g)
                        cw, cwd = TB[th]["cw"]
                        for i, r0 in enumerate((g * 256, g * 256 + 128, DSSM + g * 128, DSSM + G * 128 + g * 128)):
                            fw.dma("sp", cw[:, i, :], cwb[r0:r0 + 128, :], writes=[cwd])
                        S, Sd = TB[th]["S"]
                        Sb, Sbd = TB[th]["Sb"]
                        carry, carryd = TB[th]["carry"]
                        op("dve", lambda e, S=S: e.memset(S[:], 0.0), writes=[Sd])
                        op("dve", lambda e, Sb=Sb: e.memset(Sb[:], 0.0), writes=[Sbd])
                        op("dve", lambda e, carry=carry: e.memset(carry[:], 0.0), writes=[carryd])
                    uts = {0: load_u(0)}
                    run_rr([ssd_quad(th, g, 0, uts[0][0], uts[0][1], unit_w[g], "proj") for th, g in enumerate(grp)])
                    for qd in range(NQ):
                        cast_some(2 * cast_per_quad)
                        gens = [ssd_quad(th, g, qd, uts[qd][0], uts[qd][1], unit_w[g], "core") for th, g in enumerate(grp)]
                        pgens = []
                        if qd + 1 < NQ:
                            uts[qd + 1] = load_u(qd + 1)
                            pgens = [ssd_quad(th, g, qd + 1, uts[qd + 1][0], uts[qd + 1][1], unit_w[g], "proj") for th, g in enumerate(grp)]
                        own_q = qd >= NQ - NQO
                        period = 7 if own_q else 2
                        rnd = 0
                        while gens or pgens:
                            for gn in list(gens):
                                try:
                                    next(gn)
                                except StopIteration:
                                    gens.remove(gn)
                            rnd += 1
                            if pgens and (rnd % period == 0 or not gens):
                                gn = pgens[(rnd // period) % len(pgens)] if gens else pgens[0]
                                try:
                                    next(gn)
                                except StopIteration:
                                    pgens.remove(gn)
                    for th, g in enumerate(grp):
                        yT, yTd = TB[th]["yT"]
                        fw.dma("sp", ymT_d[g * 256:(g + 1) * 256, :].rearrange("(i p) t -> p i t", p=128), yT[:], reads=[yTd], writes=[Dep("ym")], chan_dep=yTd)
                ym_ready = [(TB[i]["yT"][1].chan.key, 16 * TB[i]["yT"][1].chan.n) for i in range(2) if TB[i]["yT"][1].chan is not None]
                fw.barrier()
                if cfg.get("stop") == 1:
                    fw.emit()
                    return nc

            with ExitStack() as pa:
                KTs = [fw.sb("KT%d" % i, [128, TT], BF16, pa) for i in range(2)]
                VTs = [fw.sb("VT%d" % i, [128, TT], BF16, pa) for i in range(2)]
                QTs = [fw.sb("QT%d" % i, [128, T], BF16, pa) for i in range(2)]
                NVB = T // 128 + DIL[2]
                Vd_, Vdd_ = fw.sb("Vd", [128, NVB, 128], BF16, pa)
                aacc, aaccd = fw.sb("aacc", [128, 2, T], F32, pa)
                PT = [fw.sb("PT%d" % i, [128, 2, 128], BF16, pa) for i in range(2)]
                PM = [fw.sb("PM%d" % i, [128, 2, 128], BF16, pa) for i in range(3)]
                yA = [fw.sb("yA%d" % i, [128, T], BF16, pa) for i in range(2)]
                b_vt = [banks[2], banks[2]]
                b_s = [banks[3], banks[4], banks[5]]
                b_o = [banks[6], banks[7]]
                uc = [0]
                SKEW = 2

                def att_proj(hd):
                    prefetch_unit(G + hd)
                    prefetch_unit(G + hd + 1)
                    wq, wk, wv = unit_w[G + hd]
                    KT, KTd = KTs[hd % 2]
                    VT, VTd = VTs[hd % 2]
                    QT, QTd = QTs[hd % 2]
                    for qd in range(NQ):
                        own = qd >= NQ - NQO
                        ut, utd = load_u(qd)
                        cast_some(cast_per_quad)
                        ts_ = slice(qd * 512, (qd + 1) * 512)
                        pb, pbd = proj_fm(wk[0], wk[1], ut, utd)
                        op("act", lambda e, pb=pb, ts_=ts_: e.copy(out=KT[:, ts_], in_=pb[:, :]), reads=[pbd], writes=[KTd])
                        yield
                        pb, pbd = proj_fm(wv[0], wv[1], ut, utd)
                        op("dve", lambda e, pb=pb, ts_=ts_: e.tensor_copy(out=VT[:, ts_], in_=pb[:, :]), reads=[pbd], writes=[VTd])
                        yield
                        if own:
                            to = (qd - (NQ - NQO)) * 512
                            pb, pbd = proj_fm(wq[0], wq[1], ut, utd)
                            op("act", lambda e, pb=pb, to=to: e.activation(out=QT[:, to:to + 512], in_=pb[:, :], func=AF.Copy, scale=128.0 ** -0.5), reads=[pbd], writes=[QTd])
                            yield

                def att_units(hd):
                    KT, KTd = KTs[hd % 2]
                    VT, VTd = VTs[hd % 2]
                    QT, QTd = QTs[hd % 2]
                    first = True
                    for d in DIL:
                        nj = T // (128 * d)
                        nblk = d * (nj + 1)
                        for b0 in range(0, nblk, 4):
                            vb, vbd = b_vt[(b0 // 4) % 2]
                            vbv = vb[:].bitcast(BF16)
                            nb = min(4, nblk - b0)
                            for i in range(nb):
                                r, jj = divmod(b0 + i, nj + 1)
                                st_ = C + (jj - 1) * 128 * d + r
                                op("pe", lambda e, vbv=vbv, i=i, st_=st_, d=d: e.transpose(out=vbv[:, i * 128:(i + 1) * 128], in_=VT[:, st_:st_ + 127 * d + 1:d], identity=ident[:]), reads=[VTd, identd], writes=[vbd], inc=(i == nb - 1))
                            op("act", lambda e, vbv=vbv, b0=b0, nb=nb: e.copy(out=Vd_[:, b0:b0 + nb, :], in_=vbv[:, 0:nb * 128].rearrange("p (i t) -> p i t", t=128)), reads=[vbd], writes=[Vdd_])
                            yield
                        units = [(r, j) for r in range(d) for j in range(nj)]
                        info = {}

                        def stage_a(idx):
                            r, j = units[idx]
                            u_i = uc[0]
                            uc[0] += 1
                            bs, bsd = b_s[u_i % 3]
                            Pm_, Pmd_ = PM[u_i % 3]
                            q0 = j * 128 * d + r
                            qsl = slice(q0, q0 + 127 * d + 1, d)
                            kcur = slice(C + q0, C + q0 + 127 * d + 1, d)
                            kprev = slice(C + q0 - 128 * d, C + q0 - d + 1, d)
                            mk, mkd = (maskB, maskBd) if j == 0 else (maskA, maskAd)
                            op("pe", lambda e: e.matmul(out=bs[:, 0:256], lhsT=ident[:], rhs=mk[:].rearrange("p a b -> p (a b)"), start=True, stop=False), reads=[identd, mkd], writes=[bsd], inc=False)
                            op("pe", lambda e: e.matmul(out=bs[:, 0:128], lhsT=KT[:, kprev], rhs=QT[:, qsl], start=False, stop=False), reads=[KTd, QTd], writes=[bsd], inc=False)
                            op("pe", lambda e: e.matmul(out=bs[:, 128:256], lhsT=KT[:, kcur], rhs=QT[:, qsl], start=False, stop=True), reads=[KTd, QTd], writes=[bsd])
                            op("act", lambda e: e.activation(out=Pm_[:].rearrange("p a b -> p (a b)"), in_=bs[:, 0:256], func=AF.Exp), reads=[bsd], writes=[Pmd_])
                            info[idx] = (u_i, Pm_, Pmd_, qsl)

                        def stage_b(idx, first):
                            r, j = units[idx]
                            u_i, Pm_, Pmd_, qsl = info.pop(idx)
                            bo, bod = b_o[u_i % 2]
                            vi_prev = r * (nj + 1) + j
                            vi_cur = vi_prev + 1
                            op("pe", lambda e: e.matmul(out=bo[:, 0:128], lhsT=Vd_[:, vi_prev, :], rhs=Pm_[:, 0, :], start=True, stop=False), reads=[Vdd_, Pmd_], writes=[bod], inc=False)
                            op("pe", lambda e: e.matmul(out=bo[:, 0:128], lhsT=Vd_[:, vi_cur, :], rhs=Pm_[:, 1, :], start=False, stop=True), reads=[Vdd_, Pmd_], writes=[bod], inc=False)
                            op("pe", lambda e: e.matmul(out=bo[:, 128:256], lhsT=onesb[:], rhs=Pm_[:, 0, :], start=True, stop=False), reads=[onesbd, Pmd_], writes=[bod], inc=False)
                            op("pe", lambda e: e.matmul(out=bo[:, 128:256], lhsT=onesb[:], rhs=Pm_[:, 1, :], start=False, stop=True), reads=[onesbd, Pmd_], writes=[bod])
                            src = bo[:, 0:256].rearrange("p (a t) -> p a t", t=128)
                            if first:
                                op("dve", lambda e: e.tensor_copy(out=aacc[:, :, qsl], in_=src), reads=[bod], writes=[aaccd])
                            else:
                                op("dve", lambda e: e.tensor_tensor(out=aacc[:, :, qsl], in0=src, in1=aacc[:, :, qsl], op=ALU.add), reads=[bod, aaccd], writes=[aaccd])

                        n_u = len(units)
                        for idx in range(n_u + SKEW):
                            if idx < n_u:
                                stage_a(idx)
                            if idx >= SKEW:
                                stage_b(idx - SKEW, first)
                            yield
                        first = False
                    ya, yad = yA[hd % 2]
                    op("dve", lambda e: e.reciprocal(out=aacc[:, 1, :], in_=aacc[:, 1, :]), reads=[aaccd], writes=[aaccd])
                    op("dve", lambda e: e.tensor_tensor(out=ya[:], in0=aacc[:, 0, :], in1=aacc[:, 1, :], op=ALU.mult), reads=[aaccd], writes=[yad])
                    fw.dma("sp", ymT_d[DSSM + hd * 128:DSSM + (hd + 1) * 128, :], ya[:], reads=[yad], writes=[Dep("ym")], chan_dep=yad)
                    yield

                for _ in att_proj(0):
                    pass
                for hd in range(H):
                    gu = att_units(hd)
                    gp = att_proj(hd + 1) if hd + 1 < H else None
                    alive_u = True
                    while alive_u or gp is not None:
                        for _ in range(3):
                            if alive_u:
                                try:
                                    next(gu)
                                except StopIteration:
                                    alive_u = False
                        if gp is not None:
                            try:
                                next(gp)
                            except StopIteration:
                                gp = None
                ym_ready += [(yA[i][1].chan.key, 16 * yA[i][1].chan.n) for i in range(2) if yA[i][1].chan is not None]
                fw.barrier()
        cast_some(len(cast_list))
        wcast_ready = [(d_.chan.key, 16 * d_.chan.n) for d_ in wcastds if d_.chan is not None]
        p01.close()

        with ExitStack() as ph:
            nwpost, nwpostd = fw.sb("nwA", [128, D], F32, ph)
            nwpre, nwpred = fw.sb("nwB", [128, D], F32, ph)
            nwfin, nwfind = nwpost, nwpostd
            fw.dma("sp", nwpre[:], nw4[2, :].partition_broadcast(128), writes=[nwpred])
            NWP = 4
            PR = 8
            wp = [fw.sb("wp%d" % i, [128, PR, 512], BF16, ph) for i in range(NWP)]
            wpc = [0]
            big, bigd = fw.sb("big", [128, max(MKC, FC), 512], BF16, ph)
            bigs = [Dep("big%d" % i) for i in range(max(MKC, FC))]
            xh, xhd = fw.sb("xh", [128, 4, D], F32, ph)
            mf, mfd = fw.sb("mf", [128, 4, D], F32, ph)
            hn, hnd = fw.sb("hn", [128, D], BF16, ph)
            hnT, hnTd = fw.sb("hnT", [128, KC, 512], BF16, ph)
            rls = [fw.sb("rl%d" % i, [128, 512], F32, ph) for i in range(2)]
            st2, st2d = fw.sb("st2", [128, 4, 12], F32, ph)

            def load_wp(src, r0, nrow_chunks, c0, ncol):
                wt, wd = wp[wpc[0] % NWP]
                wpc[0] += 1
                fw.dma("sp", wt[:, 0:nrow_chunks, 0:ncol], src[r0:r0 + nrow_chunks * 128, c0:c0 + ncol].rearrange("(kc p) n -> p kc n", p=128), writes=[wd], extra=wcast_ready)
                return wt, wd

            NCB = D // 512 if D >= 512 else 1
            CBW = min(512, D)
            for qd in range(NQO):
                tq = slice(qd * 512, (qd + 1) * 512)
                for k0 in range(0, MKC, 16):
                    k1 = min(MKC, k0 + 16)
                    tk = fw.dma("sp", big[:, k0:k1, :], ymT_d[k0 * 128:k1 * 128, tq].rearrange("(kc p) t -> p kc t", p=128), writes=bigs[k0:k1], extra=ym_ready, chan_dep=bigd)
                for k in range(MKC):
                    bigs[k].w = tk
                fw.dma("sp", xh[:], xc[C + qd * 512:C + (qd + 1) * 512, :].rearrange("(a p) d -> p a d", p=128), writes=[xhd])
                for cb in range(NCB):
                    npiece = (MKC + PR - 1) // PR
                    for pi in range(npiece):
                        nr = min(PR, MKC - pi * PR)
                        wt, wd = load_wp(wout_b, pi * PR * 128, nr, cb * CBW, CBW)
                        for tb in range(4):
                            pb, pbd = banks[(cb % 2) * 4 + tb]
                            for k in range(nr):
                                kc = pi * PR + k
                                op("pe", lambda e, pb=pb, wt=wt, k=k, kc=kc, tb=tb: e.matmul(out=pb[:, 0:CBW], lhsT=big[:, kc, tb * 128:(tb + 1) * 128], rhs=wt[:, k, 0:CBW], start=(kc == 0), stop=(kc == MKC - 1)),
                                   reads=[bigs[kc], wd], writes=[pbd], inc=(k == nr - 1))
                    for tb in range(4):
                        pb, pbd = banks[(cb % 2) * 4 + tb]
                        if tb % 2 == 0:
                            op("act", lambda e, pb=pb, tb=tb, cb=cb: e.copy(out=mf[:, tb, cb * CBW:(cb + 1) * CBW], in_=pb[:, 0:CBW]), reads=[pbd], writes=[mfd])
                        else:
                            op("dve", lambda e, pb=pb, tb=tb, cb=cb: e.tensor_copy(out=mf[:, tb, cb * CBW:(cb + 1) * CBW], in_=pb[:, 0:CBW]), reads=[pbd], writes=[mfd])
                fw.dma("sp", nwpost[:], nw4[1, :].partition_broadcast(128), writes=[nwpostd])
                junk2, junk2d = hn, hnd
                for tb in range(4):
                    op("act", lambda e, tb=tb: e.activation(out=junk2[:], in_=mf[:, tb, :], func=AF.Square, scale=float(D) ** -0.5, accum_out=st2[:, tb, 0:1]), reads=[mfd], writes=[junk2d, st2d])
                    op("act", lambda e, tb=tb: e.activation(out=st2[:, tb, 1:2], in_=st2[:, tb, 0:1], func=AF.Ln, bias=EPS), reads=[st2d], writes=[st2d])
                    op("act", lambda e, tb=tb: e.activation(out=st2[:, tb, 2:3], in_=st2[:, tb, 1:2], func=AF.Exp, scale=-0.5), reads=[st2d], writes=[st2d])
                    op("dve", lambda e, tb=tb: e.scalar_tensor_tensor(out=mf[:, tb, :], in0=mf[:, tb, :], scalar=st2[:, tb, 2:3], in1=nwpost[:], op0=ALU.mult, op1=ALU.mult), reads=[mfd, st2d, nwpostd], writes=[mfd])
                    op("pool", lambda e, tb=tb: e.tensor_tensor(out=xh[:, tb, :], in0=xh[:, tb, :], in1=mf[:, tb, :], op=ALU.add), reads=[xhd, mfd], writes=[xhd])
                    op("act", lambda e, tb=tb: e.activation(out=junk2[:], in_=xh[:, tb, :], func=AF.Square, scale=float(D) ** -0.5, accum_out=st2[:, tb, 3:4]), reads=[xhd], writes=[junk2d, st2d])
                    op("act", lambda e, tb=tb: e.activation(out=st2[:, tb, 4:5], in_=st2[:, tb, 3:4], func=AF.Ln, bias=EPS), reads=[st2d], writes=[st2d])
                    op("act", lambda e, tb=tb: e.activation(out=st2[:, tb, 5:6], in_=st2[:, tb, 4:5], func=AF.Exp, scale=-0.5), reads=[st2d], writes=[st2d])
                    op("dve", lambda e, tb=tb: e.scalar_tensor_tensor(out=hn[:], in0=xh[:, tb, :], scalar=st2[:, tb, 5:6], in1=nwpre[:], op0=ALU.mult, op1=ALU.mult), reads=[xhd, st2d, nwpred], writes=[hnd])
                    nper = 8 if KC >= 8 else KC
                    ngrp = (KC + nper - 1) // nper
                    for gi in range(ngrp):
                        pb, pbd = banks[(tb * ngrp + gi) % 8]
                        pbv = pb[:].bitcast(BF16)
                        n_in = min(nper, KC - gi * nper)
                        for j in range(n_in):
                            kc = gi * nper + j
                            op("pe", lambda e, pbv=pbv, kc=kc, j=j: e.transpose(out=pbv[:, j * 128:(j + 1) * 128], in_=hn[:, kc * 128:(kc + 1) * 128], identity=ident[:]), reads=[hnd, identd], writes=[pbd], inc=(j == n_in - 1))
                        op("act", lambda e, pbv=pbv, gi=gi, n_in=n_in, tb=tb: e.copy(out=hnT[:, gi * nper:gi * nper + n_in, tb * 128:(tb + 1) * 128], in_=pbv[:, 0:n_in * 128].rearrange("p (j t) -> p j t", t=128)), reads=[pbd], writes=[hnTd])
                nfc_per = 4
                for f0 in range(0, FC, nfc_per):
                    nf = min(nfc_per, FC - f0)
                    halves = []
                    for k0 in range(0, KC, PR):
                        nr = min(PR, KC - k0)
                        halves.append((load_wp(wup_b, k0 * 128, nr, f0 * 128, nf * 128), k0, nr))
                    for fi in range(nf):
                        fc = f0 + fi
                        pb, pbd = banks[fc % 8]
                        for (wt, wd), k0, nr in halves:
                            for k in range(nr):
                                kc = k0 + k
                                op("pe", lambda e, pb=pb, wt=wt, fi=fi, k=k, kc=kc: e.matmul(out=pb[:, :], lhsT=wt[:, k, fi * 128:(fi + 1) * 128], rhs=hnT[:, kc, :], start=(kc == 0), stop=(kc == KC - 1)),
                                   reads=[wd, hnTd], writes=[pbd], inc=(kc == KC - 1))
                        rl, rld = rls[fc % 2]
                        op("act", lambda e, pb=pb, rl=rl: e.activation(out=rl[:], in_=pb[:, :], func=AF.Relu), reads=[pbd], writes=[rld])
                        op("dve", lambda e, fc=fc, rl=rl: e.tensor_tensor(out=big[:, fc, :], in0=rl[:], in1=rl[:], op=ALU.mult), reads=[rld], writes=[bigs[fc]])
                for cb in range(NCB):
                    npiece = (FC + PR - 1) // PR
                    for pi in range(npiece):
                        nr = min(PR, FC - pi * PR)
                        wt, wd = load_wp(wdown_b, pi * PR * 128, nr, cb * CBW, CBW)
                        for tb in range(4):
                            pb, pbd = banks[(cb % 2) * 4 + tb]
                            for k in range(nr):
                                fc = pi * PR + k
                                op("pe", lambda e, pb=pb, wt=wt, k=k, fc=fc, tb=tb: e.matmul(out=pb[:, 0:CBW], lhsT=big[:, fc, tb * 128:(tb + 1) * 128], rhs=wt[:, k, 0:CBW], start=(fc == 0), stop=(fc == FC - 1)),
                                   reads=[bigs[fc], wd], writes=[pbd], inc=(k == nr - 1))
                    for tb in range(4):
                        pb, pbd = banks[(cb % 2) * 4 + tb]
                        if tb % 2 == 0:
                            op("act", lambda e, pb=pb, tb=tb, cb=cb: e.copy(out=mf[:, tb, cb * CBW:(cb + 1) * CBW], in_=pb[:, 0:CBW]), reads=[pbd], writes=[mfd])
                        else:
                            op("dve", lambda e, pb=pb, tb=tb, cb=cb: e.tensor_copy(out=mf[:, tb, cb * CBW:(cb + 1) * CBW], in_=pb[:, 0:CBW]), reads=[pbd], writes=[mfd])
                fw.dma("sp", nwfin[:], nw4[3, :].partition_broadcast(128), writes=[nwfind])
                for tb in range(4):
                    op("act", lambda e, tb=tb: e.activation(out=junk2[:], in_=mf[:, tb, :], func=AF.Square, scale=float(D) ** -0.5, accum_out=st2[:, tb, 6:7]), reads=[mfd], writes=[junk2d, st2d])
                    op("act", lambda e, tb=tb: e.activation(out=st2[:, tb, 7:8], in_=st2[:, tb, 6:7], func=AF.Ln, bias=EPS), reads=[st2d], writes=[st2d])
                    op("act", lambda e, tb=tb: e.activation(out=st2[:, tb, 8:9], in_=st2[:, tb, 7:8], func=AF.Exp, scale=-0.5), reads=[st2d], writes=[st2d])
                    op("dve", lambda e, tb=tb: e.scalar_tensor_tensor(out=mf[:, tb, :], in0=mf[:, tb, :], scalar=st2[:, tb, 8:9], in1=nwfin[:], op0=ALU.mult, op1=ALU.mult), reads=[mfd, st2d, nwfind], writes=[mfd])
                    op("pool", lambda e, tb=tb: e.tensor_tensor(out=mf[:, tb, :], in0=xh[:, tb, :], in1=mf[:, tb, :], op=ALU.add), reads=[xhd, mfd], writes=[mfd])
                fw.dma("sp", out[tq, :].rearrange("(a p) d -> p a d", p=128), mf[:], reads=[mfd], writes=[Dep("out")], chan_dep=mfd)
            fw.barrier()
        fw.emit()
    return nc


_CACHE = {}


def make_in_maps(cfg, ncores, inputs):
    D, T, G = cfg["D"], cfg["T"], cfg["G"]
    x = np.asarray(inputs["x"], np.float32)
    B_, S_, _ = x.shape
    halves = S_ // T
    assert halves == 2 and B_ * halves == ncores
    shared = {
        "w_in": np.ascontiguousarray(np.asarray(inputs["w_in"], np.float32)[0]),
        "cwb": np.ascontiguousarray(np.concatenate([np.asarray(inputs["conv_w"], np.float32)[0].T, np.asarray(inputs["conv_b"], np.float32)[0][:, None]], axis=1)),
        "hv": np.ascontiguousarray(np.stack([np.asarray(inputs["dt_bias"], np.float32)[0], np.asarray(inputs["a_log"], np.float32)[0], np.asarray(inputs["d_skip"], np.float32)[0]], axis=0)),
        "snw": np.ascontiguousarray(np.asarray(inputs["ssm_norm_w"], np.float32)),
        "nw4": np.ascontiguousarray(np.stack([np.asarray(inputs[k], np.float32)[0] for k in ("norm_mix_pre", "norm_mix_post", "norm_mlp_pre", "norm_mlp_post")], axis=0)),
        "w_out": np.ascontiguousarray(np.asarray(inputs["w_out"], np.float32)[0]),
        "w_up": np.ascontiguousarray(np.asarray(inputs["w_up"], np.float32)[0]),
        "w_down": np.ascontiguousarray(np.asarray(inputs["w_down"], np.float32)[0]),
    }
    maps = []
    for c in range(ncores):
        b, h = divmod(c, 2)
        if h == 0:
            xcc = np.concatenate([np.zeros((T, D), np.float32), x[b, :T]], axis=0)
            fl = np.zeros((128, 1), np.float32)
        else:
            xcc = x[b]
            fl = np.ones((128, 1), np.float32)
        m = dict(shared)
        m["xc"] = np.ascontiguousarray(xcc)
        m["flag"] = fl
        maps.append(m)
    return maps


def kernel(**inputs):
    cfg = FULL_CFG
    if "nc" not in _CACHE:
        _CACHE["nc"] = build_program(cfg)
    nc = _CACHE["nc"]
    maps = make_in_maps(cfg, 8, inputs)
    res = run_bass_kernel_spmd(nc, maps, core_ids=list(range(8)))
    x = inputs["x"]
    B_, S_, D = x.shape
    T = cfg["T"]
    outp = np.empty((B_, S_, D), np.float32)
    for c in range(8):
        b, h = divmod(c, 2)
        outp[b, h * T:(h + 1) * T] = res.results[c]["out"]
    return outp
```

```python
from contextlib import ExitStack
import numpy as np
import concourse.bass as bass
import concourse.mybir as mybir
from concourse.bass_utils import run_bass_kernel_spmd

F32 = mybir.dt.float32
BF16 = mybir.dt.bfloat16
AF = mybir.ActivationFunctionType
ALU = mybir.AluOpType

ENGS = ("pe", "act", "dve", "pool", "sp")
EPS = 1e-6


class Dep:
    __slots__ = ("w", "r", "chan", "name")

    def __init__(self, name=""):
        self.w = None
        self.r = []
        self.chan = None
        self.name = name


class Chan:
    __slots__ = ("key", "sem", "n")


class FW:
    def __init__(self, nc, stack, same_engine_sync=True):
        self.nc = nc
        self.stack = stack
        self.streams = {e: [] for e in ENGS}
        self.cnt = {e: 0 for e in ENGS}
        self.seen = {e: {} for e in ENGS}
        self.sems = {}
        self.latest = {}
        self.nchan = 0
        self.same = same_engine_sync
        for e in ENGS:
            self.sems[e] = stack.enter_context(nc.semaphore("s_" + e))
            self.latest[e] = 0

    def sb(self, name, shape, dtype, stack=None):
        t = (stack or self.stack).enter_context(self.nc.sbuf_tensor(name, list(shape), dtype))
        return t, Dep(name)

    def ps(self, name, shape, dtype=F32, stack=None):
        t = (stack or self.stack).enter_context(self.nc.psum_tensor(name, list(shape), dtype))
        return t, Dep(name)

    def chan_of(self, dep):
        if dep.chan is None:
            c = Chan()
            c.key = "c%d" % self.nchan
            self.nchan += 1
            c.sem = self.stack.enter_context(self.nc.semaphore("d_" + c.key))
            c.n = 0
            self.sems[c.key] = c.sem
            self.latest[c.key] = 0
            dep.chan = c
        return dep.chan

    def _waits(self, eng, reads, writes, extra, group_key=None):
        need = {}

        def add(t):
            if t is None:
                return
            k, v = t
            if need.get(k, 0) < v:
                need[k] = v

        for d in reads:
            add(d.w)
        for d in writes:
            if not (group_key is not None and d.w is not None and d.w[0] == group_key):
                add(d.w)
            for t in d.r:
                add(t)
        for t in extra:
            add(t)
        st = self.streams[eng]
        for k, v in need.items():
            if k == eng and (eng == "pe" or not self.same):
                continue
            if self.seen[eng].get(k, 0) >= v:
                continue
            self.seen[eng][k] = v
            st.append(("wait", k, v))

    def op(self, eng, fn, reads=(), writes=(), extra=(), inc=True):
        self._waits(eng, reads, writes, extra)
        if inc:
            self.cnt[eng] += 1
            self.latest[eng] = self.cnt[eng]
        ticket = (eng, self.cnt[eng] if inc else self.cnt[eng] + 1)
        self.streams[eng].append(("op", fn, eng if inc else None))
        for d in reads:
            d.r.append(ticket)
            if len(d.r) > 24:
                d.r = _compact(d.r)
        for d in writes:
            d.w = ticket
            d.r = []
        return ticket

    def dma(self, q, out_ap, in_ap, reads=(), writes=(), extra=(), chan_dep=None, group=True, **kw):
        cd = chan_dep if chan_dep is not None else writes[0]
        ch = self.chan_of(cd)
        self._waits(q, reads, writes, extra, group_key=ch.key if group else None)
        ch.n += 1
        ticket = (ch.key, 16 * ch.n)
        self.latest[ch.key] = 16 * ch.n

        def fn(e, out_ap=out_ap, in_ap=in_ap, kw=kw):
            return e.dma_start(out=out_ap, in_=in_ap, **kw)

        self.streams[q].append(("dma", fn, ch.key))
        for d in reads:
            d.r.append(ticket)
        for d in writes:
            d.w = ticket
            d.r = []
        return ticket

    def barrier(self, engs=ENGS):
        for e in engs:
            st = self.streams[e]
            for k, v in self.latest.items():
                if v == 0 or self.seen[e].get(k, 0) >= v:
                    continue
                if k == e:
                    continue
                self.seen[e][k] = v
                st.append(("wait", k, v))

    def emit(self):
        nc = self.nc
        sems = self.sems
        streams = self.streams
        with nc.Block() as block:

            def run(engname, e):
                for item in streams[engname]:
                    if item[0] == "wait":
                        e.wait_ge(sems[item[1]], item[2])
                    elif item[0] == "op":
                        ins = item[1](e)
                        if item[2] is not None:
                            ins.then_inc(sems[item[2]], 1)
                    else:
                        ins = item[1](e)
                        ins.then_inc(sems[item[2]], 16)

            @block.tensor
            def _(e):
                run("pe", e)

            @block.scalar
            def _(e):
                run("act", e)

            @block.vector
            def _(e):
                run("dve", e)

            @block.gpsimd
            def _(e):
                run("pool", e)

            @block.sync
            def _(e):
                run("sp", e)


def _compact(tickets):
    best = {}
    for k, v in tickets:
        if best.get(k, 0) < v:
            best[k] = v
    return list(best.items())


FULL_CFG = dict(D=2048, T=2048, G=8, H=16, FF=8192, DIL=(1, 4, 16))


def build_program(cfg):
    D, T, G, H, FF, DIL = cfg["D"], cfg["T"], cfg["G"], cfg["H"], cfg["FF"], cfg["DIL"]
    KC = D // 128
    C = T
    TT = C + T
    NBLK = TT // 128
    NQ = TT // 512
    NQO = T // 512
    NH = 4 * G
    DSSM = G * 256
    DATT = H * 128
    DMIX = DSSM + DATT
    MKC = DMIX // 128
    FC = FF // 128
    CH = DSSM + 2 * G * 128
    OFF_Z, OFF_X, OFF_B = 0, DSSM, 2 * DSSM
    OFF_C = OFF_B + G * 128
    OFF_DT = OFF_C + G * 128
    OFF_Q = OFF_DT + NH
    OFF_K = OFF_Q + DATT
    OFF_V = OFF_K + DATT
    NIN = OFF_V + DATT
    assert 128 * DIL[2] == T and T % 512 == 0

    nc = bass.Bass("TRN2", target_bir_lowering=False)

    def din(name, shape, dt=F32):
        return nc.dram_tensor(name, list(shape), dt, kind="ExternalInput").ap()

    xc = din("xc", [TT, D])
    flag = din("flag", [128, 1])
    w_in = din("w_in", [D, NIN])
    cwb = din("cwb", [CH, 5])
    hv = din("hv", [3, NH])
    snw = din("snw", [1, DSSM])
    nw4 = din("nw4", [4, D])
    w_out = din("w_out", [DMIX, D])
    w_up = din("w_up", [D, FF])
    w_down = din("w_down", [FF, D])
    out = nc.dram_tensor("out", [T, D], F32, kind="ExternalOutput").ap()
    uT_d = nc.dram_tensor("uT_d", [D, TT], BF16, kind="Internal").ap()
    ymT_d = nc.dram_tensor("ymT_d", [DMIX, T], BF16, kind="Internal").ap()
    wout_b = nc.dram_tensor("wout_b", [DMIX, D], BF16, kind="Internal").ap()
    wup_b = nc.dram_tensor("wup_b", [D, FF], BF16, kind="Internal").ap()
    wdown_b = nc.dram_tensor("wdown_b", [FF, D], BF16, kind="Internal").ap()

    with ExitStack() as top:
        fw = FW(nc, top)
        op = fw.op
        banks = [fw.ps("bank%d" % i, [128, 512], F32) for i in range(8)]

        ident, identd = fw.sb("ident", [128, 128], BF16)
        tri, trid = fw.sb("tri", [128, 128], F32)
        onesf, onesfd = fw.sb("onesf", [128, 128], F32)
        onesb, onesbd = fw.sb("onesb", [128, 128], BF16)
        maskA, maskAd = fw.sb("maskA", [128, 2, 128], BF16)
        maskB, maskBd = fw.sb("maskB", [128, 2, 128], BF16)
        flg, flgd = fw.sb("flg", [128, 1], F32)
        sgt, sgtd = fw.sb("sgt", [128, 128], F32)
        tmpc, tmpcd = fw.sb("tmpc", [128, 128], F32)
        hvb, hvbd = fw.sb("hvb", [128, 3, NH], F32)
        p01 = top.enter_context(ExitStack())
        s_dt, s_dtd = fw.sb("s_dt", [128, NBLK, NH], F32, p01)
        s_dta, s_dtad = fw.sb("s_dta", [128, NBLK, NH], F32, p01)
        s_eacs, s_eacsd = fw.sb("s_eacs", [128, NBLK, NH], F32, p01)
        s_nacs, s_nacsd = fw.sb("s_nacs", [128, NBLK, NH], F32, p01)
        s_dtdte, s_dtdted = fw.sb("s_dtdte", [128, NBLK, NH], F32, p01)
        s_cd, s_cdd = fw.sb("s_cd", [128, NBLK, NH], F32, p01)

        fw.dma("sp", flg[:], flag[:, :], writes=[flgd])
        fw.dma("sp", hvb[:], hv.partition_broadcast(128), writes=[hvbd])
        op("pool", lambda e: e.memset(onesf[:], 1.0), writes=[onesfd])
        op("pool", lambda e: e.memset(onesb[:], 1.0), writes=[onesbd])
        op("pool", lambda e: e.affine_select(out=tri[:], in_=onesf[:], pattern=[[1, 128]], compare_op=ALU.is_ge, fill=0.0, base=0, channel_multiplier=-1), reads=[onesfd], writes=[trid])
        op("pool", lambda e: e.affine_select(out=sgt[:], in_=onesf[:], pattern=[[-1, 128]], compare_op=ALU.is_ge, fill=0.0, base=-1, channel_multiplier=1), reads=[onesfd], writes=[sgtd])
        op("pool", lambda e: e.affine_select(out=tmpc[:], in_=onesf[:], pattern=[[-1, 128]], compare_op=ALU.is_equal, fill=0.0, base=0, channel_multiplier=1), reads=[onesfd], writes=[tmpcd])
        op("dve", lambda e: e.tensor_copy(out=ident[:], in_=tmpc[:]), reads=[tmpcd], writes=[identd])
        op("pool", lambda e: e.affine_select(out=tmpc[:], in_=onesf[:], pattern=[[-1, 128]], compare_op=ALU.is_ge, fill=0.0, base=0, channel_multiplier=1), reads=[onesfd, identd], writes=[tmpcd])
        NEGB = 30000.0
        op("dve", lambda e: e.tensor_scalar(out=maskA[:, 0, :], in0=tmpc[:], scalar1=-1.0, scalar2=NEGB, op0=ALU.add, op1=ALU.mult), reads=[tmpcd], writes=[maskAd])
        op("dve", lambda e: e.tensor_scalar(out=tmpc[:], in0=tmpc[:], scalar1=flg[:, 0:1], scalar2=None, op0=ALU.mult), reads=[tmpcd, flgd], writes=[tmpcd])
        op("dve", lambda e: e.tensor_scalar(out=maskB[:, 0, :], in0=tmpc[:], scalar1=-1.0, scalar2=NEGB, op0=ALU.add, op1=ALU.mult), reads=[tmpcd], writes=[maskBd])
        op("dve", lambda e: e.tensor_scalar(out=maskA[:, 1, :], in0=tri[:], scalar1=-1.0, scalar2=NEGB, op0=ALU.add, op1=ALU.mult), reads=[trid], writes=[maskAd])
        op("dve", lambda e: e.tensor_scalar(out=maskB[:, 1, :], in0=tri[:], scalar1=-1.0, scalar2=NEGB, op0=ALU.add, op1=ALU.mult), reads=[trid], writes=[maskBd])
        op("act", lambda e: e.activation(out=hvb[:, 1, :], in_=hvb[:, 1, :], func=AF.Exp), reads=[hvbd], writes=[hvbd])
        op("dve", lambda e: e.tensor_scalar(out=hvb[:, 1, :], in0=hvb[:, 1, :], scalar1=-1.0, scalar2=None, op0=ALU.mult), reads=[hvbd], writes=[hvbd])

        NCAST = 4
        wcastds = [Dep("wcast%d" % i) for i in range(NCAST)]
        cast_list = []
        for (src, dst, rows, cols) in ((w_out, wout_b, DMIX, D), (w_up, wup_b, D, FF), (w_down, wdown_b, FF, D)):
            for r0 in range(0, rows, 128):
                for c0 in range(0, cols, 2048):
                    c1 = min(cols, c0 + 2048)
                    cast_list.append((dst[r0:r0 + 128, c0:c1], src[r0:r0 + 128, c0:c1]))
        cast_pos = [0]

        def cast_some(n):
            for _ in range(n):
                i = cast_pos[0]
                if i >= len(cast_list):
                    return
                cast_pos[0] += 1
                fw.dma("pool", cast_list[i][0], cast_list[i][1], writes=[wcastds[i % NCAST]], group=False)

        with ExitStack() as ph:
            nwt, nwtd = fw.sb("nwt", [128, D], F32, ph)
            wdt, wdtd = fw.sb("wdt", [128, KC, 128], BF16, ph)
            xr = [fw.sb("xr%d" % i, [128, D], F32, ph) for i in range(2)]
            ur = [fw.sb("ur%d" % i, [128, D], BF16, ph) for i in range(2)]
            junk, junkd = fw.sb("junk", [128, D], BF16, ph)
            uq = [fw.sb("uq%d" % i, [128, KC, 512], BF16, ph) for i in range(2)]
            st0, st0d = fw.sb("st0", [128, NBLK, 4], F32, ph)
            dtt = [fw.sb("dtt%d" % i, [128, 2, NH], F32, ph) for i in range(2)]
            fw.dma("sp", nwt[:], nw4[0, :].partition_broadcast(128), writes=[nwtd])
            fw.dma("pool", wdt[:], w_in[:, OFF_DT + NH - 128:OFF_DT + NH].rearrange("(kc p) n -> p kc n", p=128), writes=[wdtd])
            ptb = [banks[0], banks[1], banks[2], banks[3]]
            nper = 8 if KC >= 8 else KC
            for tb in range(NBLK):
                xt, xtd = xr[tb % 2]
                u, ud = ur[tb % 2]
                uqt, uqd = uq[(tb // 4) % 2]
                c4 = tb % 4
                fw.dma("sp", xt[:], xc[tb * 128:(tb + 1) * 128, :], writes=[xtd])
                op("act", lambda e, xt=xt, tb=tb: e.activation(out=junk[:], in_=xt[:], func=AF.Square, scale=float(D) ** -0.5, accum_out=st0[:, tb, 0:1]), reads=[xtd], writes=[junkd, st0d])
                op("act", lambda e, tb=tb: e.activation(out=st0[:, tb, 1:2], in_=st0[:, tb, 0:1], func=AF.Ln, bias=EPS), reads=[st0d], writes=[st0d])
                op("act", lambda e, tb=tb: e.activation(out=st0[:, tb, 2:3], in_=st0[:, tb, 1:2], func=AF.Exp, scale=-0.5), reads=[st0d], writes=[st0d])
                op("dve", lambda e, xt=xt, u=u, tb=tb: e.scalar_tensor_tensor(out=u[:], in0=xt[:], scalar=st0[:, tb, 2:3], in1=nwt[:], op0=ALU.mult, op1=ALU.mult), reads=[xtd, st0d, nwtd], writes=[ud])
                ngrp = (KC + nper - 1) // nper
                for gi in range(ngrp):
                    pb, pbd = ptb[(tb * ngrp + gi) % 4]
                    pbv = pb[:].bitcast(BF16)
                    n_in = min(nper, KC - gi * nper)
                    for j in range(n_in):
                        kc = gi * nper + j
                        op("pe", lambda e, pbv=pbv, u=u, kc=kc, j=j: e.transpose(out=pbv[:, j * 128:(j + 1) * 128], in_=u[:, kc * 128:(kc + 1) * 128], identity=ident[:]),
                           reads=[ud, identd], writes=[pbd], inc=(j == n_in - 1))
                    eng = "act" if gi % 2 == 0 else "dve"
                    src = pbv[:, 0:n_in * 128].rearrange("p (j t) -> p j t", t=128)
                    dst = uqt[:, gi * nper:gi * nper + n_in, c4 * 128:(c4 + 1) * 128]
                    if eng == "act":
                        op("act", lambda e, src=src, dst=dst: e.copy(out=dst, in_=src), reads=[pbd], writes=[uqd])
                    else:
                        op("dve", lambda e, src=src, dst=dst: e.tensor_copy(out=dst, in_=src), reads=[pbd], writes=[uqd])
                pd, pdd = banks[4 + tb % 2]
                dbg = cfg.get("dbg", 99)
                if dbg == 1:
                    if c4 == 3:
                        qd = tb // 4
                        fw.dma("sp", uT_d[:, qd * 512:(qd + 1) * 512].rearrange("(kc p) t -> p kc t", p=128), uqt[:], reads=[uqd], writes=[Dep("uTd")], chan_dep=uqd)
                    continue
                for kc in range(KC):
                    op("pe", lambda e, pd=pd, uqt=uqt, kc=kc, c4=c4: e.matmul(out=pd[:, 0:128], lhsT=uqt[:, kc, c4 * 128:(c4 + 1) * 128], rhs=wdt[:, kc, :], start=(kc == 0), stop=(kc == KC - 1)),
                       reads=[uqd, wdtd], writes=[pdd], inc=(kc == KC - 1))
                dt_, dtd_ = dtt[tb % 2]
                op("dve", lambda e, pd=pd, dt_=dt_: e.tensor_tensor(out=dt_[:, 0, :], in0=pd[:, 128 - NH:128], in1=hvb[:, 0, :], op=ALU.add), reads=[pdd, hvbd], writes=[dtd_])
                op("act", lambda e, dt_=dt_: e.activation(out=dt_[:, 0, :], in_=dt_[:, 0, :], func=AF.Exp), reads=[dtd_], writes=[dtd_])
                op("act", lambda e, dt_=dt_, tb=tb: e.activation(out=s_dt[:, tb, :], in_=dt_[:, 0, :], func=AF.Ln, bias=1.0), reads=[dtd_], writes=[s_dtd])
                op("dve", lambda e, tb=tb: e.tensor_tensor(out=s_dta[:, tb, :], in0=s_dt[:, tb, :], in1=hvb[:, 1, :], op=ALU.mult), reads=[s_dtd, hvbd], writes=[s_dtad])
                if dbg == 2:
                    if c4 == 3:
                        qd = tb // 4
                        fw.dma("sp", uT_d[:, qd * 512:(qd + 1) * 512].rearrange("(kc p) t -> p kc t", p=128), uqt[:], reads=[uqd], writes=[Dep("uTd")], chan_dep=uqd)
                    continue
                op("pe", lambda e, pd=pd, tb=tb: e.matmul(out=pd[:, 192:192 + NH], lhsT=tri[:], rhs=s_dta[:, tb, :], start=True, stop=True), reads=[trid, s_dtad], writes=[pdd], inc=False)
                op("pe", lambda e, pd=pd, tb=tb: e.matmul(out=pd[:, 256:256 + NH], lhsT=onesf[:], rhs=s_dta[:, tb, :], start=True, stop=True), reads=[onesfd, s_dtad], writes=[pdd])
                op("act", lambda e, pd=pd, tb=tb: e.activation(out=s_eacs[:, tb, :], in_=pd[:, 192:192 + NH], func=AF.Exp), reads=[], writes=[s_eacsd, pdd])
                op("act", lambda e, pd=pd, tb=tb: e.activation(out=s_cd[:, tb, :], in_=pd[:, 256:256 + NH], func=AF.Exp), reads=[], writes=[s_cdd, pdd])
                op("dve", lambda e, pd=pd, tb=tb: e.tensor_scalar(out=s_nacs[:, tb, :], in0=pd[:, 192:192 + NH], scalar1=-1.0, scalar2=None, op0=ALU.mult), reads=[], writes=[s_nacsd, pdd])
                op("dve", lambda e, pd=pd, dt_=dt_, tb=tb: e.tensor_tensor(out=dt_[:, 1, :], in0=pd[:, 256:256 + NH], in1=s_nacs[:, tb, :], op=ALU.add), reads=[s_nacsd], writes=[dtd_, pdd])
                op("act", lambda e, dt_=dt_: e.activation(out=dt_[:, 1, :], in_=dt_[:, 1, :], func=AF.Exp), reads=[dtd_], writes=[dtd_])
                op("dve", lambda e, dt_=dt_, tb=tb: e.tensor_tensor(out=s_dtdte[:, tb, :], in0=dt_[:, 1, :], in1=s_dt[:, tb, :], op=ALU.mult), reads=[dtd_, s_dtd], writes=[s_dtdted])
                if c4 == 3:
                    qd = tb // 4
                    fw.dma("sp", uT_d[:, qd * 512:(qd + 1) * 512].rearrange("(kc p) t -> p kc t", p=128), uqt[:], reads=[uqd], writes=[Dep("uTd")], chan_dep=uqd)
            fw.barrier()
        uT_ready = [(uq[i][1].chan.key, 16 * uq[i][1].chan.n) for i in range(2)]
        if cfg.get("stop") == 0:
            fw.emit()
            return nc

        with ExitStack() as ph:
            uring = [fw.sb("uring%d" % i, [128, KC, 512], BF16, ph) for i in range(3)]
            NW = 12
            wring = [fw.sb("wring%d" % i, [128, KC, 128], BF16, ph) for i in range(NW)]
            wctr = [0]
            uctr = [0]

            def load_w(col0):
                wt, wd = wring[wctr[0] % NW]
                wctr[0] += 1
                fw.dma("pool", wt[:], w_in[:, col0:col0 + 128].rearrange("(kc p) n -> p kc n", p=128), writes=[wd])
                return wt, wd

            unit_cols = []
            for g in range(G):
                unit_cols.append([OFF_Z + g * 256, OFF_Z + g * 256 + 128, OFF_X + g * 256, OFF_X + g * 256 + 128, OFF_B + g * 128, OFF_C + g * 128])
            for hd in range(H):
                unit_cols.append([OFF_Q + hd * 128, OFF_K + hd * 128, OFF_V + hd * 128])
            unit_w = {}

            def prefetch_unit(i):
                if i < len(unit_cols) and i not in unit_w:
                    unit_w[i] = [load_w(c) for c in unit_cols[i]]

            prefetch_unit(0)
            cast_per_quad = -(-len(cast_list) // max(1, (G + H // 2) * NQ))

            def load_u(qd):
                ut, utd = uring[uctr[0] % 3]
                uctr[0] += 1
                fw.dma("sp", ut[:], uT_d[:, qd * 512:(qd + 1) * 512].rearrange("(kc p) t -> p kc t", p=128), writes=[utd], extra=uT_ready)
                return ut, utd

            pjb = [banks[0], banks[1]]
            pjc = [0]

            def proj_fm(wt, wd, ut, utd):
                pb, pbd = pjb[pjc[0] % 2]
                pjc[0] += 1
                for kc in range(KC):
                    op("pe", lambda e, pb=pb, wt=wt, ut=ut, kc=kc: e.matmul(out=pb[:, :], lhsT=wt[:, kc, :], rhs=ut[:, kc, :], start=(kc == 0), stop=(kc == KC - 1)),
                       reads=[wd, utd], writes=[pbd], inc=(kc == KC - 1))
                return pb, pbd

            with ExitStack() as pa:
                assert G % 2 == 0
                snwt, snwtd = fw.sb("snwt", [128, DSSM], F32, pa)
                fw.dma("sp", snwt[:], snw[0, :].partition_broadcast(128), writes=[snwtd])
                TB = []
                for th in range(2):
                    n = lambda x, th=th: "%s_%d" % (x, th)
                    TB.append(dict(
                        cw=fw.sb(n("cw"), [128, 4, 5], F32, pa),
                        stage=[fw.sb(n("stage%d" % i), [128, 515], F32, pa) for i in range(2)],
                        acc=[fw.sb(n("acc%d" % i), [128, 512], F32, pa) for i in range(2)],
                        carry=fw.sb(n("carry"), [128, 4, 3], F32, pa),
                        xTq=[fw.sb(n("xTq%d" % i), [128, 2, 512], BF16, pa) for i in range(2)],
                        BTq=[fw.sb(n("BTq%d" % i), [128, 512], BF16, pa) for i in range(2)],
                        CTq=[fw.sb(n("CTq%d" % i), [128, 512], BF16, pa) for i in range(2)],
                        xB=fw.sb(n("xB"), [128, 384], BF16, pa),
                        xdte=fw.sb(n("xdte"), [128, 256], BF16, pa),
                        xdt=fw.sb(n("xdt"), [128, 256], BF16, pa),
                        cbm=fw.sb(n("cbm"), [128, 128], BF16, pa),
                        ldta=[fw.sb(n("ldta%d" % i), [128, 2, 128], F32, pa) for i in range(2)],
                        Lh=[fw.sb(n("Lh%d" % i), [128, 2, 128], F32, pa) for i in range(2)],
                        Mh=[fw.sb(n("Mh%d" % i), [128, 2, 128], BF16, pa) for i in range(2)],
                        S=fw.sb(n("S"), [128, 256], F32, pa),
                        Sb=fw.sb(n("Sb"), [128, 256], BF16, pa),
                        t1=fw.sb(n("t1"), [128, 256], F32, pa),
                        t2=fw.sb(n("t2"), [128, 256], F32, pa),
                        sz=fw.sb(n("sz"), [128, 256], F32, pa),
                        yj=fw.sb(n("yj"), [128, 256], BF16, pa),
                        yo=fw.sb(n("yo"), [128, 256], BF16, pa),
                        gst=fw.sb(n("gst"), [128, 4], F32, pa),
                        yT=fw.sb(n("yTs"), [128, 2, T], BF16, pa),
                    ))
                b_tr, b_trd = banks[2]
                b_cb, b_cbd = banks[2]
                b_ar, b_ard = banks[3]
                b_ys = [banks[4], banks[5]]
                b_z, b_zd = banks[6]
                b_st, b_std = banks[7]

                def run_rr(gens):
                    gens = list(gens)
                    while gens:
                        for gn in list(gens):
                            try:
                                next(gn)
                            except StopIteration:
                                gens.remove(gn)

                def ssd_quad(th, g, qd, ut, utd, uw, part):
                    Bf = TB[th]
                    cw, cwd = Bf["cw"]
                    stage, acc = Bf["stage"], Bf["acc"]
                    carry, carryd = Bf["carry"]
                    xTq, xTqd = Bf["xTq"][qd % 2]
                    BTq, BTqd = Bf["BTq"][qd % 2]
                    CTq, CTqd = Bf["CTq"][qd % 2]
                    xB, xBd = Bf["xB"]
                    xdte, xdted = Bf["xdte"]
                    xdt, xdtd = Bf["xdt"]
                    cbm, cbmd = Bf["cbm"]
                    ldta, Lh, Mh = Bf["ldta"], Bf["Lh"], Bf["Mh"]
                    S, Sd = Bf["S"]
                    Sb, Sbd = Bf["Sb"]
                    t1, t1d = Bf["t1"]
                    t2, t2d = Bf["t2"]
                    sz, szd = Bf["sz"]
                    yj, yjd = Bf["yj"]
                    yo, yod = Bf["yo"]
                    gst, gstd = Bf["gst"]
                    yT, yTd = Bf["yT"]
                    wz = [uw[0], uw[1]]
                    wx = [uw[2], uw[3]]
                    wB = uw[4]
                    wC = uw[5]
                    b_y, b_yd = b_ys[th]
                    sto = th * 256
                    own = qd >= NQ - NQO
                    chunks = [(0, wx[0], xTq[:, 0, :], xTqd), (1, wx[1], xTq[:, 1, :], xTqd), (2, wB, BTq[:, :], BTqd)]
                    if own or qd == NQ - NQO - 1:
                        chunks.append((3, wC, CTq[:, :], CTqd))
                    for ci, (wt, wd), dst, dstd in (chunks if part == "proj" else []):
                        pb, pbd = proj_fm(wt, wd, ut, utd)
                        sg, sgd = stage[ci % 2]
                        ac, acd = acc[ci % 2]
                        op("dve", lambda e, sg=sg, ci=ci: e.tensor_copy(out=sg[:, 0:3], in_=carry[:, ci, :]), reads=[carryd], writes=[sgd])
                        op("act", lambda e, sg=sg, pb=pb: e.copy(out=sg[:, 3:515], in_=pb[:, :]), reads=[pbd], writes=[sgd])
                        op("dve", lambda e, sg=sg, ci=ci: e.tensor_copy(out=carry[:, ci, :], in_=sg[:, 512:515]), reads=[sgd], writes=[carryd])
                        yield
                        if ci == 3 and not own:
                            continue
                        op("dve", lambda e, sg=sg, ac=ac, ci=ci: e.tensor_scalar(out=ac[:], in0=sg[:, 3:515], scalar1=cw[:, ci, 3:4], scalar2=cw[:, ci, 4:5], op0=ALU.mult, op1=ALU.add), reads=[sgd, cwd], writes=[acd])
                        for k in (2, 1, 0):
                            op("dve", lambda e, sg=sg, ac=ac, ci=ci, k=k: e.scalar_tensor_tensor(out=ac[:], in0=sg[:, k:k + 512], scalar=cw[:, ci, k:k + 1], in1=ac[:], op0=ALU.mult, op1=ALU.add), reads=[sgd, cwd, acd], writes=[acd])
                        op("act", lambda e, ac=ac, dst=dst: e.activation(out=dst, in_=ac[:], func=AF.Silu), reads=[acd], writes=[dstd])
                        yield
                    for c in (range(4) if part == "core" else []):
                        tb = qd * 4 + c
                        cs = slice(c * 128, (c + 1) * 128)
                        hs = slice(g * 4, g * 4 + 4)
                        trv = b_tr[:].bitcast(BF16)
                        if own:
                            for hp in range(2):
                                hh = g * 4 + 2 * hp
                                ld_, ldd_ = ldta[hp % 2]
                                L_, Ld_ = Lh[hp % 2]
                                ar = 2 * th * 128
                                op("pool", lambda e, ld_=ld_, tb=tb, hh=hh: e.tensor_tensor(out=ld_[:], in0=sgt[:].unsqueeze(1).to_broadcast([128, 2, 128]),
                                                                                 in1=s_dta[:, tb, hh:hh + 2].unsqueeze(2).to_broadcast([128, 2, 128]), op=ALU.mult), reads=[sgtd, s_dtad], writes=[ldd_])
                                for k2 in range(2):
                                    op("pe", lambda e, ld_=ld_, ar=ar, k2=k2: e.matmul(out=b_ar[:, ar + k2 * 128:ar + (k2 + 1) * 128], lhsT=ld_[:, k2, :], rhs=tri[:], start=True, stop=True), reads=[ldd_, trid], writes=[b_ard], inc=(k2 == 1))
                                yield
                                op("act", lambda e, L_=L_, ar=ar: e.activation(out=L_[:].rearrange("p a b -> p (a b)"), in_=b_ar[:, ar:ar + 256], func=AF.Exp), reads=[b_ard], writes=[Ld_])
                                yield
                        for i in range(2):
                            op("pe", lambda e, i=i, cs=cs, trv=trv: e.transpose(out=trv[:, i * 128:(i + 1) * 128], in_=xTq[:, i, cs], identity=ident[:]), reads=[xTqd, identd], writes=[b_trd], inc=False)
                        op("pe", lambda e, cs=cs, trv=trv: e.transpose(out=trv[:, 256:384], in_=BTq[:, cs], identity=ident[:]), reads=[BTqd, identd], writes=[b_trd])
                        op("act", lambda e, trv=trv: e.copy(out=xB[:], in_=trv[:, 0:384]), reads=[b_trd], writes=[xBd])
                        yield
                        if own:
                            for i in range(2):
                                for kc in range(KC):
                                    op("pe", lambda e, kc=kc, i=i, cs=cs: e.matmul(out=b_z[:, i * 128:(i + 1) * 128], lhsT=ut[:, kc, cs], rhs=wz[i][0][:, kc, :], start=(kc == 0), stop=(kc == KC - 1)),
                                       reads=[utd, wz[i][1]], writes=[b_zd], inc=(kc == KC - 1 and i == 1))
                            op("act", lambda e: e.activation(out=sz[:], in_=b_z[:, 0:256], func=AF.Silu), reads=[b_zd], writes=[szd])
                            yield
                        op("dve", lambda e, tb=tb, hs=hs: e.tensor_tensor(out=xdte[:].rearrange("p (h d) -> p h d", d=64), in0=xB[:, 0:256].rearrange("p (h d) -> p h d", d=64),
                                                                  in1=s_dtdte[:, tb, hs].unsqueeze(2).to_broadcast([128, 4, 64]), op=ALU.mult), reads=[xBd, s_dtdted], writes=[xdted])
                        op("pe", lambda e: e.matmul(out=b_st[:, sto:sto + 256], lhsT=xB[:, 256:384], rhs=xdte[:], start=True, stop=True), reads=[xBd, xdted], writes=[b_std])
                        yield
                        if own:
                            op("pe", lambda e, cs=cs: e.matmul(out=b_cb[:, 256:384], lhsT=BTq[:, cs], rhs=CTq[:, cs], start=True, stop=True), reads=[BTqd, CTqd], writes=[b_cbd])
                            op("dve", lambda e: e.tensor_tensor(out=cbm[:], in0=b_cb[:, 256:384], in1=tri[:], op=ALU.mult), reads=[b_cbd, trid], writes=[cbmd])
                            op("dve", lambda e, tb=tb, hs=hs: e.tensor_tensor(out=xdt[:].rearrange("p (h d) -> p h d", d=64), in0=xB[:, 0:256].rearrange("p (h d) -> p h d", d=64),
                                                                      in1=s_dt[:, tb, hs].unsqueeze(2).to_broadcast([128, 4, 64]), op=ALU.mult), reads=[xBd, s_dtd], writes=[xdtd])
                            op("pe", lambda e, cs=cs: e.matmul(out=b_y[:, 256:512], lhsT=CTq[:, cs], rhs=Sb[:], start=True, stop=True), reads=[CTqd, Sbd], writes=[b_yd])
                            yield
                        op("dve", lambda e, tb=tb, hs=hs: e.tensor_tensor(out=S[:].rearrange("p (h d) -> p h d", d=64), in0=S[:].rearrange("p (h d) -> p h d", d=64),
                                                                  in1=s_cd[:, tb, hs].unsqueeze(2).to_broadcast([128, 4, 64]), op=ALU.mult), reads=[Sd, s_cdd], writes=[Sd])
                        op("dve", lambda e: e.tensor_tensor(out=S[:], in0=b_st[:, sto:sto + 256], in1=S[:], op=ALU.add), reads=[b_std, Sd], writes=[Sd])
                        if tb == NBLK - T // 128 - 1:
                            op("dve", lambda e: e.tensor_scalar(out=S[:], in0=S[:], scalar1=flg[:, 0:1], scalar2=None, op0=ALU.mult), reads=[Sd, flgd], writes=[Sd])
                        op("act", lambda e: e.copy(out=Sb[:], in_=S[:]), reads=[Sd], writes=[Sbd])
                        yield
                        if own:
                            for hp in range(2):
                                L_, Ld_ = Lh[hp % 2]
                                M_, Md_ = Mh[hp % 2]
                                op("pool", lambda e, L_=L_, M_=M_: e.tensor_tensor(out=M_[:], in0=cbm[:].unsqueeze(1).to_broadcast([128, 2, 128]), in1=L_[:], op=ALU.mult), reads=[cbmd, Ld_], writes=[Md_])
                                yield
                                for k2 in range(2):
                                    h = 2 * hp + k2
                                    op("pe", lambda e, M_=M_, h=h, k2=k2: e.matmul(out=b_y[:, h * 64:(h + 1) * 64], lhsT=M_[:, k2, :], rhs=xdt[:, h * 64:(h + 1) * 64], start=True, stop=True), reads=[Md_, xdtd], writes=[b_yd], inc=(h == 3))
                            yield
                            op("pool", lambda e, hs=hs: e.tensor_tensor(out=t2[:].rearrange("p (h d) -> p h d", d=64), in0=xB[:, 0:256].rearrange("p (h d) -> p h d", d=64),
                                                                 in1=hvb[:, 2, hs].unsqueeze(2).to_broadcast([128, 4, 64]), op=ALU.mult), reads=[xBd, hvbd], writes=[t2d])
                            op("dve", lambda e, tb=tb, hs=hs: e.tensor_tensor(out=t1[:].rearrange("p (h d) -> p h d", d=64), in0=b_y[:, 256:512].rearrange("p (h d) -> p h d", d=64),
                                                                      in1=s_eacs[:, tb, hs].unsqueeze(2).to_broadcast([128, 4, 64]), op=ALU.mult), reads=[b_yd, s_eacsd], writes=[t1d])
                            op("dve", lambda e: e.tensor_tensor(out=t1[:], in0=b_y[:, 0:256], in1=t1[:], op=ALU.add), reads=[b_yd, t1d], writes=[t1d])
                            yield
                            op("pool", lambda e: e.tensor_tensor(out=t1[:], in0=t1[:], in1=t2[:], op=ALU.add), reads=[t1d, t2d], writes=[t1d])
                            op("dve", lambda e: e.tensor_tensor(out=t1[:], in0=t1[:], in1=sz[:], op=ALU.mult), reads=[t1d, szd], writes=[t1d])
                            yield
                            op("act", lambda e: e.activation(out=yj[:], in_=t1[:], func=AF.Square, scale=1.0 / 16.0, accum_out=gst[:, 0:1]), reads=[t1d], writes=[yjd, gstd])
                            op("act", lambda e: e.activation(out=gst[:, 1:2], in_=gst[:, 0:1], func=AF.Ln, bias=EPS), reads=[gstd], writes=[gstd])
                            op("act", lambda e: e.activation(out=gst[:, 2:3], in_=gst[:, 1:2], func=AF.Exp, scale=-0.5), reads=[gstd], writes=[gstd])
                            yield
                            op("dve", lambda e: e.scalar_tensor_tensor(out=yo[:], in0=t1[:], scalar=gst[:, 2:3], in1=snwt[:, g * 256:(g + 1) * 256], op0=ALU.mult, op1=ALU.mult), reads=[t1d, gstd, snwtd], writes=[yod])
                            zv = b_z[:].bitcast(BF16)
                            for i in range(2):
                                op("pe", lambda e, i=i, zv=zv: e.transpose(out=zv[:, 512 + i * 128:512 + (i + 1) * 128], in_=yo[:, i * 128:(i + 1) * 128], identity=ident[:]), reads=[yod, identd], writes=[b_zd], inc=(i == 1))
                            to = (qd - (NQ - NQO)) * 512 + c * 128
                            op("act", lambda e, zv=zv, to=to: e.copy(out=yT[:, :, to:to + 128], in_=zv[:, 512:768].rearrange("p (i t) -> p i t", t=128)), reads=[b_zd], writes=[yTd])
                            yield

                for g0 in range(0, G, 2):
                    grp = (g0, g0 + 1)
                    for th, g in enumerate(grp):
                        prefetch_unit(g)
                        cw, cwd = TB[th]["cw"]
                        for i, r0 in enumerate((g * 256, g * 256 + 128, DSSM + g * 128, DSSM + G * 128 + g * 128)):
                            fw.dma("sp", cw[:, i, :], cwb[r0:r0 + 128, :], writes=[cwd])
                        S, Sd = TB[th]["S"]
                        Sb, Sbd = TB[th]["Sb"]
                        carry, carryd = TB[th]["carry"]
                        op("dve", lambda e, S=S: e.memset(S[:], 0.0), writes=[Sd])
                        op("dve", lambda e, Sb=Sb: e.memset(Sb[:], 0.0), writes=[Sbd])
                        op("dve", lambda e, carry=carry: e.memset(carry[:], 0.0), writes=[carryd])
                    uts = {0: load_u(0)}
                    run_rr([ssd_quad(th, g, 0, uts[0][0], uts[0][1], unit_w[g], "proj") for th, g in enumerate(grp)])
                    for qd in range(NQ):
                        cast_some(2 * cast_per_quad)
                        gens = [ssd_quad(th, g, qd, uts[qd][0], uts[qd][1], unit_w[g], "core") for th, g in enumerate(grp)]
                        pgens = []
                        if qd + 1 < NQ:
                            uts[qd + 1] = load_u(qd + 1)
                            pgens = [ssd_quad(th, g, qd + 1, uts[qd + 1][0], uts[qd + 1][1], unit_w[g], "proj") for th, g in enumerate(grp)]
                        own_q = qd >= NQ - NQO
                        period = 7 if own_q else 2
                        rnd = 0
                        while gens or pgens:
                            for gn in list(gens):
                                try:
                                    next(gn)
                                except StopIteration:
                                    gens.remove(gn)
                            rnd += 1
                            if pgens and (rnd % period == 0 or not gens):
                                gn = pgens[(rnd // period) % len(pgens)] if gens else pgens[0]
                                try:
                                    next(gn)
                                except StopIteration:
                                    pgens.remove(gn)
                    for th, g in enumerate(grp):
                        yT, yTd = TB[th]["yT"]
                        fw.dma("sp", ymT_d[g * 256:(g + 1) * 256, :].rearrange("(i p) t -> p i t", p=128), yT[:], reads=[yTd], writes=[Dep("ym")], chan_dep=yTd)
                ym_ready = [(TB[i]["yT"][1].chan.key, 16 * TB[i]["yT"][1].chan.n) for i in range(2) if TB[i]["yT"][1].chan is not None]
                fw.barrier()
                if cfg.get("stop") == 1:
                    fw.emit()
                    return nc

            with ExitStack() as pa:
                KTs = [fw.sb("KT%d" % i, [128, TT], BF16, pa) for i in range(2)]
                VTs = [fw.sb("VT%d" % i, [128, TT], BF16, pa) for i in range(2)]
                QTs = [fw.sb("QT%d" % i, [128, T], BF16, pa) for i in range(2)]
                NVB = T // 128 + DIL[2]
                Vd_, Vdd_ = fw.sb("Vd", [128, NVB, 128], BF16, pa)
                aacc, aaccd = fw.sb("aacc", [128, 2, T], F32, pa)
                PT = [fw.sb("PT%d" % i, [128, 2, 128], BF16, pa) for i in range(2)]
                PM = [fw.sb("PM%d" % i, [128, 2, 128], BF16, pa) for i in range(3)]
                yA = [fw.sb("yA%d" % i, [128, T], BF16, pa) for i in range(2)]
                b_vt = [banks[2], banks[2]]
                b_s = [banks[3], banks[4], banks[5]]
                b_o = [banks[6], banks[7]]
                uc = [0]
                SKEW = 2

                def att_proj(hd):
                    prefetch_unit(G + hd)
                    prefetch_unit(G + hd + 1)
                    wq, wk, wv = unit_w[G + hd]
                    KT, KTd = KTs[hd % 2]
                    VT, VTd = VTs[hd % 2]
                    QT, QTd = QTs[hd % 2]
                    for qd in range(NQ):
                        own = qd >= NQ - NQO
                        ut, utd = load_u(qd)
                        cast_some(cast_per_quad)
                        ts_ = slice(qd * 512, (qd + 1) * 512)
                        pb, pbd = proj_fm(wk[0], wk[1], ut, utd)
                        op("act", lambda e, pb=pb, ts_=ts_: e.copy(out=KT[:, ts_], in_=pb[:, :]), reads=[pbd], writes=[KTd])
                        yield
                        pb, pbd = proj_fm(wv[0], wv[1], ut, utd)
                        op("dve", lambda e, pb=pb, ts_=ts_: e.tensor_copy(out=VT[:, ts_], in_=pb[:, :]), reads=[pbd], writes=[VTd])
                        yield
                        if own:
                            to = (qd - (NQ - NQO)) * 512
                            pb, pbd = proj_fm(wq[0], wq[1], ut, utd)
                            op("act", lambda e, pb=pb, to=to: e.activation(out=QT[:, to:to + 512], in_=pb[:, :], func=AF.Copy, scale=128.0 ** -0.5), reads=[pbd], writes=[QTd])
                            yield

                def att_units(hd):
                    KT, KTd = KTs[hd % 2]
                    VT, VTd = VTs[hd % 2]
                    QT, QTd = QTs[hd % 2]
                    first = True
                    for d in DIL:
                        nj = T // (128 * d)
                        nblk = d * (nj + 1)
                        for b0 in range(0, nblk, 4):
                            vb, vbd = b_vt[(b0 // 4) % 2]
                            vbv = vb[:].bitcast(BF16)
                            nb = min(4, nblk - b0)
                            for i in range(nb):
                                r, jj = divmod(b0 + i, nj + 1)
                                st_ = C + (jj - 1) * 128 * d + r
                                op("pe", lambda e, vbv=vbv, i=i, st_=st_, d=d: e.transpose(out=vbv[:, i * 128:(i + 1) * 128], in_=VT[:, st_:st_ + 127 * d + 1:d], identity=ident[:]), reads=[VTd, identd], writes=[vbd], inc=(i == nb - 1))
                            op("act", lambda e, vbv=vbv, b0=b0, nb=nb: e.copy(out=Vd_[:, b0:b0 + nb, :], in_=vbv[:, 0:nb * 128].rearrange("p (i t) -> p i t", t=128)), reads=[vbd], writes=[Vdd_])
                            yield
                        units = [(r, j) for r in range(d) for j in range(nj)]
                        info = {}

                        def stage_a(idx):
                            r, j = units[idx]
                            u_i = uc[0]
                            uc[0] += 1
                            bs, bsd = b_s[u_i % 3]
                            Pm_, Pmd_ = PM[u_i % 3]
                            q0 = j * 128 * d + r
                            qsl = slice(q0, q0 + 127 * d + 1, d)
                            kcur = slice(C + q0, C + q0 + 127 * d + 1, d)
                            kprev = slice(C + q0 - 128 * d, C + q0 - d + 1, d)
                            mk, mkd = (maskB, maskBd) if j == 0 else (maskA, maskAd)
                            op("pe", lambda e: e.matmul(out=bs[:, 0:256], lhsT=ident[:], rhs=mk[:].rearrange("p a b -> p (a b)"), start=True, stop=False), reads=[identd, mkd], writes=[bsd], inc=False)
                            op("pe", lambda e: e.matmul(out=bs[:, 0:128], lhsT=KT[:, kprev], rhs=QT[:, qsl], start=False, stop=False), reads=[KTd, QTd], writes=[bsd], inc=False)
                            op("pe", lambda e: e.matmul(out=bs[:, 128:256], lhsT=KT[:, kcur], rhs=QT[:, qsl], start=False, stop=True), reads=[KTd, QTd], writes=[bsd])
                            op("act", lambda e: e.activation(out=Pm_[:].rearrange("p a b -> p (a b)"), in_=bs[:, 0:256], func=AF.Exp), reads=[bsd], writes=[Pmd_])
                            info[idx] = (u_i, Pm_, Pmd_, qsl)

                        def stage_b(idx, first):
                            r, j = units[idx]
                            u_i, Pm_, Pmd_, qsl = info.pop(idx)
                            bo, bod = b_o[u_i % 2]
                            vi_prev = r * (nj + 1) + j
                            vi_cur = vi_prev + 1
                            op("pe", lambda e: e.matmul(out=bo[:, 0:128], lhsT=Vd_[:, vi_prev, :], rhs=Pm_[:, 0, :], start=True, stop=False), reads=[Vdd_, Pmd_], writes=[bod], inc=False)
                            op("pe", lambda e: e.matmul(out=bo[:, 0:128], lhsT=Vd_[:, vi_cur, :], rhs=Pm_[:, 1, :], start=False, stop=True), reads=[Vdd_, Pmd_], writes=[bod], inc=False)
                            op("pe", lambda e: e.matmul(out=bo[:, 128:256], lhsT=onesb[:], rhs=Pm_[:, 0, :], start=True, stop=False), reads=[onesbd, Pmd_], writes=[bod], inc=False)
                            op("pe", lambda e: e.matmul(out=bo[:, 128:256], lhsT=onesb[:], rhs=Pm_[:, 1, :], start=False, stop=True), reads=[onesbd, Pmd_], writes=[bod])
                            src = bo[:, 0:256].rearrange("p (a t) -> p a t", t=128)
                            if first:
                                op("dve", lambda e: e.tensor_copy(out=aacc[:, :, qsl], in_=src), reads=[bod], writes=[aaccd])
                            else:
                                op("dve", lambda e: e.tensor_tensor(out=aacc[:, :, qsl], in0=src, in1=aacc[:, :, qsl], op=ALU.add), reads=[bod, aaccd], writes=[aaccd])

                        n_u = len(units)
                        for idx in range(n_u + SKEW):
                            if idx < n_u:
                                stage_a(idx)
                            if idx >= SKEW:
                                stage_b(idx - SKEW, first)
                            yield
                        first = False
                    ya, yad = yA[hd % 2]
                    op("dve", lambda e: e.reciprocal(out=aacc[:, 1, :], in_=aacc[:, 1, :]), reads=[aaccd], writes=[aaccd])
                    op("dve", lambda e: e.tensor_tensor(out=ya[:], in0=aacc[:, 0, :], in1=aacc[:, 1, :], op=ALU.mult), reads=[aaccd], writes=[yad])
                    fw.dma("sp", ymT_d[DSSM + hd * 128:DSSM + (hd + 1) * 128, :], ya[:], reads=[yad], writes=[Dep("ym")], chan_dep=yad)
                    yield

                for _ in att_proj(0):
                    pass
                for hd in range(H):
                    gu = att_units(hd)
                    gp = att_proj(hd + 1) if hd + 1 < H else None
                    alive_u = True
                    while alive_u or gp is not None:
                        for _ in range(3):
                            if alive_u:
                                try:
                                    next(gu)
                                except StopIteration:
                                    alive_u = False
                        if gp is not None:
                            try:
                                next(gp)
                            except StopIteration:
                                gp = None
                ym_ready += [(yA[i][1].chan.key, 16 * yA[i][1].chan.n) for i in range(2) if yA[i][1].chan is not None]
                fw.barrier()
        cast_some(len(cast_list))
        wcast_ready = [(d_.chan.key, 16 * d_.chan.n) for d_ in wcastds if d_.chan is not None]
        p01.close()

        with ExitStack() as ph:
            nwpost, nwpostd = fw.sb("nwA", [128, D], F32, ph)
            nwpre, nwpred = fw.sb("nwB", [128, D], F32, ph)
            nwfin, nwfind = nwpost, nwpostd
            fw.dma("sp", nwpre[:], nw4[2, :].partition_broadcast(128), writes=[nwpred])
            NWP = 4
            PR = 8
            wp = [fw.sb("wp%d" % i, [128, PR, 512], BF16, ph) for i in range(NWP)]
            wpc = [0]
            big, bigd = fw.sb("big", [128, max(MKC, FC), 512], BF16, ph)
            bigs = [Dep("big%d" % i) for i in range(max(MKC, FC))]
            xh, xhd = fw.sb("xh", [128, 4, D], F32, ph)
            mf, mfd = fw.sb("mf", [128, 4, D], F32, ph)
            hn, hnd = fw.sb("hn", [128, D], BF16, ph)
            hnT, hnTd = fw.sb("hnT", [128, KC, 512], BF16, ph)
            rls = [fw.sb("rl%d" % i, [128, 512], F32, ph) for i in range(2)]
            st2, st2d = fw.sb("st2", [128, 4, 12], F32, ph)

            def load_wp(src, r0, nrow_chunks, c0, ncol):
                wt, wd = wp[wpc[0] % NWP]
                wpc[0] += 1
                fw.dma("sp", wt[:, 0:nrow_chunks, 0:ncol], src[r0:r0 + nrow_chunks * 128, c0:c0 + ncol].rearrange("(kc p) n -> p kc n", p=128), writes=[wd], extra=wcast_ready)
                return wt, wd

            NCB = D // 512 if D >= 512 else 1
            CBW = min(512, D)
            for qd in range(NQO):
                tq = slice(qd * 512, (qd + 1) * 512)
                for k0 in range(0, MKC, 16):
                    k1 = min(MKC, k0 + 16)
                    tk = fw.dma("sp", big[:, k0:k1, :], ymT_d[k0 * 128:k1 * 128, tq].rearrange("(kc p) t -> p kc t", p=128), writes=bigs[k0:k1], extra=ym_ready, chan_dep=bigd)
                for k in range(MKC):
                    bigs[k].w = tk
                fw.dma("sp", xh[:], xc[C + qd * 512:C + (qd + 1) * 512, :].rearrange("(a p) d -> p a d", p=128), writes=[xhd])
                for cb in range(NCB):
                    npiece = (MKC + PR - 1) // PR
                    for pi in range(npiece):
                        nr = min(PR, MKC - pi * PR)
                        wt, wd = load_wp(wout_b, pi * PR * 128, nr, cb * CBW, CBW)
                        for tb in range(4):
                            pb, pbd = banks[(cb % 2) * 4 + tb]
                            for k in range(nr):
                                kc = pi * PR + k
                                op("pe", lambda e, pb=pb, wt=wt, k=k, kc=kc, tb=tb: e.matmul(out=pb[:, 0:CBW], lhsT=big[:, kc, tb * 128:(tb + 1) * 128], rhs=wt[:, k, 0:CBW], start=(kc == 0), stop=(kc == MKC - 1)),
                                   reads=[bigs[kc], wd], writes=[pbd], inc=(k == nr - 1))
                    for tb in range(4):
                        pb, pbd = banks[(cb % 2) * 4 + tb]
                        if tb % 2 == 0:
                            op("act", lambda e, pb=pb, tb=tb, cb=cb: e.copy(out=mf[:, tb, cb * CBW:(cb + 1) * CBW], in_=pb[:, 0:CBW]), reads=[pbd], writes=[mfd])
                        else:
                            op("dve", lambda e, pb=pb, tb=tb, cb=cb: e.tensor_copy(out=mf[:, tb, cb * CBW:(cb + 1) * CBW], in_=pb[:, 0:CBW]), reads=[pbd], writes=[mfd])
                fw.dma("sp", nwpost[:], nw4[1, :].partition_broadcast(128), writes=[nwpostd])
                junk2 = big[:, 0:D // 512, :].rearrange("p a t -> p (a t)") if D >= 512 else hn[:]
                junk2ds = bigs[0:D // 512] if D >= 512 else [hnd]
                for tb in range(4):
                    op("act", lambda e, tb=tb: e.activation(out=junk2, in_=mf[:, tb, :], func=AF.Square, scale=float(D) ** -0.5, accum_out=st2[:, tb, 0:1]), reads=[mfd], writes=junk2ds + [st2d])
                    op("act", lambda e, tb=tb: e.activation(out=st2[:, tb, 1:2], in_=st2[:, tb, 0:1], func=AF.Ln, bias=EPS), reads=[st2d], writes=[st2d])
                    op("act", lambda e, tb=tb: e.activation(out=st2[:, tb, 2:3], in_=st2[:, tb, 1:2], func=AF.Exp, scale=-0.5), reads=[st2d], writes=[st2d])
                    op("dve", lambda e, tb=tb: e.scalar_tensor_tensor(out=mf[:, tb, :], in0=mf[:, tb, :], scalar=st2[:, tb, 2:3], in1=nwpost[:], op0=ALU.mult, op1=ALU.mult), reads=[mfd, st2d, nwpostd], writes=[mfd])
                    op("dve", lambda e, tb=tb: e.tensor_tensor(out=xh[:, tb, :], in0=xh[:, tb, :], in1=mf[:, tb, :], op=ALU.add), reads=[xhd, mfd], writes=[xhd])
                    op("act", lambda e, tb=tb: e.activation(out=junk2, in_=xh[:, tb, :], func=AF.Square, scale=float(D) ** -0.5, accum_out=st2[:, tb, 3:4]), reads=[xhd], writes=junk2ds + [st2d])
                    op("act", lambda e, tb=tb: e.activation(out=st2[:, tb, 4:5], in_=st2[:, tb, 3:4], func=AF.Ln, bias=EPS), reads=[st2d], writes=[st2d])
                    op("act", lambda e, tb=tb: e.activation(out=st2[:, tb, 5:6], in_=st2[:, tb, 4:5], func=AF.Exp, scale=-0.5), reads=[st2d], writes=[st2d])
                    op("dve", lambda e, tb=tb: e.scalar_tensor_tensor(out=hn[:], in0=xh[:, tb, :], scalar=st2[:, tb, 5:6], in1=nwpre[:], op0=ALU.mult, op1=ALU.mult), reads=[xhd, st2d, nwpred], writes=[hnd])
                    nper = 8 if KC >= 8 else KC
                    ngrp = (KC + nper - 1) // nper
                    for gi in range(ngrp):
                        pb, pbd = banks[(tb * ngrp + gi) % 8]
                        pbv = pb[:].bitcast(BF16)
                        n_in = min(nper, KC - gi * nper)
                        for j in range(n_in):
                            kc = gi * nper + j
                            op("pe", lambda e, pbv=pbv, kc=kc, j=j: e.transpose(out=pbv[:, j * 128:(j + 1) * 128], in_=hn[:, kc * 128:(kc + 1) * 128], identity=ident[:]), reads=[hnd, identd], writes=[pbd], inc=(j == n_in - 1))
                        op("act", lambda e, pbv=pbv, gi=gi, n_in=n_in, tb=tb: e.copy(out=hnT[:, gi * nper:gi * nper + n_in, tb * 128:(tb + 1) * 128], in_=pbv[:, 0:n_in * 128].rearrange("p (j t) -> p j t", t=128)), reads=[pbd], writes=[hnTd])
                nfc_per = 4
                for f0 in range(0, FC, nfc_per):
                    nf = min(nfc_per, FC - f0)
                    halves = []
                    for k0 in range(0, KC, PR):
                        nr = min(PR, KC - k0)
                        halves.append((load_wp(wup_b, k0 * 128, nr, f0 * 128, nf * 128), k0, nr))
                    for fi in range(nf):
                        fc = f0 + fi
                        pb, pbd = banks[fc % 8]
                        for (wt, wd), k0, nr in halves:
                            for k in range(nr):
                                kc = k0 + k
                                op("pe", lambda e, pb=pb, wt=wt, fi=fi, k=k, kc=kc: e.matmul(out=pb[:, :], lhsT=wt[:, k, fi * 128:(fi + 1) * 128], rhs=hnT[:, kc, :], start=(kc == 0), stop=(kc == KC - 1)),
                                   reads=[wd, hnTd], writes=[pbd], inc=(kc == KC - 1))
                        rl, rld = rls[fc % 2]
                        op("act", lambda e, pb=pb, rl=rl: e.activation(out=rl[:], in_=pb[:, :], func=AF.Relu), reads=[pbd], writes=[rld])
                        op("dve", lambda e, fc=fc, rl=rl: e.tensor_tensor(out=big[:, fc, :], in0=rl[:], in1=rl[:], op=ALU.mult), reads=[rld], writes=[bigs[fc]])
                for cb in range(NCB):
                    npiece = (FC + PR - 1) // PR
                    for pi in range(npiece):
                        nr = min(PR, FC - pi * PR)
                        wt, wd = load_wp(wdown_b, pi * PR * 128, nr, cb * CBW, CBW)
                        for tb in range(4):
                            pb, pbd = banks[(cb % 2) * 4 + tb]
                            for k in range(nr):
                                fc = pi * PR + k
                                op("pe", lambda e, pb=pb, wt=wt, k=k, fc=fc, tb=tb: e.matmul(out=pb[:, 0:CBW], lhsT=big[:, fc, tb * 128:(tb + 1) * 128], rhs=wt[:, k, 0:CBW], start=(fc == 0), stop=(fc == FC - 1)),
                                   reads=[bigs[fc], wd], writes=[pbd], inc=(k == nr - 1))
                    for tb in range(4):
                        pb, pbd = banks[(cb % 2) * 4 + tb]
                        if tb % 2 == 0:
                            op("act", lambda e, pb=pb, tb=tb, cb=cb: e.copy(out=mf[:, tb, cb * CBW:(cb + 1) * CBW], in_=pb[:, 0:CBW]), reads=[pbd], writes=[mfd])
                        else:
                            op("dve", lambda e, pb=pb, tb=tb, cb=cb: e.tensor_copy(out=mf[:, tb, cb * CBW:(cb + 1) * CBW], in_=pb[:, 0:CBW]), reads=[pbd], writes=[mfd])
                fw.dma("sp", nwfin[:], nw4[3, :].partition_broadcast(128), writes=[nwfind])
                for tb in range(4):
                    op("act", lambda e, tb=tb: e.activation(out=junk2, in_=mf[:, tb, :], func=AF.Square, scale=float(D) ** -0.5, accum_out=st2[:, tb, 6:7]), reads=[mfd], writes=junk2ds + [st2d])
                    op("act", lambda e, tb=tb: e.activation(out=st2[:, tb, 7:8], in_=st2[:, tb, 6:7], func=AF.Ln, bias=EPS), reads=[st2d], writes=[st2d])
                    op("act", lambda e, tb=tb: e.activation(out=st2[:, tb, 8:9], in_=st2[:, tb, 7:8], func=AF.Exp, scale=-0.5), reads=[st2d], writes=[st2d])
                    op("dve", lambda e, tb=tb: e.scalar_tensor_tensor(out=mf[:, tb, :], in0=mf[:, tb, :], scalar=st2[:, tb, 8:9], in1=nwfin[:], op0=ALU.mult, op1=ALU.mult), reads=[mfd, st2d, nwfind], writes=[mfd])
                    op("dve", lambda e, tb=tb: e.tensor_tensor(out=mf[:, tb, :], in0=xh[:, tb, :], in1=mf[:, tb, :], op=ALU.add), reads=[xhd, mfd], writes=[mfd])
                fw.dma("sp", out[tq, :].rearrange("(a p) d -> p a d", p=128), mf[:], reads=[mfd], writes=[Dep("out")], chan_dep=mfd)
            fw.barrier()
        fw.emit()
    return nc


_CACHE = {}


def make_in_maps(cfg, ncores, inputs):
    D, T, G = cfg["D"], cfg["T"], cfg["G"]
    x = np.asarray(inputs["x"], np.float32)
    B_, S_, _ = x.shape
    halves = S_ // T
    assert halves == 2 and B_ * halves == ncores
    shared = {
        "w_in": np.ascontiguousarray(np.asarray(inputs["w_in"], np.float32)[0]),
        "cwb": np.ascontiguousarray(np.concatenate([np.asarray(inputs["conv_w"], np.float32)[0].T, np.asarray(inputs["conv_b"], np.float32)[0][:, None]], axis=1)),
        "hv": np.ascontiguousarray(np.stack([np.asarray(inputs["dt_bias"], np.float32)[0], np.asarray(inputs["a_log"], np.float32)[0], np.asarray(inputs["d_skip"], np.float32)[0]], axis=0)),
        "snw": np.ascontiguousarray(np.asarray(inputs["ssm_norm_w"], np.float32)),
        "nw4": np.ascontiguousarray(np.stack([np.asarray(inputs[k], np.float32)[0] for k in ("norm_mix_pre", "norm_mix_post", "norm_mlp_pre", "norm_mlp_post")], axis=0)),
        "w_out": np.ascontiguousarray(np.asarray(inputs["w_out"], np.float32)[0]),
        "w_up": np.ascontiguousarray(np.asarray(inputs["w_up"], np.float32)[0]),
        "w_down": np.ascontiguousarray(np.asarray(inputs["w_down"], np.float32)[0]),
    }
    maps = []
    for c in range(ncores):
        b, h = divmod(c, 2)
        if h == 0:
            xcc = np.concatenate([np.zeros((T, D), np.float32), x[b, :T]], axis=0)
            fl = np.zeros((128, 1), np.float32)
        else:
            xcc = x[b]
            fl = np.ones((128, 1), np.float32)
        m = dict(shared)
        m["xc"] = np.ascontiguousarray(xcc)
        m["flag"] = fl
        maps.append(m)
    return maps


def kernel(**inputs):
    cfg = FULL_CFG
    if "nc" not in _CACHE:
        _CACHE["nc"] = build_program(cfg)
    nc = _CACHE["nc"]
    maps = make_in_maps(cfg, 8, inputs)
    res = run_bass_kernel_spmd(nc, maps, core_ids=list(range(8)))
    x = inputs["x"]
    B_, S_, D = x.shape
    T = cfg["T"]
    outp = np.empty((B_, S_, D), np.float32)
    for c in range(8):
        b, h = divmod(c, 2)
        outp[b, h * T:(h + 1) * T] = res.results[c]["out"]
    return outp
```

```python
from contextlib import ExitStack
import numpy as np
import concourse.bass as bass
import concourse.mybir as mybir
from concourse.bass_utils import run_bass_kernel_spmd

F32 = mybir.dt.float32
BF16 = mybir.dt.bfloat16
AF = mybir.ActivationFunctionType
ALU = mybir.AluOpType

ENGS = ("pe", "act", "dve", "pool", "sp")
EPS = 1e-6


class Dep:
    __slots__ = ("w", "r", "chan", "name")

    def __init__(self, name=""):
        self.w = None
        self.r = []
        self.chan = None
        self.name = name


class Chan:
    __slots__ = ("key", "sem", "n")


class FW:
    def __init__(self, nc, stack, same_engine_sync=True):
        self.nc = nc
        self.stack = stack
        self.streams = {e: [] for e in ENGS}
        self.cnt = {e: 0 for e in ENGS}
        self.seen = {e: {} for e in ENGS}
        self.sems = {}
        self.latest = {}
        self.nchan = 0
        self.same = same_engine_sync
        for e in ENGS:
            self.sems[e] = stack.enter_context(nc.semaphore("s_" + e))
            self.latest[e] = 0

    def sb(self, name, shape, dtype, stack=None):
        t = (stack or self.stack).enter_context(self.nc.sbuf_tensor(name, list(shape), dtype))
        return t, Dep(name)

    def ps(self, name, shape, dtype=F32, stack=None):
        t = (stack or self.stack).enter_context(self.nc.psum_tensor(name, list(shape), dtype))
        return t, Dep(name)

    def chan_of(self, dep):
        if dep.chan is None:
            c = Chan()
            c.key = "c%d" % self.nchan
            self.nchan += 1
            c.sem = self.stack.enter_context(self.nc.semaphore("d_" + c.key))
            c.n = 0
            self.sems[c.key] = c.sem
            self.latest[c.key] = 0
            dep.chan = c
        return dep.chan

    def _waits(self, eng, reads, writes, extra, group_key=None):
        need = {}

        def add(t):
            if t is None:
                return
            k, v = t
            if need.get(k, 0) < v:
                need[k] = v

        for d in reads:
            add(d.w)
        for d in writes:
            if not (group_key is not None and d.w is not None and d.w[0] == group_key):
                add(d.w)
            for t in d.r:
                add(t)
        for t in extra:
            add(t)
        st = self.streams[eng]
        for k, v in need.items():
            if k == eng and (eng == "pe" or not self.same):
                continue
            if self.seen[eng].get(k, 0) >= v:
                continue
            self.seen[eng][k] = v
            st.append(("wait", k, v))

    def op(self, eng, fn, reads=(), writes=(), extra=(), inc=True):
        self._waits(eng, reads, writes, extra)
        if inc:
            self.cnt[eng] += 1
            self.latest[eng] = self.cnt[eng]
        ticket = (eng, self.cnt[eng] if inc else self.cnt[eng] + 1)
        self.streams[eng].append(("op", fn, eng if inc else None))
        for d in reads:
            d.r.append(ticket)
            if len(d.r) > 24:
                d.r = _compact(d.r)
        for d in writes:
            d.w = ticket
            d.r = []
        return ticket

    def dma(self, q, out_ap, in_ap, reads=(), writes=(), extra=(), chan_dep=None, group=True, **kw):
        cd = chan_dep if chan_dep is not None else writes[0]
        ch = self.chan_of(cd)
        self._waits(q, reads, writes, extra, group_key=ch.key if group else None)
        ch.n += 1
        ticket = (ch.key, 16 * ch.n)
        self.latest[ch.key] = 16 * ch.n

        def fn(e, out_ap=out_ap, in_ap=in_ap, kw=kw):
            return e.dma_start(out=out_ap, in_=in_ap, **kw)

        self.streams[q].append(("dma", fn, ch.key))
        for d in reads:
            d.r.append(ticket)
        for d in writes:
            d.w = ticket
            d.r = []
        return ticket

    def barrier(self, engs=ENGS):
        for e in engs:
            st = self.streams[e]
            for k, v in self.latest.items():
                if v == 0 or self.seen[e].get(k, 0) >= v:
                    continue
                if k == e:
                    continue
                self.seen[e][k] = v
                st.append(("wait", k, v))

    def emit(self):
        nc = self.nc
        sems = self.sems
        streams = self.streams
        with nc.Block() as block:

            def run(engname, e):
                for item in streams[engname]:
                    if item[0] == "wait":
                        e.wait_ge(sems[item[1]], item[2])
                    elif item[0] == "op":
                        ins = item[1](e)
                        if item[2] is not None:
                            ins.then_inc(sems[item[2]], 1)
                    else:
                        ins = item[1](e)
                        ins.then_inc(sems[item[2]], 16)

            @block.tensor
            def _(e):
                run("pe", e)

            @block.scalar
            def _(e):
                run("act", e)

            @block.vector
            def _(e):
                run("dve", e)

            @block.gpsimd
            def _(e):
                run("pool", e)

            @block.sync
            def _(e):
                run("sp", e)


def _compact(tickets):
    best = {}
    for k, v in tickets:
        if best.get(k, 0) < v:
            best[k] = v
    return list(best.items())


FULL_CFG = dict(D=2048, T=2048, G=8, H=16, FF=8192, DIL=(1, 4, 16))


def build_program(cfg):
    D, T, G, H, FF, DIL = cfg["D"], cfg["T"], cfg["G"], cfg["H"], cfg["FF"], cfg["DIL"]
    KC = D // 128
    C = T
    TT = C + T
    NBLK = TT // 128
    NQ = TT // 512
    NQO = T // 512
    NH = 4 * G
    DSSM = G * 256
    DATT = H * 128
    DMIX = DSSM + DATT
    MKC = DMIX // 128
    FC = FF // 128
    CH = DSSM + 2 * G * 128
    OFF_Z, OFF_X, OFF_B = 0, DSSM, 2 * DSSM
    OFF_C = OFF_B + G * 128
    OFF_DT = OFF_C + G * 128
    OFF_Q = OFF_DT + NH
    OFF_K = OFF_Q + DATT
    OFF_V = OFF_K + DATT
    NIN = OFF_V + DATT
    assert 128 * DIL[2] == T and T % 512 == 0

    nc = bass.Bass("TRN2", target_bir_lowering=False)

    def din(name, shape, dt=F32):
        return nc.dram_tensor(name, list(shape), dt, kind="ExternalInput").ap()

    xc = din("xc", [TT, D])
    flag = din("flag", [128, 1])
    w_in = din("w_in", [D, NIN])
    cwb = din("cwb", [CH, 5])
    hv = din("hv", [3, NH])
    snw = din("snw", [1, DSSM])
    nw4 = din("nw4", [4, D])
    w_out = din("w_out", [DMIX, D])
    w_up = din("w_up", [D, FF])
    w_down = din("w_down", [FF, D])
    out = nc.dram_tensor("out", [T, D], F32, kind="ExternalOutput").ap()
    uT_d = nc.dram_tensor("uT_d", [D, TT], BF16, kind="Internal").ap()
    ymT_d = nc.dram_tensor("ymT_d", [DMIX, T], BF16, kind="Internal").ap()
    wout_b = nc.dram_tensor("wout_b", [DMIX, D], BF16, kind="Internal").ap()
    wup_b = nc.dram_tensor("wup_b", [D, FF], BF16, kind="Internal").ap()
    wdown_b = nc.dram_tensor("wdown_b", [FF, D], BF16, kind="Internal").ap()

    with ExitStack() as top:
        fw = FW(nc, top)
        op = fw.op
        banks = [fw.ps("bank%d" % i, [128, 512], F32) for i in range(8)]

        ident, identd = fw.sb("ident", [128, 128], BF16)
        tri, trid = fw.sb("tri", [128, 128], F32)
        onesf, onesfd = fw.sb("onesf", [128, 128], F32)
        onesb, onesbd = fw.sb("onesb", [128, 128], BF16)
        maskA, maskAd = fw.sb("maskA", [128, 2, 128], BF16)
        maskB, maskBd = fw.sb("maskB", [128, 2, 128], BF16)
        flg, flgd = fw.sb("flg", [128, 1], F32)
        sgt, sgtd = fw.sb("sgt", [128, 128], F32)
        tmpc, tmpcd = fw.sb("tmpc", [128, 128], F32)
        hvb, hvbd = fw.sb("hvb", [128, 3, NH], F32)
        p01 = top.enter_context(ExitStack())
        s_dt, s_dtd = fw.sb("s_dt", [128, NBLK, NH], F32, p01)
        s_dta, s_dtad = fw.sb("s_dta", [128, NBLK, NH], F32, p01)
        s_eacs, s_eacsd = fw.sb("s_eacs", [128, NBLK, NH], F32, p01)
        s_nacs, s_nacsd = fw.sb("s_nacs", [128, NBLK, NH], F32, p01)
        s_dtdte, s_dtdted = fw.sb("s_dtdte", [128, NBLK, NH], F32, p01)
        s_cd, s_cdd = fw.sb("s_cd", [128, NBLK, NH], F32, p01)

        fw.dma("sp", flg[:], flag[:, :], writes=[flgd])
        fw.dma("sp", hvb[:], hv.partition_broadcast(128), writes=[hvbd])
        op("pool", lambda e: e.memset(onesf[:], 1.0), writes=[onesfd])
        op("pool", lambda e: e.memset(onesb[:], 1.0), writes=[onesbd])
        op("pool", lambda e: e.affine_select(out=tri[:], in_=onesf[:], pattern=[[1, 128]], compare_op=ALU.is_ge, fill=0.0, base=0, channel_multiplier=-1), reads=[onesfd], writes=[trid])
        op("pool", lambda e: e.affine_select(out=sgt[:], in_=onesf[:], pattern=[[-1, 128]], compare_op=ALU.is_ge, fill=0.0, base=-1, channel_multiplier=1), reads=[onesfd], writes=[sgtd])
        op("pool", lambda e: e.affine_select(out=tmpc[:], in_=onesf[:], pattern=[[-1, 128]], compare_op=ALU.is_equal, fill=0.0, base=0, channel_multiplier=1), reads=[onesfd], writes=[tmpcd])
        op("dve", lambda e: e.tensor_copy(out=ident[:], in_=tmpc[:]), reads=[tmpcd], writes=[identd])
        op("pool", lambda e: e.affine_select(out=tmpc[:], in_=onesf[:], pattern=[[-1, 128]], compare_op=ALU.is_ge, fill=0.0, base=0, channel_multiplier=1), reads=[onesfd, identd], writes=[tmpcd])
        NEGB = 30000.0
        op("dve", lambda e: e.tensor_scalar(out=maskA[:, 0, :], in0=tmpc[:], scalar1=-1.0, scalar2=NEGB, op0=ALU.add, op1=ALU.mult), reads=[tmpcd], writes=[maskAd])
        op("dve", lambda e: e.tensor_scalar(out=tmpc[:], in0=tmpc[:], scalar1=flg[:, 0:1], scalar2=None, op0=ALU.mult), reads=[tmpcd, flgd], writes=[tmpcd])
        op("dve", lambda e: e.tensor_scalar(out=maskB[:, 0, :], in0=tmpc[:], scalar1=-1.0, scalar2=NEGB, op0=ALU.add, op1=ALU.mult), reads=[tmpcd], writes=[maskBd])
        op("dve", lambda e: e.tensor_scalar(out=maskA[:, 1, :], in0=tri[:], scalar1=-1.0, scalar2=NEGB, op0=ALU.add, op1=ALU.mult), reads=[trid], writes=[maskAd])
        op("dve", lambda e: e.tensor_scalar(out=maskB[:, 1, :], in0=tri[:], scalar1=-1.0, scalar2=NEGB, op0=ALU.add, op1=ALU.mult), reads=[trid], writes=[maskBd])
        op("act", lambda e: e.activation(out=hvb[:, 1, :], in_=hvb[:, 1, :], func=AF.Exp), reads=[hvbd], writes=[hvbd])
        op("dve", lambda e: e.tensor_scalar(out=hvb[:, 1, :], in0=hvb[:, 1, :], scalar1=-1.0, scalar2=None, op0=ALU.mult), reads=[hvbd], writes=[hvbd])

        NCAST = 4
        wcastds = [Dep("wcast%d" % i) for i in range(NCAST)]
        cast_list = []
        for (src, dst, rows, cols) in ((w_out, wout_b, DMIX, D), (w_up, wup_b, D, FF), (w_down, wdown_b, FF, D)):
            for r0 in range(0, rows, 128):
                for c0 in range(0, cols, 2048):
                    c1 = min(cols, c0 + 2048)
                    cast_list.append((dst[r0:r0 + 128, c0:c1], src[r0:r0 + 128, c0:c1]))
        cast_pos = [0]

        def cast_some(n):
            for _ in range(n):
                i = cast_pos[0]
                if i >= len(cast_list):
                    return
                cast_pos[0] += 1
                fw.dma("pool", cast_list[i][0], cast_list[i][1], writes=[wcastds[i % NCAST]], group=False)

        with ExitStack() as ph:
            nwt, nwtd = fw.sb("nwt", [128, D], F32, ph)
            wdt, wdtd = fw.sb("wdt", [128, KC, 128], BF16, ph)
            xr = [fw.sb("xr%d" % i, [128, D], F32, ph) for i in range(2)]
            ur = [fw.sb("ur%d" % i, [128, D], BF16, ph) for i in range(2)]
            junk, junkd = fw.sb("junk", [128, D], BF16, ph)
            uq = [fw.sb("uq%d" % i, [128, KC, 512], BF16, ph) for i in range(2)]
            st0, st0d = fw.sb("st0", [128, NBLK, 4], F32, ph)
            dtt = [fw.sb("dtt%d" % i, [128, 2, NH], F32, ph) for i in range(2)]
            fw.dma("sp", nwt[:], nw4[0, :].partition_broadcast(128), writes=[nwtd])
            fw.dma("pool", wdt[:], w_in[:, OFF_DT + NH - 128:OFF_DT + NH].rearrange("(kc p) n -> p kc n", p=128), writes=[wdtd])
            ptb = [banks[0], banks[1], banks[2], banks[3]]
            nper = 8 if KC >= 8 else KC
            for tb in range(NBLK):
                xt, xtd = xr[tb % 2]
                u, ud = ur[tb % 2]
                uqt, uqd = uq[(tb // 4) % 2]
                c4 = tb % 4
                fw.dma("sp", xt[:], xc[tb * 128:(tb + 1) * 128, :], writes=[xtd])
                op("act", lambda e, xt=xt, tb=tb: e.activation(out=junk[:], in_=xt[:], func=AF.Square, scale=float(D) ** -0.5, accum_out=st0[:, tb, 0:1]), reads=[xtd], writes=[junkd, st0d])
                op("act", lambda e, tb=tb: e.activation(out=st0[:, tb, 1:2], in_=st0[:, tb, 0:1], func=AF.Ln, bias=EPS), reads=[st0d], writes=[st0d])
                op("act", lambda e, tb=tb: e.activation(out=st0[:, tb, 2:3], in_=st0[:, tb, 1:2], func=AF.Exp, scale=-0.5), reads=[st0d], writes=[st0d])
                op("dve", lambda e, xt=xt, u=u, tb=tb: e.scalar_tensor_tensor(out=u[:], in0=xt[:], scalar=st0[:, tb, 2:3], in1=nwt[:], op0=ALU.mult, op1=ALU.mult), reads=[xtd, st0d, nwtd], writes=[ud])
                ngrp = (KC + nper - 1) // nper
                for gi in range(ngrp):
                    pb, pbd = ptb[(tb * ngrp + gi) % 4]
                    pbv = pb[:].bitcast(BF16)
                    n_in = min(nper, KC - gi * nper)
                    for j in range(n_in):
                        kc = gi * nper + j
                        op("pe", lambda e, pbv=pbv, u=u, kc=kc, j=j: e.transpose(out=pbv[:, j * 128:(j + 1) * 128], in_=u[:, kc * 128:(kc + 1) * 128], identity=ident[:]),
                           reads=[ud, identd], writes=[pbd], inc=(j == n_in - 1))
                    eng = "act" if gi % 2 == 0 else "dve"
                    src = pbv[:, 0:n_in * 128].rearrange("p (j t) -> p j t", t=128)
                    dst = uqt[:, gi * nper:gi * nper + n_in, c4 * 128:(c4 + 1) * 128]
                    if eng == "act":
                        op("act", lambda e, src=src, dst=dst: e.copy(out=dst, in_=src), reads=[pbd], writes=[uqd])
                    else:
                        op("dve", lambda e, src=src, dst=dst: e.tensor_copy(out=dst, in_=src), reads=[pbd], writes=[uqd])
                pd, pdd = banks[4 + tb % 2]
                dbg = cfg.get("dbg", 99)
                if dbg == 1:
                    if c4 == 3:
                        qd = tb // 4
                        fw.dma("sp", uT_d[:, qd * 512:(qd + 1) * 512].rearrange("(kc p) t -> p kc t", p=128), uqt[:], reads=[uqd], writes=[Dep("uTd")], chan_dep=uqd)
                    continue
                for kc in range(KC):
                    op("pe", lambda e, pd=pd, uqt=uqt, kc=kc, c4=c4: e.matmul(out=pd[:, 0:128], lhsT=uqt[:, kc, c4 * 128:(c4 + 1) * 128], rhs=wdt[:, kc, :], start=(kc == 0), stop=(kc == KC - 1)),
                       reads=[uqd, wdtd], writes=[pdd], inc=(kc == KC - 1))
                dt_, dtd_ = dtt[tb % 2]
                op("dve", lambda e, pd=pd, dt_=dt_: e.tensor_tensor(out=dt_[:, 0, :], in0=pd[:, 128 - NH:128], in1=hvb[:, 0, :], op=ALU.add), reads=[pdd, hvbd], writes=[dtd_])
                op("act", lambda e, dt_=dt_: e.activation(out=dt_[:, 0, :], in_=dt_[:, 0, :], func=AF.Exp), reads=[dtd_], writes=[dtd_])
                op("act", lambda e, dt_=dt_, tb=tb: e.activation(out=s_dt[:, tb, :], in_=dt_[:, 0, :], func=AF.Ln, bias=1.0), reads=[dtd_], writes=[s_dtd])
                op("dve", lambda e, tb=tb: e.tensor_tensor(out=s_dta[:, tb, :], in0=s_dt[:, tb, :], in1=hvb[:, 1, :], op=ALU.mult), reads=[s_dtd, hvbd], writes=[s_dtad])
                if dbg == 2:
                    if c4 == 3:
                        qd = tb // 4
                        fw.dma("sp", uT_d[:, qd * 512:(qd + 1) * 512].rearrange("(kc p) t -> p kc t", p=128), uqt[:], reads=[uqd], writes=[Dep("uTd")], chan_dep=uqd)
                    continue
                op("pe", lambda e, pd=pd, tb=tb: e.matmul(out=pd[:, 192:192 + NH], lhsT=tri[:], rhs=s_dta[:, tb, :], start=True, stop=True), reads=[trid, s_dtad], writes=[pdd], inc=False)
                op("pe", lambda e, pd=pd, tb=tb: e.matmul(out=pd[:, 256:256 + NH], lhsT=onesf[:], rhs=s_dta[:, tb, :], start=True, stop=True), reads=[onesfd, s_dtad], writes=[pdd])
                op("act", lambda e, pd=pd, tb=tb: e.activation(out=s_eacs[:, tb, :], in_=pd[:, 192:192 + NH], func=AF.Exp), reads=[], writes=[s_eacsd, pdd])
                op("act", lambda e, pd=pd, tb=tb: e.activation(out=s_cd[:, tb, :], in_=pd[:, 256:256 + NH], func=AF.Exp), reads=[], writes=[s_cdd, pdd])
                op("dve", lambda e, pd=pd, tb=tb: e.tensor_scalar(out=s_nacs[:, tb, :], in0=pd[:, 192:192 + NH], scalar1=-1.0, scalar2=None, op0=ALU.mult), reads=[], writes=[s_nacsd, pdd])
                op("dve", lambda e, pd=pd, dt_=dt_, tb=tb: e.tensor_tensor(out=dt_[:, 1, :], in0=pd[:, 256:256 + NH], in1=s_nacs[:, tb, :], op=ALU.add), reads=[s_nacsd], writes=[dtd_, pdd])
                op("act", lambda e, dt_=dt_: e.activation(out=dt_[:, 1, :], in_=dt_[:, 1, :], func=AF.Exp), reads=[dtd_], writes=[dtd_])
                op("dve", lambda e, dt_=dt_, tb=tb: e.tensor_tensor(out=s_dtdte[:, tb, :], in0=dt_[:, 1, :], in1=s_dt[:, tb, :], op=ALU.mult), reads=[dtd_, s_dtd], writes=[s_dtdted])
                if c4 == 3:
                    qd = tb // 4
                    fw.dma("sp", uT_d[:, qd * 512:(qd + 1) * 512].rearrange("(kc p) t -> p kc t", p=128), uqt[:], reads=[uqd], writes=[Dep("uTd")], chan_dep=uqd)
            fw.barrier()
        uT_ready = [(uq[i][1].chan.key, 16 * uq[i][1].chan.n) for i in range(2)]
        if cfg.get("stop") == 0:
            fw.emit()
            return nc

        with ExitStack() as ph:
            uring = [fw.sb("uring%d" % i, [128, KC, 512], BF16, ph) for i in range(3)]
            NW = 12
            wring = [fw.sb("wring%d" % i, [128, KC, 128], BF16, ph) for i in range(NW)]
            wctr = [0]
            uctr = [0]

            def load_w(col0):
                wt, wd = wring[wctr[0] % NW]
                wctr[0] += 1
                fw.dma("pool", wt[:], w_in[:, col0:col0 + 128].rearrange("(kc p) n -> p kc n", p=128), writes=[wd])
                return wt, wd

            unit_cols = []
            for g in range(G):
                unit_cols.append([OFF_Z + g * 256, OFF_Z + g * 256 + 128, OFF_X + g * 256, OFF_X + g * 256 + 128, OFF_B + g * 128, OFF_C + g * 128])
            for hd in range(H):
                unit_cols.append([OFF_Q + hd * 128, OFF_K + hd * 128, OFF_V + hd * 128])
            unit_w = {}

            def prefetch_unit(i):
                if i < len(unit_cols) and i not in unit_w:
                    unit_w[i] = [load_w(c) for c in unit_cols[i]]

            prefetch_unit(0)
            cast_per_quad = -(-len(cast_list) // max(1, (G + H // 2) * NQ))

            def load_u(qd):
                ut, utd = uring[uctr[0] % 3]
                uctr[0] += 1
                fw.dma("sp", ut[:], uT_d[:, qd * 512:(qd + 1) * 512].rearrange("(kc p) t -> p kc t", p=128), writes=[utd], extra=uT_ready)
                return ut, utd

            pjb = [banks[0], banks[1]]
            pjc = [0]

            def proj_fm(wt, wd, ut, utd):
                pb, pbd = pjb[pjc[0] % 2]
                pjc[0] += 1
                for kc in range(KC):
                    op("pe", lambda e, pb=pb, wt=wt, ut=ut, kc=kc: e.matmul(out=pb[:, :], lhsT=wt[:, kc, :], rhs=ut[:, kc, :], start=(kc == 0), stop=(kc == KC - 1)),
                       reads=[wd, utd], writes=[pbd], inc=(kc == KC - 1))
                return pb, pbd

            with ExitStack() as pa:
                assert G % 2 == 0
                snwt, snwtd = fw.sb("snwt", [128, DSSM], F32, pa)
                fw.dma("sp", snwt[:], snw[0, :].partition_broadcast(128), writes=[snwtd])
                TB = []
                for th in range(2):
                    n = lambda x, th=th: "%s_%d" % (x, th)
                    TB.append(dict(
                        cw=fw.sb(n("cw"), [128, 4, 5], F32, pa),
                        stage=[fw.sb(n("stage%d" % i), [128, 515], F32, pa) for i in range(2)],
                        acc=[fw.sb(n("acc%d" % i), [128, 512], F32, pa) for i in range(2)],
                        carry=fw.sb(n("carry"), [128, 4, 3], F32, pa),
                        xTq=[fw.sb(n("xTq%d" % i), [128, 2, 512], BF16, pa) for i in range(2)],
                        BTq=[fw.sb(n("BTq%d" % i), [128, 512], BF16, pa) for i in range(2)],
                        CTq=[fw.sb(n("CTq%d" % i), [128, 512], BF16, pa) for i in range(2)],
                        xB=fw.sb(n("xB"), [128, 384], BF16, pa),
                        xdte=fw.sb(n("xdte"), [128, 256], BF16, pa),
                        xdt=fw.sb(n("xdt"), [128, 256], BF16, pa),
                        cbm=fw.sb(n("cbm"), [128, 128], BF16, pa),
                        ldta=[fw.sb(n("ldta%d" % i), [128, 2, 128], F32, pa) for i in range(2)],
                        Lh=[fw.sb(n("Lh%d" % i), [128, 2, 128], F32, pa) for i in range(2)],
                        Mh=[fw.sb(n("Mh%d" % i), [128, 2, 128], BF16, pa) for i in range(2)],
                        S=fw.sb(n("S"), [128, 256], F32, pa),
                        Sb=fw.sb(n("Sb"), [128, 256], BF16, pa),
                        t1=fw.sb(n("t1"), [128, 256], F32, pa),
                        t2=fw.sb(n("t2"), [128, 256], F32, pa),
                        sz=fw.sb(n("sz"), [128, 256], F32, pa),
                        yj=fw.sb(n("yj"), [128, 256], BF16, pa),
                        yo=fw.sb(n("yo"), [128, 256], BF16, pa),
                        gst=fw.sb(n("gst"), [128, 4], F32, pa),
                        yT=fw.sb(n("yTs"), [128, 2, T], BF16, pa),
                    ))
                b_tr, b_trd = banks[2]
                b_cb, b_cbd = banks[2]
                b_ar, b_ard = banks[3]
                b_ys = [banks[4], banks[5]]
                b_z, b_zd = banks[6]
                b_st, b_std = banks[7]

                def run_rr(gens):
                    gens = list(gens)
                    while gens:
                        for gn in list(gens):
                            try:
                                next(gn)
                            except StopIteration:
                                gens.remove(gn)

                def ssd_quad(th, g, qd, ut, utd, uw, part):
                    Bf = TB[th]
                    cw, cwd = Bf["cw"]
                    stage, acc = Bf["stage"], Bf["acc"]
                    carry, carryd = Bf["carry"]
                    xTq, xTqd = Bf["xTq"][qd % 2]
                    BTq, BTqd = Bf["BTq"][qd % 2]
                    CTq, CTqd = Bf["CTq"][qd % 2]
                    xB, xBd = Bf["xB"]
                    xdte, xdted = Bf["xdte"]
                    xdt, xdtd = Bf["xdt"]
                    cbm, cbmd = Bf["cbm"]
                    ldta, Lh, Mh = Bf["ldta"], Bf["Lh"], Bf["Mh"]
                    S, Sd = Bf["S"]
                    Sb, Sbd = Bf["Sb"]
                    t1, t1d = Bf["t1"]
                    t2, t2d = Bf["t2"]
                    sz, szd = Bf["sz"]
                    yj, yjd = Bf["yj"]
                    yo, yod = Bf["yo"]
                    gst, gstd = Bf["gst"]
                    yT, yTd = Bf["yT"]
                    wz = [uw[0], uw[1]]
                    wx = [uw[2], uw[3]]
                    wB = uw[4]
                    wC = uw[5]
                    b_y, b_yd = b_ys[th]
                    sto = th * 256
                    own = qd >= NQ - NQO
                    chunks = [(0, wx[0], xTq[:, 0, :], xTqd), (1, wx[1], xTq[:, 1, :], xTqd), (2, wB, BTq[:, :], BTqd)]
                    if own or qd == NQ - NQO - 1:
                        chunks.append((3, wC, CTq[:, :], CTqd))
                    for ci, (wt, wd), dst, dstd in (chunks if part == "proj" else []):
                        pb, pbd = proj_fm(wt, wd, ut, utd)
                        sg, sgd = stage[ci % 2]
                        ac, acd = acc[ci % 2]
                        op("dve", lambda e, sg=sg, ci=ci: e.tensor_copy(out=sg[:, 0:3], in_=carry[:, ci, :]), reads=[carryd], writes=[sgd])
                        op("act", lambda e, sg=sg, pb=pb: e.copy(out=sg[:, 3:515], in_=pb[:, :]), reads=[pbd], writes=[sgd])
                        op("dve", lambda e, sg=sg, ci=ci: e.tensor_copy(out=carry[:, ci, :], in_=sg[:, 512:515]), reads=[sgd], writes=[carryd])
                        yield
                        if ci == 3 and not own:
                            continue
                        op("dve", lambda e, sg=sg, ac=ac, ci=ci: e.tensor_scalar(out=ac[:], in0=sg[:, 3:515], scalar1=cw[:, ci, 3:4], scalar2=cw[:, ci, 4:5], op0=ALU.mult, op1=ALU.add), reads=[sgd, cwd], writes=[acd])
                        for k in (2, 1, 0):
                            op("dve", lambda e, sg=sg, ac=ac, ci=ci, k=k: e.scalar_tensor_tensor(out=ac[:], in0=sg[:, k:k + 512], scalar=cw[:, ci, k:k + 1], in1=ac[:], op0=ALU.mult, op1=ALU.add), reads=[sgd, cwd, acd], writes=[acd])
                        op("act", lambda e, ac=ac, dst=dst: e.activation(out=dst, in_=ac[:], func=AF.Silu), reads=[acd], writes=[dstd])
                        yield
                    for c in (range(4) if part == "core" else []):
                        tb = qd * 4 + c
                        cs = slice(c * 128, (c + 1) * 128)
                        hs = slice(g * 4, g * 4 + 4)
                        trv = b_tr[:].bitcast(BF16)
                        if own:
                            for hp in range(2):
                                hh = g * 4 + 2 * hp
                                ld_, ldd_ = ldta[hp % 2]
                                L_, Ld_ = Lh[hp % 2]
                                ar = 2 * th * 128
                                op("pool", lambda e, ld_=ld_, tb=tb, hh=hh: e.tensor_tensor(out=ld_[:], in0=sgt[:].unsqueeze(1).to_broadcast([128, 2, 128]),
                                                                                 in1=s_dta[:, tb, hh:hh + 2].unsqueeze(2).to_broadcast([128, 2, 128]), op=ALU.mult), reads=[sgtd, s_dtad], writes=[ldd_])
                                for k2 in range(2):
                                    op("pe", lambda e, ld_=ld_, ar=ar, k2=k2: e.matmul(out=b_ar[:, ar + k2 * 128:ar + (k2 + 1) * 128], lhsT=ld_[:, k2, :], rhs=tri[:], start=True, stop=True), reads=[ldd_, trid], writes=[b_ard], inc=(k2 == 1))
                                yield
                                op("act", lambda e, L_=L_, ar=ar: e.activation(out=L_[:].rearrange("p a b -> p (a b)"), in_=b_ar[:, ar:ar + 256], func=AF.Exp), reads=[b_ard], writes=[Ld_])
                                yield
                        for i in range(2):
                            op("pe", lambda e, i=i, cs=cs, trv=trv: e.transpose(out=trv[:, i * 128:(i + 1) * 128], in_=xTq[:, i, cs], identity=ident[:]), reads=[xTqd, identd], writes=[b_trd], inc=False)
                        op("pe", lambda e, cs=cs, trv=trv: e.transpose(out=trv[:, 256:384], in_=BTq[:, cs], identity=ident[:]), reads=[BTqd, identd], writes=[b_trd])
                        op("act", lambda e, trv=trv: e.copy(out=xB[:], in_=trv[:, 0:384]), reads=[b_trd], writes=[xBd])
                        yield
                        if own:
                            for i in range(2):
                                for kc in range(KC):
                                    op("pe", lambda e, kc=kc, i=i, cs=cs: e.matmul(out=b_z[:, i * 128:(i + 1) * 128], lhsT=ut[:, kc, cs], rhs=wz[i][0][:, kc, :], start=(kc == 0), stop=(kc == KC - 1)),
                                       reads=[utd, wz[i][1]], writes=[b_zd], inc=(kc == KC - 1 and i == 1))
                            op("act", lambda e: e.activation(out=sz[:], in_=b_z[:, 0:256], func=AF.Silu), reads=[b_zd], writes=[szd])
                            yield
                        op("dve", lambda e, tb=tb, hs=hs: e.tensor_tensor(out=xdte[:].rearrange("p (h d) -> p h d", d=64), in0=xB[:, 0:256].rearrange("p (h d) -> p h d", d=64),
                                                                  in1=s_dtdte[:, tb, hs].unsqueeze(2).to_broadcast([128, 4, 64]), op=ALU.mult), reads=[xBd, s_dtdted], writes=[xdted])
                        op("pe", lambda e: e.matmul(out=b_st[:, sto:sto + 256], lhsT=xB[:, 256:384], rhs=xdte[:], start=True, stop=True), reads=[xBd, xdted], writes=[b_std])
                        yield
                        if own:
                            op("pe", lambda e, cs=cs: e.matmul(out=b_cb[:, 256:384], lhsT=BTq[:, cs], rhs=CTq[:, cs], start=True, stop=True), reads=[BTqd, CTqd], writes=[b_cbd])
                            op("dve", lambda e: e.tensor_tensor(out=cbm[:], in0=b_cb[:, 256:384], in1=tri[:], op=ALU.mult), reads=[b_cbd, trid], writes=[cbmd])
                            op("dve", lambda e, tb=tb, hs=hs: e.tensor_tensor(out=xdt[:].rearrange("p (h d) -> p h d", d=64), in0=xB[:, 0:256].rearrange("p (h d) -> p h d", d=64),
                                                                      in1=s_dt[:, tb, hs].unsqueeze(2).to_broadcast([128, 4, 64]), op=ALU.mult), reads=[xBd, s_dtd], writes=[xdtd])
                            op("pe", lambda e, cs=cs: e.matmul(out=b_y[:, 256:512], lhsT=CTq[:, cs], rhs=Sb[:], start=True, stop=True), reads=[CTqd, Sbd], writes=[b_yd])
                            yield
                        op("dve", lambda e, tb=tb, hs=hs: e.tensor_tensor(out=S[:].rearrange("p (h d) -> p h d", d=64), in0=S[:].rearrange("p (h d) -> p h d", d=64),
                                                                  in1=s_cd[:, tb, hs].unsqueeze(2).to_broadcast([128, 4, 64]), op=ALU.mult), reads=[Sd, s_cdd], writes=[Sd])
                        op("dve", lambda e: e.tensor_tensor(out=S[:], in0=b_st[:, sto:sto + 256], in1=S[:], op=ALU.add), reads=[b_std, Sd], writes=[Sd])
                        if tb == NBLK - T // 128 - 1:
                            op("dve", lambda e: e.tensor_scalar(out=S[:], in0=S[:], scalar1=flg[:, 0:1], scalar2=None, op0=ALU.mult), reads=[Sd, flgd], writes=[Sd])
                        op("act", lambda e: e.copy(out=Sb[:], in_=S[:]), reads=[Sd], writes=[Sbd])
                        yield
                        if own:
                            for hp in range(2):
                                L_, Ld_ = Lh[hp % 2]
                                M_, Md_ = Mh[hp % 2]
                                op("pool", lambda e, L_=L_, M_=M_: e.tensor_tensor(out=M_[:], in0=cbm[:].unsqueeze(1).to_broadcast([128, 2, 128]), in1=L_[:], op=ALU.mult), reads=[cbmd, Ld_], writes=[Md_])
                                yield
                                for k2 in range(2):
                                    h = 2 * hp + k2
                                    op("pe", lambda e, M_=M_, h=h, k2=k2: e.matmul(out=b_y[:, h * 64:(h + 1) * 64], lhsT=M_[:, k2, :], rhs=xdt[:, h * 64:(h + 1) * 64], start=True, stop=True), reads=[Md_, xdtd], writes=[b_yd], inc=(h == 3))
                            yield
                            op("pool", lambda e, hs=hs: e.tensor_tensor(out=t2[:].rearrange("p (h d) -> p h d", d=64), in0=xB[:, 0:256].rearrange("p (h d) -> p h d", d=64),
                                                                 in1=hvb[:, 2, hs].unsqueeze(2).to_broadcast([128, 4, 64]), op=ALU.mult), reads=[xBd, hvbd], writes=[t2d])
                            op("dve", lambda e, tb=tb, hs=hs: e.tensor_tensor(out=t1[:].rearrange("p (h d) -> p h d", d=64), in0=b_y[:, 256:512].rearrange("p (h d) -> p h d", d=64),
                                                                      in1=s_eacs[:, tb, hs].unsqueeze(2).to_broadcast([128, 4, 64]), op=ALU.mult), reads=[b_yd, s_eacsd], writes=[t1d])
                            op("dve", lambda e: e.tensor_tensor(out=t1[:], in0=b_y[:, 0:256], in1=t1[:], op=ALU.add), reads=[b_yd, t1d], writes=[t1d])
                            yield
                            op("pool", lambda e: e.tensor_tensor(out=t1[:], in0=t1[:], in1=t2[:], op=ALU.add), reads=[t1d, t2d], writes=[t1d])
                            op("dve", lambda e: e.tensor_tensor(out=t1[:], in0=t1[:], in1=sz[:], op=ALU.mult), reads=[t1d, szd], writes=[t1d])
                            yield
                            op("act", lambda e: e.activation(out=yj[:], in_=t1[:], func=AF.Square, scale=1.0 / 16.0, accum_out=gst[:, 0:1]), reads=[t1d], writes=[yjd, gstd])
                            op("act", lambda e: e.activation(out=gst[:, 1:2], in_=gst[:, 0:1], func=AF.Ln, bias=EPS), reads=[gstd], writes=[gstd])
                            op("act", lambda e: e.activation(out=gst[:, 2:3], in_=gst[:, 1:2], func=AF.Exp, scale=-0.5), reads=[gstd], writes=[gstd])
                            yield
                            op("dve", lambda e: e.scalar_tensor_tensor(out=yo[:], in0=t1[:], scalar=gst[:, 2:3], in1=snwt[:, g * 256:(g + 1) * 256], op0=ALU.mult, op1=ALU.mult), reads=[t1d, gstd, snwtd], writes=[yod])
                            zv = b_z[:].bitcast(BF16)
                            for i in range(2):
                                op("pe", lambda e, i=i, zv=zv: e.transpose(out=zv[:, 512 + i * 128:512 + (i + 1) * 128], in_=yo[:, i * 128:(i + 1) * 128], identity=ident[:]), reads=[yod, identd], writes=[b_zd], inc=(i == 1))
                            to = (qd - (NQ - NQO)) * 512 + c * 128
                            op("act", lambda e, zv=zv, to=to: e.copy(out=yT[:, :, to:to + 128], in_=zv[:, 512:768].rearrange("p (i t) -> p i t", t=128)), reads=[b_zd], writes=[yTd])
                            yield

                for g0 in range(0, G, 2):
                    grp = (g0, g0 + 1)
                    for th, g in enumerate(grp):
                        prefetch_unit(g)
                        cw, cwd = TB[th]["cw"]
                        for i, r0 in enumerate((g * 256, g * 256 + 128, DSSM + g * 128, DSSM + G * 128 + g * 128)):
                            fw.dma("sp", cw[:, i, :], cwb[r0:r0 + 128, :], writes=[cwd])
                        S, Sd = TB[th]["S"]
                        Sb, Sbd = TB[th]["Sb"]
                        carry, carryd = TB[th]["carry"]
                        op("dve", lambda e, S=S: e.memset(S[:], 0.0), writes=[Sd])
                        op("dve", lambda e, Sb=Sb: e.memset(Sb[:], 0.0), writes=[Sbd])
                        op("dve", lambda e, carry=carry: e.memset(carry[:], 0.0), writes=[carryd])
                    uts = {0: load_u(0)}
                    run_rr([ssd_quad(th, g, 0, uts[0][0], uts[0][1], unit_w[g], "proj") for th, g in enumerate(grp)])
                    for qd in range(NQ):
                        cast_some(2 * cast_per_quad)
                        gens = [ssd_quad(th, g, qd, uts[qd][0], uts[qd][1], unit_w[g], "core") for th, g in enumerate(grp)]
                        pgens = []
                        if qd + 1 < NQ:
                            uts[qd + 1] = load_u(qd + 1)
                            pgens = [ssd_quad(th, g, qd + 1, uts[qd + 1][0], uts[qd + 1][1], unit_w[g], "proj") for th, g in enumerate(grp)]
                        own_q = qd >= NQ - NQO
                        period = 7 if own_q else 2
                        rnd = 0
                        while gens or pgens:
                            for gn in list(gens):
                                try:
                                    next(gn)
                                except StopIteration:
                                    gens.remove(gn)
                            rnd += 1
                            if pgens and (rnd % period == 0 or not gens):
                                gn = pgens[(rnd // period) % len(pgens)] if gens else pgens[0]
                                try:
                                    next(gn)
                                except StopIteration:
                                    pgens.remove(gn)
                    for th, g in enumerate(grp):
                        yT, yTd = TB[th]["yT"]
                        fw.dma("sp", ymT_d[g * 256:(g + 1) * 256, :].rearrange("(i p) t -> p i t", p=128), yT[:], reads=[yTd], writes=[Dep("ym")], chan_dep=yTd)
                ym_ready = [(TB[i]["yT"][1].chan.key, 16 * TB[i]["yT"][1].chan.n) for i in range(2) if TB[i]["yT"][1].chan is not None]
                fw.barrier()
                if cfg.get("stop") == 1:
                    fw.emit()
                    return nc

            with ExitStack() as pa:
                KTs = [fw.sb("KT%d" % i, [128, TT], BF16, pa) for i in range(2)]
                VTs = [fw.sb("VT%d" % i, [128, TT], BF16, pa) for i in range(2)]
                QTs = [fw.sb("QT%d" % i, [128, T], BF16, pa) for i in range(2)]
                NVB = T // 128 + DIL[2]
                Vd_, Vdd_ = fw.sb("Vd", [128, NVB, 128], BF16, pa)
                aacc, aaccd = fw.sb("aacc", [128, 2, T], F32, pa)
                PT = [fw.sb("PT%d" % i, [128, 2, 128], BF16, pa) for i in range(2)]
                PM = [fw.sb("PM%d" % i, [128, 2, 128], BF16, pa) for i in range(3)]
                yA = [fw.sb("yA%d" % i, [128, T], BF16, pa) for i in range(2)]
                b_vt = [banks[2], banks[2]]
                b_s = [banks[3], banks[4], banks[5]]
                b_o = [banks[6], banks[7]]
                uc = [0]
                SKEW = 2

                def att_proj(hd):
                    prefetch_unit(G + hd)
                    prefetch_unit(G + hd + 1)
                    wq, wk, wv = unit_w[G + hd]
                    KT, KTd = KTs[hd % 2]
                    VT, VTd = VTs[hd % 2]
                    QT, QTd = QTs[hd % 2]
                    for qd in range(NQ):
                        own = qd >= NQ - NQO
                        ut, utd = load_u(qd)
                        cast_some(cast_per_quad)
                        ts_ = slice(qd * 512, (qd + 1) * 512)
                        pb, pbd = proj_fm(wk[0], wk[1], ut, utd)
                        op("act", lambda e, pb=pb, ts_=ts_: e.copy(out=KT[:, ts_], in_=pb[:, :]), reads=[pbd], writes=[KTd])
                        yield
                        pb, pbd = proj_fm(wv[0], wv[1], ut, utd)
                        op("dve", lambda e, pb=pb, ts_=ts_: e.tensor_copy(out=VT[:, ts_], in_=pb[:, :]), reads=[pbd], writes=[VTd])
                        yield
                        if own:
                            to = (qd - (NQ - NQO)) * 512
                            pb, pbd = proj_fm(wq[0], wq[1], ut, utd)
                            op("act", lambda e, pb=pb, to=to: e.activation(out=QT[:, to:to + 512], in_=pb[:, :], func=AF.Copy, scale=128.0 ** -0.5), reads=[pbd], writes=[QTd])
                            yield

                def att_units(hd):
                    KT, KTd = KTs[hd % 2]
                    VT, VTd = VTs[hd % 2]
                    QT, QTd = QTs[hd % 2]
                    first = True
                    for d in DIL:
                        nj = T // (128 * d)
                        nblk = d * (nj + 1)
                        for b0 in range(0, nblk, 4):
                            vb, vbd = b_vt[(b0 // 4) % 2]
                            vbv = vb[:].bitcast(BF16)
                            nb = min(4, nblk - b0)
                            for i in range(nb):
                                r, jj = divmod(b0 + i, nj + 1)
                                st_ = C + (jj - 1) * 128 * d + r
                                op("pe", lambda e, vbv=vbv, i=i, st_=st_, d=d: e.transpose(out=vbv[:, i * 128:(i + 1) * 128], in_=VT[:, st_:st_ + 127 * d + 1:d], identity=ident[:]), reads=[VTd, identd], writes=[vbd], inc=(i == nb - 1))
                            op("act", lambda e, vbv=vbv, b0=b0, nb=nb: e.copy(out=Vd_[:, b0:b0 + nb, :], in_=vbv[:, 0:nb * 128].rearrange("p (i t) -> p i t", t=128)), reads=[vbd], writes=[Vdd_])
                            yield
                        units = [(r, j) for r in range(d) for j in range(nj)]
                        info = {}

                        def stage_a(idx):
                            r, j = units[idx]
                            u_i = uc[0]
                            uc[0] += 1
                            bs, bsd = b_s[u_i % 3]
                            Pm_, Pmd_ = PM[u_i % 3]
                            q0 = j * 128 * d + r
                            qsl = slice(q0, q0 + 127 * d + 1, d)
                            kcur = slice(C + q0, C + q0 + 127 * d + 1, d)
                            kprev = slice(C + q0 - 128 * d, C + q0 - d + 1, d)
                            mk, mkd = (maskB, maskBd) if j == 0 else (maskA, maskAd)
                            op("pe", lambda e: e.matmul(out=bs[:, 0:256], lhsT=ident[:], rhs=mk[:].rearrange("p a b -> p (a b)"), start=True, stop=False), reads=[identd, mkd], writes=[bsd], inc=False)
                            op("pe", lambda e: e.matmul(out=bs[:, 0:128], lhsT=KT[:, kprev], rhs=QT[:, qsl], start=False, stop=False), reads=[KTd, QTd], writes=[bsd], inc=False)
                            op("pe", lambda e: e.matmul(out=bs[:, 128:256], lhsT=KT[:, kcur], rhs=QT[:, qsl], start=False, stop=True), reads=[KTd, QTd], writes=[bsd])
                            op("act", lambda e: e.activation(out=Pm_[:].rearrange("p a b -> p (a b)"), in_=bs[:, 0:256], func=AF.Exp), reads=[bsd], writes=[Pmd_])
                            info[idx] = (u_i, Pm_, Pmd_, qsl)

                        def stage_b(idx, first):
                            r, j = units[idx]
                            u_i, Pm_, Pmd_, qsl = info.pop(idx)
                            bo, bod = b_o[u_i % 2]
                            vi_prev = r * (nj + 1) + j
                            vi_cur = vi_prev + 1
                            op("pe", lambda e: e.matmul(out=bo[:, 0:128], lhsT=Vd_[:, vi_prev, :], rhs=Pm_[:, 0, :], start=True, stop=False), reads=[Vdd_, Pmd_], writes=[bod], inc=False)
                            op("pe", lambda e: e.matmul(out=bo[:, 0:128], lhsT=Vd_[:, vi_cur, :], rhs=Pm_[:, 1, :], start=False, stop=True), reads=[Vdd_, Pmd_], writes=[bod], inc=False)
                            op("pe", lambda e: e.matmul(out=bo[:, 128:256], lhsT=onesb[:], rhs=Pm_[:, 0, :], start=True, stop=False), reads=[onesbd, Pmd_], writes=[bod], inc=False)
                            op("pe", lambda e: e.matmul(out=bo[:, 128:256], lhsT=onesb[:], rhs=Pm_[:, 1, :], start=False, stop=True), reads=[onesbd, Pmd_], writes=[bod])
                            src = bo[:, 0:256].rearrange("p (a t) -> p a t", t=128)
                            if first:
                                op("dve", lambda e: e.tensor_copy(out=aacc[:, :, qsl], in_=src), reads=[bod], writes=[aaccd])
                            else:
                                op("dve", lambda e: e.tensor_tensor(out=aacc[:, :, qsl], in0=src, in1=aacc[:, :, qsl], op=ALU.add), reads=[bod, aaccd], writes=[aaccd])

                        n_u = len(units)
                        for idx in range(n_u + SKEW):
                            if idx < n_u:
                                stage_a(idx)
                            if idx >= SKEW:
                                stage_b(idx - SKEW, first)
                            yield
                        first = False
                    ya, yad = yA[hd % 2]
                    op("dve", lambda e: e.reciprocal(out=aacc[:, 1, :], in_=aacc[:, 1, :]), reads=[aaccd], writes=[aaccd])
                    op("dve", lambda e: e.tensor_tensor(out=ya[:], in0=aacc[:, 0, :], in1=aacc[:, 1, :], op=ALU.mult), reads=[aaccd], writes=[yad])
                    fw.dma("sp", ymT_d[DSSM + hd * 128:DSSM + (hd + 1) * 128, :], ya[:], reads=[yad], writes=[Dep("ym")], chan_dep=yad)
                    yield

                for _ in att_proj(0):
                    pass
                for hd in range(H):
                    gu = att_units(hd)
                    gp = att_proj(hd + 1) if hd + 1 < H else None
                    alive_u = True
                    while alive_u or gp is not None:
                        for _ in range(3):
                            if alive_u:
                                try:
                                    next(gu)
                                except StopIteration:
                                    alive_u = False
                        if gp is not None:
                            try:
                                next(gp)
                            except StopIteration:
                                gp = None
                ym_ready += [(yA[i][1].chan.key, 16 * yA[i][1].chan.n) for i in range(2) if yA[i][1].chan is not None]
                fw.barrier()
        cast_some(len(cast_list))
        wcast_ready = [(d_.chan.key, 16 * d_.chan.n) for d_ in wcastds if d_.chan is not None]
        p01.close()

        with ExitStack() as ph:
            nwpost, nwpostd = fw.sb("nwA", [128, D], F32, ph)
            nwpre, nwpred = fw.sb("nwB", [128, D], F32, ph)
            nwfin, nwfind = nwpost, nwpostd
            fw.dma("sp", nwpre[:], nw4[2, :].partition_broadcast(128), writes=[nwpred])
            NWP = 4
            PR = 8
            wp = [fw.sb("wp%d" % i, [128, PR, 512], BF16, ph) for i in range(NWP)]
            wpc = [0]
            big, bigd = fw.sb("big", [128, max(MKC, FC), 512], BF16, ph)
            bigs = [Dep("big%d" % i) for i in range(max(MKC, FC))]
            xh, xhd = fw.sb("xh", [128, 4, D], F32, ph)
            mf, mfd = fw.sb("mf", [128, 4, D], F32, ph)
            hn, hnd = fw.sb("hn", [128, D], BF16, ph)
            hnT, hnTd = fw.sb("hnT", [128, KC, 512], BF16, ph)
            rls = [fw.sb("rl%d" % i, [128, 512], F32, ph) for i in range(2)]
            st2, st2d = fw.sb("st2", [128, 4, 12], F32, ph)

            def load_wp(src, r0, nrow_chunks, c0, ncol):
                wt, wd = wp[wpc[0] % NWP]
                wpc[0] += 1
                fw.dma("sp", wt[:, 0:nrow_chunks, 0:ncol], src[r0:r0 + nrow_chunks * 128, c0:c0 + ncol].rearrange("(kc p) n -> p kc n", p=128), writes=[wd], extra=wcast_ready)
                return wt, wd

            NCB = D // 512 if D >= 512 else 1
            CBW = min(512, D)
            for qd in range(NQO):
                tq = slice(qd * 512, (qd + 1) * 512)
                for k0 in range(0, MKC, 16):
                    k1 = min(MKC, k0 + 16)
                    tk = fw.dma("sp", big[:, k0:k1, :], ymT_d[k0 * 128:k1 * 128, tq].rearrange("(kc p) t -> p kc t", p=128), writes=bigs[k0:k1], extra=ym_ready, chan_dep=bigd)
                for k in range(MKC):
                    bigs[k].w = tk
                fw.dma("sp", xh[:], xc[C + qd * 512:C + (qd + 1) * 512, :].rearrange("(a p) d -> p a d", p=128), writes=[xhd])
                for cb in range(NCB):
                    npiece = (MKC + PR - 1) // PR
                    for pi in range(npiece):
                        nr = min(PR, MKC - pi * PR)
                        wt, wd = load_wp(wout_b, pi * PR * 128, nr, cb * CBW, CBW)
                        for tb in range(4):
                            pb, pbd = banks[(cb % 2) * 4 + tb]
                            for k in range(nr):
                                kc = pi * PR + k
                                op("pe", lambda e, pb=pb, wt=wt, k=k, kc=kc, tb=tb: e.matmul(out=pb[:, 0:CBW], lhsT=big[:, kc, tb * 128:(tb + 1) * 128], rhs=wt[:, k, 0:CBW], start=(kc == 0), stop=(kc == MKC - 1)),
                                   reads=[bigs[kc], wd], writes=[pbd], inc=(k == nr - 1))
                    for tb in range(4):
                        pb, pbd = banks[(cb % 2) * 4 + tb]
                        if tb % 2 == 0:
                            op("act", lambda e, pb=pb, tb=tb, cb=cb: e.copy(out=mf[:, tb, cb * CBW:(cb + 1) * CBW], in_=pb[:, 0:CBW]), reads=[pbd], writes=[mfd])
                        else:
                            op("dve", lambda e, pb=pb, tb=tb, cb=cb: e.tensor_copy(out=mf[:, tb, cb * CBW:(cb + 1) * CBW], in_=pb[:, 0:CBW]), reads=[pbd], writes=[mfd])
                fw.dma("sp", nwpost[:], nw4[1, :].partition_broadcast(128), writes=[nwpostd])
                junk2, junk2d = hn, hnd
                for tb in range(4):
                    op("act", lambda e, tb=tb: e.activation(out=junk2[:], in_=mf[:, tb, :], func=AF.Square, scale=float(D) ** -0.5, accum_out=st2[:, tb, 0:1]), reads=[mfd], writes=[junk2d, st2d])
                    op("act", lambda e, tb=tb: e.activation(out=st2[:, tb, 1:2], in_=st2[:, tb, 0:1], func=AF.Ln, bias=EPS), reads=[st2d], writes=[st2d])
                    op("act", lambda e, tb=tb: e.activation(out=st2[:, tb, 2:3], in_=st2[:, tb, 1:2], func=AF.Exp, scale=-0.5), reads=[st2d], writes=[st2d])
                    op("dve", lambda e, tb=tb: e.scalar_tensor_tensor(out=mf[:, tb, :], in0=mf[:, tb, :], scalar=st2[:, tb, 2:3], in1=nwpost[:], op0=ALU.mult, op1=ALU.mult), reads=[mfd, st2d, nwpostd], writes=[mfd])
                    op("dve", lambda e, tb=tb: e.tensor_tensor(out=xh[:, tb, :], in0=xh[:, tb, :], in1=mf[:, tb, :], op=ALU.add), reads=[xhd, mfd], writes=[xhd])
                    op("act", lambda e, tb=tb: e.activation(out=junk2[:], in_=xh[:, tb, :], func=AF.Square, scale=float(D) ** -0.5, accum_out=st2[:, tb, 3:4]), reads=[xhd], writes=[junk2d, st2d])
                    op("act", lambda e, tb=tb: e.activation(out=st2[:, tb, 4:5], in_=st2[:, tb, 3:4], func=AF.Ln, bias=EPS), reads=[st2d], writes=[st2d])
                    op("act", lambda e, tb=tb: e.activation(out=st2[:, tb, 5:6], in_=st2[:, tb, 4:5], func=AF.Exp, scale=-0.5), reads=[st2d], writes=[st2d])
                    op("dve", lambda e, tb=tb: e.scalar_tensor_tensor(out=hn[:], in0=xh[:, tb, :], scalar=st2[:, tb, 5:6], in1=nwpre[:], op0=ALU.mult, op1=ALU.mult), reads=[xhd, st2d, nwpred], writes=[hnd])
                    nper = 8 if KC >= 8 else KC
                    ngrp = (KC + nper - 1) // nper
                    for gi in range(ngrp):
                        pb, pbd = banks[(tb * ngrp + gi) % 8]
                        pbv = pb[:].bitcast(BF16)
                        n_in = min(nper, KC - gi * nper)
                        for j in range(n_in):
                            kc = gi * nper + j
                            op("pe", lambda e, pbv=pbv, kc=kc, j=j: e.transpose(out=pbv[:, j * 128:(j + 1) * 128], in_=hn[:, kc * 128:(kc + 1) * 128], identity=ident[:]), reads=[hnd, identd], writes=[pbd], inc=(j == n_in - 1))
                        op("act", lambda e, pbv=pbv, gi=gi, n_in=n_in, tb=tb: e.copy(out=hnT[:, gi * nper:gi * nper + n_in, tb * 128:(tb + 1) * 128], in_=pbv[:, 0:n_in * 128].rearrange("p (j t) -> p j t", t=128)), reads=[pbd], writes=[hnTd])
                nfc_per = 4
                for f0 in range(0, FC, nfc_per):
                    nf = min(nfc_per, FC - f0)
                    halves = []
                    for k0 in range(0, KC, PR):
                        nr = min(PR, KC - k0)
                        halves.append((load_wp(wup_b, k0 * 128, nr, f0 * 128, nf * 128), k0, nr))
                    for fi in range(nf):
                        fc = f0 + fi
                        pb, pbd = banks[fc % 8]
                        for (wt, wd), k0, nr in halves:
                            for k in range(nr):
                                kc = k0 + k
                                op("pe", lambda e, pb=pb, wt=wt, fi=fi, k=k, kc=kc: e.matmul(out=pb[:, :], lhsT=wt[:, k, fi * 128:(fi + 1) * 128], rhs=hnT[:, kc, :], start=(kc == 0), stop=(kc == KC - 1)),
                                   reads=[wd, hnTd], writes=[pbd], inc=(kc == KC - 1))
                        rl, rld = rls[fc % 2]
                        op("act", lambda e, pb=pb, rl=rl: e.activation(out=rl[:], in_=pb[:, :], func=AF.Relu), reads=[pbd], writes=[rld])
                        op("dve", lambda e, fc=fc, rl=rl: e.tensor_tensor(out=big[:, fc, :], in0=rl[:], in1=rl[:], op=ALU.mult), reads=[rld], writes=[bigs[fc]])
                for cb in range(NCB):
                    npiece = (FC + PR - 1) // PR
                    for pi in range(npiece):
                        nr = min(PR, FC - pi * PR)
                        wt, wd = load_wp(wdown_b, pi * PR * 128, nr, cb * CBW, CBW)
                        for tb in range(4):
                            pb, pbd = banks[(cb % 2) * 4 + tb]
                            for k in range(nr):
                                fc = pi * PR + k
                                op("pe", lambda e, pb=pb, wt=wt, k=k, fc=fc, tb=tb: e.matmul(out=pb[:, 0:CBW], lhsT=big[:, fc, tb * 128:(tb + 1) * 128], rhs=wt[:, k, 0:CBW], start=(fc == 0), stop=(fc == FC - 1)),
                                   reads=[bigs[fc], wd], writes=[pbd], inc=(k == nr - 1))
                    for tb in range(4):
                        pb, pbd = banks[(cb % 2) * 4 + tb]
                        if tb % 2 == 0:
                            op("act", lambda e, pb=pb, tb=tb, cb=cb: e.copy(out=mf[:, tb, cb * CBW:(cb + 1) * CBW], in_=pb[:, 0:CBW]), reads=[pbd], writes=[mfd])
                        else:
                            op("dve", lambda e, pb=pb, tb=tb, cb=cb: e.tensor_copy(out=mf[:, tb, cb * CBW:(cb + 1) * CBW], in_=pb[:, 0:CBW]), reads=[pbd], writes=[mfd])
                fw.dma("sp", nwfin[:], nw4[3, :].partition_broadcast(128), writes=[nwfind])
                for tb in range(4):
                    op("act", lambda e, tb=tb: e.activation(out=junk2[:], in_=mf[:, tb, :], func=AF.Square, scale=float(D) ** -0.5, accum_out=st2[:, tb, 6:7]), reads=[mfd], writes=[junk2d, st2d])
                    op("act", lambda e, tb=tb: e.activation(out=st2[:, tb, 7:8], in_=st2[:, tb, 6:7], func=AF.Ln, bias=EPS), reads=[st2d], writes=[st2d])
                    op("act", lambda e, tb=tb: e.activation(out=st2[:, tb, 8:9], in_=st2[:, tb, 7:8], func=AF.Exp, scale=-0.5), reads=[st2d], writes=[st2d])
                    op("dve", lambda e, tb=tb: e.scalar_tensor_tensor(out=mf[:, tb, :], in0=mf[:, tb, :], scalar=st2[:, tb, 8:9], in1=nwfin[:], op0=ALU.mult, op1=ALU.mult), reads=[mfd, st2d, nwfind], writes=[mfd])
                    op("dve", lambda e, tb=tb: e.tensor_tensor(out=mf[:, tb, :], in0=xh[:, tb, :], in1=mf[:, tb, :], op=ALU.add), reads=[xhd, mfd], writes=[mfd])
                fw.dma("sp", out[tq, :].rearrange("(a p) d -> p a d", p=128), mf[:], reads=[mfd], writes=[Dep("out")], chan_dep=mfd)
            fw.barrier()
        fw.emit()
    return nc


_CACHE = {}


def make_in_maps(cfg, ncores, inputs):
    D, T, G = cfg["D"], cfg["T"], cfg["G"]
    x = np.asarray(inputs["x"], np.float32)
    B_, S_, _ = x.shape
    halves = S_ // T
    assert halves == 2 and B_ * halves == ncores
    shared = {
        "w_in": np.ascontiguousarray(np.asarray(inputs["w_in"], np.float32)[0]),
        "cwb": np.ascontiguousarray(np.concatenate([np.asarray(inputs["conv_w"], np.float32)[0].T, np.asarray(inputs["conv_b"], np.float32)[0][:, None]], axis=1)),
        "hv": np.ascontiguousarray(np.stack([np.asarray(inputs["dt_bias"], np.float32)[0], np.asarray(inputs["a_log"], np.float32)[0], np.asarray(inputs["d_skip"], np.float32)[0]], axis=0)),
        "snw": np.ascontiguousarray(np.asarray(inputs["ssm_norm_w"], np.float32)),
        "nw4": np.ascontiguousarray(np.stack([np.asarray(inputs[k], np.float32)[0] for k in ("norm_mix_pre", "norm_mix_post", "norm_mlp_pre", "norm_mlp_post")], axis=0)),
        "w_out": np.ascontiguousarray(np.asarray(inputs["w_out"], np.float32)[0]),
        "w_up": np.ascontiguousarray(np.asarray(inputs["w_up"], np.float32)[0]),
        "w_down": np.ascontiguousarray(np.asarray(inputs["w_down"], np.float32)[0]),
    }
    maps = []
    for c in range(ncores):
        b, h = divmod(c, 2)
        if h == 0:
            xcc = np.concatenate([np.zeros((T, D), np.float32), x[b, :T]], axis=0)
            fl = np.zeros((128, 1), np.float32)
        else:
            xcc = x[b]
            fl = np.ones((128, 1), np.float32)
        m = dict(shared)
        m["xc"] = np.ascontiguousarray(xcc)
        m["flag"] = fl
        maps.append(m)
    return maps


def kernel(**inputs):
    cfg = FULL_CFG
    if "nc" not in _CACHE:
        _CACHE["nc"] = build_program(cfg)
    nc = _CACHE["nc"]
    maps = make_in_maps(cfg, 8, inputs)
    res = run_bass_kernel_spmd(nc, maps, core_ids=list(range(8)))
    x = inputs["x"]
    B_, S_, D = x.shape
    T = cfg["T"]
    outp = np.empty((B_, S_, D), np.float32)
    for c in range(8):
        b, h = divmod(c, 2)
        outp[b, h * T:(h + 1) * T] = res.results[c]["out"]
    return outp
```
